# Optimizing a Trainium2 kernel written in Bass

```python
import math
import jax, jax.numpy as jnp
from jax import lax
import numpy as np

D_MODEL = 2048
BATCH = 4
SEQ = 4096
DEPTH = 2

HEAD_DIM = 128
GRID_W = 64
NA_HEADS = 6
NA_WIN_H = 8
NA_WIN_W = 16
DN_HEADS = 6
DN_CONV = 5
DN_CHUNK = 64
MEM_HEADS = 4
MEM_LEN = 256
DIFF_HEADS = 6
DIFF_VDIM = 2 * HEAD_DIM
Q_BLOCK = 128
ROPE_THETA = 10000.0
EPS = 1e-6
N_EVEN = (DEPTH + 1) // 2
N_ODD = DEPTH // 2

NA_W = NA_HEADS * HEAD_DIM
DN_W = DN_HEADS * HEAD_DIM
MEM_W = MEM_HEADS * HEAD_DIM
DIFF_W = DIFF_HEADS * DIFF_VDIM
MIX_W = NA_W + DN_W + MEM_W
EVEN_IN = 4 * NA_W + 4 * DN_W + 4 * DN_HEADS + 2 * MEM_W
ODD_IN = 4 * DIFF_W + 2 * MEM_W

kernel_name = 'hybrid_grid_encoder_block'


def rmsnorm(x, g):
    xf = x.astype(jnp.float32)
    y = xf * lax.rsqrt(jnp.mean(xf * xf, axis=-1, keepdims=True) + EPS)
    return (y * g.astype(jnp.float32)).astype(x.dtype)


def l2norm(x):
    xf = x.astype(jnp.float32)
    return (xf * lax.rsqrt(jnp.sum(xf * xf, axis=-1, keepdims=True) + EPS)).astype(x.dtype)


def rope(x, pos):
    d = x.shape[-1]
    inv = ROPE_THETA ** (-jnp.arange(0, d, 2, dtype=jnp.float32) / d)
    ang = pos.astype(jnp.float32)[:, None] * inv[None, :]
    cos, sin = jnp.cos(ang), jnp.sin(ang)
    x1, x2 = jnp.split(x.astype(jnp.float32), 2, axis=-1)
    return jnp.concatenate([x1 * cos - x2 * sin, x1 * sin + x2 * cos], axis=-1).astype(x.dtype)


def split_cols(t, sizes):
    out, start = [], 0
    for s in sizes:
        out.append(t[..., start:start + s])
        start += s
    return out


def to_heads(t, n_heads):
    B, T, _ = t.shape
    return t.reshape(B, T, n_heads, -1).transpose(0, 2, 1, 3)


def merge_heads(o):
    B, H, T, d = o.shape
    return o.transpose(0, 2, 1, 3).reshape(B, T, H * d)


def dwconv_centred(x, w):
    K, C = w.shape
    pad = K // 2
    return lax.conv_general_dilated(x, w[:, None, :], window_strides=(1,), padding=[(pad, pad)],
                                    dimension_numbers=('NWC', 'WIO', 'NWC'), feature_group_count=C)


def neighbourhood_attention(q, k, v, rpb):
    B, H, T, d = q.shape
    rows = T // GRID_W
    kh = min(NA_WIN_H, rows)
    r = jnp.arange(rows)
    r0 = jnp.clip(r - kh // 2, 0, rows - kh)
    key_rows = r0[:, None] + jnp.arange(kh)[None, :]
    c = jnp.arange(GRID_W)
    c0 = jnp.clip(c - NA_WIN_W // 2, 0, GRID_W - NA_WIN_W)
    col_ok = (c[None, :] >= c0[:, None]) & (c[None, :] < c0[:, None] + NA_WIN_W)
    qg = q.reshape(B, H, rows, GRID_W, d)
    kg = k.reshape(B, H, rows, GRID_W, d)[:, :, key_rows]
    vg = v.reshape(B, H, rows, GRID_W, d)[:, :, key_rows]
    s = jnp.einsum('bhrqd,bhrikd->bhrqik', qg, kg, preferred_element_type=jnp.float32) * (d ** -0.5)
    dr = key_rows - r[:, None] + (NA_WIN_H - 1)
    dc = jnp.clip(c[None, :] - c[:, None], -(NA_WIN_W - 1), NA_WIN_W - 1) + (NA_WIN_W - 1)
    bias = rpb[:, dr[:, None, :, None], dc[None, :, None, :]]
    s = s + bias[None].astype(jnp.float32)
    s = jnp.where(col_ok[None, None, None, :, None, :], s, -jnp.inf)
    p = jax.nn.softmax(s.reshape(B, H, rows, GRID_W, kh * GRID_W), axis=-1).reshape(s.shape)
    o = jnp.einsum('bhrqik,bhrikd->bhrqd', p.astype(v.dtype), vg)
    return o.reshape(B, H, T, d)


def gated_delta_rule_chunked(q, k, v, g, beta):
    out_dtype = v.dtype
    f32 = jnp.float32
    B, H, T, dk = q.shape
    dv = v.shape[-1]
    C = DN_CHUNK
    N = T // C
    q = q.astype(f32).reshape(B, H, N, C, dk)
    k = k.astype(f32).reshape(B, H, N, C, dk)
    v = v.astype(f32).reshape(B, H, N, C, dv)
    g = g.astype(f32).reshape(B, H, N, C)
    beta = beta.astype(f32).reshape(B, H, N, C)
    G = jnp.cumsum(g, axis=-1)
    lower_incl = jnp.tril(jnp.ones((C, C), bool))
    lower_strict = jnp.tril(jnp.ones((C, C), bool), -1)
    decay = jnp.exp(jnp.where(lower_incl, G[..., :, None] - G[..., None, :], -jnp.inf))
    kb = k * beta[..., None]
    a_mat = jnp.where(lower_strict, jnp.einsum('bhnid,bhnjd->bhnij', kb, k) * decay, 0.0)
    rhs = jnp.concatenate([v * beta[..., None], kb * jnp.exp(G)[..., None]], axis=-1)
    sol = lax.linalg.triangular_solve(a_mat + jnp.eye(C, dtype=f32), rhs, left_side=True, lower=True)
    u, w = sol[..., :dv], sol[..., dv:]
    qk = jnp.einsum('bhnid,bhnjd->bhnij', q, k) * decay
    q_dec = q * jnp.exp(G)[..., None]
    k_dec = k * jnp.exp(G[..., -1:] - G)[..., None]
    g_end = jnp.exp(G[..., -1])

    def step(S, xs):
        u_c, w_c, qk_c, qd_c, kd_c, ge = xs
        v_new = u_c - jnp.einsum('bhcd,bhde->bhce', w_c, S)
        o = jnp.einsum('bhcd,bhde->bhce', qd_c, S) + jnp.einsum('bhij,bhje->bhie', qk_c, v_new)
        S = S * ge[..., None, None] + jnp.einsum('bhcd,bhce->bhde', kd_c, v_new)
        return S, o

    xs = tuple(jnp.moveaxis(a, 2, 0) for a in (u, w, qk, q_dec, k_dec, g_end))
    S0 = jnp.zeros((B, H, dk, dv), f32)
    _, o = lax.scan(step, S0, xs)
    return jnp.moveaxis(o, 0, 2).reshape(B, H, T, dv).astype(out_dtype)


def deltanet_branch(qkv, ba, conv_w, a_log, dt_bias, out_norm):
    qkv = jax.nn.silu(dwconv_centred(qkv, conv_w))
    q, k, v = [to_heads(t, DN_HEADS) for t in jnp.split(qkv, 3, axis=-1)]
    q = l2norm(q) * (HEAD_DIM ** -0.5)
    k = l2norm(k)
    b_f, b_b, a_f, a_b = [t.transpose(0, 2, 1).astype(jnp.float32) for t in jnp.split(ba, 4, axis=-1)]
    al = a_log.astype(jnp.float32)
    dtb = dt_bias.astype(jnp.float32)
    g_f = -jnp.exp(al[0])[None, :, None] * jax.nn.softplus(a_f + dtb[0][None, :, None])
    g_b = -jnp.exp(al[1])[None, :, None] * jax.nn.softplus(a_b + dtb[1][None, :, None])
    o_f = gated_delta_rule_chunked(q, k, v, g_f, jax.nn.sigmoid(b_f))
    flip = lambda t: jnp.flip(t, axis=2)
    o_b = flip(gated_delta_rule_chunked(flip(q), flip(k), flip(v), flip(g_b), flip(jax.nn.sigmoid(b_b))))
    return merge_heads(rmsnorm(o_f + o_b, out_norm))


def memory_branch(mem_q, mem_n, w_kv, q_norm, k_norm):
    q = rmsnorm(to_heads(mem_q, MEM_HEADS), q_norm)
    k, v = jnp.split(mem_n @ w_kv, 2, axis=-1)
    k = rmsnorm(to_heads(k, MEM_HEADS), k_norm)
    v = to_heads(v, MEM_HEADS)
    s = jnp.einsum('bhtd,bhmd->bhtm', q, k, preferred_element_type=jnp.float32) * (HEAD_DIM ** -0.5)
    p = jax.nn.softmax(s, axis=-1)
    return merge_heads(jnp.einsum('bhtm,bhmd->bhtd', p.astype(v.dtype), v))


def diff_attention(q, k, v, lam):
    B, H, _, T, d = q.shape
    nb = T // Q_BLOCK
    qb = jnp.moveaxis(q.reshape(B, H, 2, nb, Q_BLOCK, d), 3, 0)

    def block(qi):
        s = jnp.einsum('bhmqd,bhmkd->bhmqk', qi, k, preferred_element_type=jnp.float32) * (d ** -0.5)
        p = jax.nn.softmax(s, axis=-1)
        attn = p[:, :, 0] - lam * p[:, :, 1]
        return jnp.einsum('bhqk,bhke->bhqe', attn.astype(v.dtype), v)

    o = lax.map(block, qb)
    return jnp.moveaxis(o, 0, 2).reshape(B, H, T, v.shape[-1])


def even_layer(x, mem_n, ln_g, w_in, w_mem_kv, w_out, na_q_norm, na_k_norm, na_rpb,
               dn_conv_w, dn_a_log, dn_dt_bias, dn_out_norm, mem_q_norm, mem_k_norm):
    h = rmsnorm(x, ln_g)
    proj = h @ w_in
    na_qkv, na_gate, dn_qkv, dn_gate, dn_ba, mem_q, mem_gate = split_cols(
        proj, [3 * NA_W, NA_W, 3 * DN_W, DN_W, 4 * DN_HEADS, MEM_W, MEM_W])
    q, k, v = [to_heads(t, NA_HEADS) for t in jnp.split(na_qkv, 3, axis=-1)]
    q = rmsnorm(q, na_q_norm)
    k = rmsnorm(k, na_k_norm)
    na_o = merge_heads(neighbourhood_attention(q, k, v, na_rpb)) * jax.nn.silu(na_gate)
    dn_o = deltanet_branch(dn_qkv, dn_ba, dn_conv_w, dn_a_log, dn_dt_bias, dn_out_norm) * jax.nn.silu(dn_gate)
    mem_o = memory_branch(mem_q, mem_n, w_mem_kv, mem_q_norm, mem_k_norm) * jax.nn.silu(mem_gate)
    return x + jnp.concatenate([na_o, dn_o, mem_o], axis=-1) @ w_out


def odd_layer(x, mem_n, layer_idx, ln_g, w_in, w_mem_kv, w_out, q_norm, k_norm, lam_vecs, subln_g,
              mem_q_norm, mem_k_norm):
    B, T, _ = x.shape
    h = rmsnorm(x, ln_g)
    proj = h @ w_in
    q, k, v, gate, mem_q, mem_gate = split_cols(proj, [DIFF_W, DIFF_W, DIFF_W, DIFF_W, MEM_W, MEM_W])
    pos = jnp.arange(T)
    q = q.reshape(B, T, DIFF_HEADS, 2, HEAD_DIM).transpose(0, 2, 3, 1, 4)
    k = k.reshape(B, T, DIFF_HEADS, 2, HEAD_DIM).transpose(0, 2, 3, 1, 4)
    q = rope(rmsnorm(q, q_norm), pos)
    k = rope(rmsnorm(k, k_norm), pos)
    v = to_heads(v, DIFF_HEADS)
    lam_init = 0.8 - 0.6 * math.exp(-0.3 * layer_idx)
    lv = lam_vecs.astype(jnp.float32)
    lam = jnp.exp(jnp.sum(lv[0] * lv[1])) - jnp.exp(jnp.sum(lv[2] * lv[3])) + lam_init
    o = rmsnorm(diff_attention(q, k, v, lam), subln_g) * (1.0 - lam_init)
    diff_o = merge_heads(o) * jax.nn.silu(gate)
    mem_o = memory_branch(mem_q, mem_n, w_mem_kv, mem_q_norm, mem_k_norm) * jax.nn.silu(mem_gate)
    return x + jnp.concatenate([diff_o, mem_o], axis=-1) @ w_out


def setup_inputs(seed: int = 0) -> dict:
    key = jax.random.key(seed)
    keys = iter(jax.random.split(key, 40))

    def nrm(shape, scale):
        return jax.random.normal(next(keys), shape, jnp.float32) * scale

    def gain(shape):
        return 1.0 + nrm(shape, 0.05)

    D = D_MODEL
    return {
        'x': nrm((BATCH, SEQ, D), 1.0),
        'mem': nrm((BATCH, MEM_LEN, D), 1.0),
        'mem_norm_g': gain((D,)),
        'e_ln_g': gain((N_EVEN, D)),
        'e_w_in': nrm((N_EVEN, D, EVEN_IN), D ** -0.5),
        'e_w_mem_kv': nrm((N_EVEN, D, 2 * MEM_W), D ** -0.5),
        'e_w_out': nrm((N_EVEN, MIX_W, D), MIX_W ** -0.5),
        'na_q_norm': gain((N_EVEN, HEAD_DIM)),
        'na_k_norm': gain((N_EVEN, HEAD_DIM)),
        'na_rpb': nrm((N_EVEN, NA_HEADS, 2 * NA_WIN_H - 1, 2 * NA_WIN_W - 1), 0.1),
        'dn_conv_w': nrm((N_EVEN, DN_CONV, 3 * DN_W), DN_CONV ** -0.5),
        'dn_a_log': jnp.log(jax.random.uniform(next(keys), (N_EVEN, 2, DN_HEADS), jnp.float32, 1.0, 16.0)),
        'dn_dt_bias': nrm((N_EVEN, 2, DN_HEADS), 0.1),
        'dn_out_norm': gain((N_EVEN, HEAD_DIM)),
        'e_mem_q_norm': gain((N_EVEN, HEAD_DIM)),
        'e_mem_k_norm': gain((N_EVEN, HEAD_DIM)),
        'o_ln_g': gain((N_ODD, D)),
        'o_w_in': nrm((N_ODD, D, ODD_IN), D ** -0.5),
        'o_w_mem_kv': nrm((N_ODD, D, 2 * MEM_W), D ** -0.5),
        'o_w_out': nrm((N_ODD, MIX_W, D), MIX_W ** -0.5),
        'df_q_norm': gain((N_ODD, HEAD_DIM)),
        'df_k_norm': gain((N_ODD, HEAD_DIM)),
        'df_lambda': nrm((N_ODD, 4, HEAD_DIM), 0.1),
        'df_subln': gain((N_ODD, DIFF_VDIM)),
        'o_mem_q_norm': gain((N_ODD, HEAD_DIM)),
        'o_mem_k_norm': gain((N_ODD, HEAD_DIM)),
    }


def reference(x, mem, mem_norm_g, e_ln_g, e_w_in, e_w_mem_kv, e_w_out, na_q_norm, na_k_norm, na_rpb,
              dn_conv_w, dn_a_log, dn_dt_bias, dn_out_norm, e_mem_q_norm, e_mem_k_norm,
              o_ln_g, o_w_in, o_w_mem_kv, o_w_out, df_q_norm, df_k_norm, df_lambda, df_subln,
              o_mem_q_norm, o_mem_k_norm):
    mem_n = rmsnorm(mem, mem_norm_g)
    for layer in range(DEPTH):
        i = layer // 2
        if layer % 2 == 0:
            x = even_layer(x, mem_n, e_ln_g[i], e_w_in[i], e_w_mem_kv[i], e_w_out[i], na_q_norm[i], na_k_norm[i],
                           na_rpb[i], dn_conv_w[i], dn_a_log[i], dn_dt_bias[i], dn_out_norm[i],
                           e_mem_q_norm[i], e_mem_k_norm[i])
        else:
            x = odd_layer(x, mem_n, layer, o_ln_g[i], o_w_in[i], o_w_mem_kv[i], o_w_out[i], df_q_norm[i],
                          df_k_norm[i], df_lambda[i], df_subln[i], o_mem_q_norm[i], o_mem_k_norm[i])
    return x
```

```python
import numpy as np
import concourse.bass as bass
import concourse.mybir as mybir
from concourse.bass_utils import run_bass_kernel_spmd

F32 = mybir.dt.float32
BF16 = mybir.dt.bfloat16
ALU = mybir.AluOpType
AF = mybir.ActivationFunctionType
AX = mybir.AxisListType


class Trk:
    __slots__ = ("name", "w", "r", "dsem", "dcnt", "excl")

    def __init__(self, name, excl=False):
        self.name = name
        self.excl = excl
        self.w = None
        self.r = {}
        self.dsem = None
        self.dcnt = 0


class V:
    __slots__ = ("trk", "ap")

    def __init__(self, trk, ap):
        self.trk = trk
        self.ap = ap

    def __getitem__(self, idx):
        return V(self.trk, self.ap[idx])

    def sub(self, name, idx=None):
        return V(Trk(name), self.ap if idx is None else self.ap[idx])

    def bitcast(self, dt):
        return V(self.trk, self.ap.bitcast(dt))

    def re(self, pat, **kw):
        return V(self.trk, self.ap.rearrange(pat, **kw))


class K:
    def __init__(self, nc):
        self.nc = nc
        self.E = {"pe": nc.tensor, "act": nc.scalar, "dve": nc.vector, "pool": nc.gpsimd, "sp": nc.sync}
        self.sems = {}
        self.cnt = {}
        for e in ("pe", "act", "dve", "pool"):
            self.sems[e] = nc.alloc_semaphore("s_" + e)
            self.cnt[e] = 0
        self.waited = {}
        self.dma_trks = []
        self.nsem = 4
        self.uid = 0
        self.guards = []
        self.marks = []
        self.free_dsems = []
        self.dtot = {}
        self.cc_sem = None
        self.cc_cnt = 0

    def sb(self, name, shape, dt=F32):
        self.uid += 1
        g = self.nc.sbuf_tensor("%s_%d" % (name, self.uid), list(shape), dt)
        t = g.__enter__()
        v = V(Trk(name), t[:])
        self.guards.append((g, v))
        return v

    def push(self):
        self.marks.append(len(self.guards))

    def pop(self):
        self.barrier()
        m = self.marks.pop()
        while len(self.guards) > m:
            g, v = self.guards.pop()
            if v.trk.dsem is not None:
                self.free_dsems.append(v.trk.dsem)
                v.trk.dsem = None
            g.__exit__(None, None, None)

    def ps(self, name, shape, dt=F32):
        self.uid += 1
        t = self.nc.alloc_psum_tensor("%s_%d" % (name, self.uid), list(shape), dt)
        return V(Trk(name, excl=True), t[:])

    def dram(self, name, shape, dt=F32, kind="Internal"):
        t = self.nc.dram_tensor(name, list(shape), dt, kind=kind)
        return V(None, t.ap())

    def _dsem(self, trk):
        if trk.dsem is None:
            if self.free_dsems:
                trk.dsem = self.free_dsems.pop()
            else:
                self.nsem += 1
                key = "d%d" % self.nsem
                self.sems[key] = self.nc.alloc_semaphore("d_%d" % self.nsem)
                self.dtot[key] = 0
                trk.dsem = key
        return trk.dsem

    def _deps(self, eng, w, r):
        need = {}

        def add(dep):
            if dep is None:
                return
            k, v = dep
            if k == eng and eng == "pe":
                return
            if need.get(k, 0) < v:
                need[k] = v
        for x in r:
            if x is not None and x.trk is not None:
                add(x.trk.w)
                if x.trk.excl:
                    for k, v in x.trk.r.items():
                        if k != eng:
                            add((k, v))
        for x in w:
            if x is not None and x.trk is not None:
                add(x.trk.w)
                for k, v in x.trk.r.items():
                    add((k, v))
        return need

    def _wait(self, eng, need):
        E = self.E[eng]
        for k, v in need.items():
            if self.waited.get((eng, k), 0) >= v:
                continue
            self.waited[(eng, k)] = v
            E.wait_ge(self.sems[k], v)

    def emit(self, eng, fn, w=(), r=(), sig=True):
        need = self._deps(eng, w, r)
        self._wait(eng, need)
        ins = fn(self.E[eng])
        n = self.cnt[eng] + 1
        if sig:
            self.cnt[eng] = n
            ins.then_inc(self.sems[eng], 1)
        for x in r:
            if x is not None and x.trk is not None:
                x.trk.r[eng] = n
        for x in w:
            if x is not None and x.trk is not None:
                x.trk.w = (eng, n)
                x.trk.r = {}
        return ins

    def dma(self, out, in_, eng="sp", **kw):
        own = out if out.trk is not None else in_
        assert own.trk is not None
        sk = self._dsem(own.trk)
        need = {}
        if in_.trk is not None and in_.trk.w is not None:
            need[in_.trk.w[0]] = in_.trk.w[1]
        if out.trk is not None:
            if out.trk.w is not None and out.trk.w[0] != sk:
                need[out.trk.w[0]] = max(need.get(out.trk.w[0], 0), out.trk.w[1])
            for k2, v2 in out.trk.r.items():
                need[k2] = max(need.get(k2, 0), v2)
        need.pop(None, None)
        self._wait(eng, need)
        ins = self.E[eng].dma_start(out=out.ap, in_=in_.ap, **kw)
        self.dtot[sk] += 1
        val = 16 * self.dtot[sk]
        ins.then_inc(self.sems[sk], 16)
        if out.trk is not None:
            out.trk.w = (sk, val)
            out.trk.r = {}
        if in_.trk is not None:
            in_.trk.r[sk] = val
        return ins

    def barrier(self):
        tot = {}
        for e in ("pe", "act", "dve", "pool"):
            if self.cnt[e] > 0:
                tot[e] = self.cnt[e]
        for key, n in self.dtot.items():
            if n > 0:
                tot[key] = 16 * n
        if self.cc_cnt:
            tot["cc"] = self.cc_cnt
        for e in ("pe", "act", "dve", "pool", "sp"):
            self._wait(e, dict(tot))

    def allgather(self, out, in_, groups, inc=1):
        self.barrier()
        if self.cc_sem is None:
            self.cc_sem = self.nc.alloc_semaphore("cc")
            self.sems["cc"] = self.cc_sem
        ins = self.E["pool"].collective_compute("AllGather", ALU.bypass, replica_groups=groups,
                                                ins=[in_.ap.opt()], outs=[out.ap.opt()])
        ins.then_inc(self.cc_sem, inc)
        self.cc_cnt += inc
        self.barrier()

    def mm(self, out, lhsT, rhs, start=True, stop=True, sig=None, **kw):
        if sig is None:
            sig = True
        return self.emit("pe", lambda e: e.matmul(out.ap, lhsT.ap, rhs.ap, start=start, stop=stop, **kw),
                         w=[out], r=[lhsT, rhs], sig=sig)

    def tr(self, out, in_, ident, sig=True):
        return self.emit("pe", lambda e: e.transpose(out.ap, in_.ap, ident.ap), w=[out], r=[in_, ident], sig=sig)

    def act(self, out, in_, func, bias=None, scale=None, accum_out=None, eng="act"):
        kw = {}
        r = [in_]
        if bias is not None:
            if isinstance(bias, V):
                kw["bias"] = bias.ap
                r.append(bias)
            else:
                kw["bias"] = bias
        if scale is not None:
            if isinstance(scale, V):
                kw["scale"] = scale.ap
                r.append(scale)
            else:
                kw["scale"] = scale
        w = [out]
        if accum_out is not None:
            kw["accum_out"] = accum_out.ap
            w.append(accum_out)
        return self.emit("act", lambda e: e.activation(out.ap, in_.ap, func, **kw), w=w, r=r)

    def tt(self, eng, out, a, b, op):
        return self.emit(eng, lambda e: e.tensor_tensor(out=out.ap, in0=a.ap, in1=b.ap, op=op), w=[out], r=[a, b])

    def ts(self, eng, out, a, s1, op0, s2=None, op1=None, accum_out=None):
        r = [a]
        s1a = s1.ap if isinstance(s1, V) else s1
        s2a = s2.ap if isinstance(s2, V) else s2
        if isinstance(s1, V):
            r.append(s1)
        if isinstance(s2, V):
            r.append(s2)
        kw = {}
        w = [out]
        if op1 is not None:
            kw["op1"] = op1
        if accum_out is not None:
            kw["accum_out"] = accum_out.ap
            w.append(accum_out)
        return self.emit(eng, lambda e: e.tensor_scalar(out=out.ap, in0=a.ap, scalar1=s1a, scalar2=s2a, op0=op0, **kw),
                         w=w, r=r)

    def stt(self, out, a, s, b, op0, op1, eng="dve"):
        r = [a, b]
        sa = s.ap if isinstance(s, V) else s
        if isinstance(s, V):
            r.append(s)
        return self.emit(eng, lambda e: e.scalar_tensor_tensor(out=out.ap, in0=a.ap, scalar=sa, in1=b.ap, op0=op0, op1=op1),
                         w=[out], r=r)

    def cp(self, eng, out, in_):
        if eng == "act":
            return self.emit("act", lambda e: e.activation(out.ap, in_.ap, AF.Copy), w=[out], r=[in_])
        return self.emit(eng, lambda e: e.tensor_copy(out=out.ap, in_=in_.ap), w=[out], r=[in_])

    def memset(self, eng, out, val):
        return self.emit(eng, lambda e: e.memset(out.ap, val), w=[out], r=[])

    def recip(self, out, in_, eng="dve"):
        return self.emit(eng, lambda e: e.reciprocal(out=out.ap, in_=in_.ap), w=[out], r=[in_])


T = 4096
D = 2048
EPS = 1e-6
NTB = 8
DBG = 99
NFM1 = 25
TM1 = 396


def rms_fm(k, B, ones, epsc, src, dst, gain, n, sq, rs, rstd, nfeat):
    k.act(sq[:, :n], src, AF.Square)
    k.mm(B[:, :n], ones, sq[:, :n])
    k.act(rs[:, :n], B[:, :n], AF.Sqrt, scale=1.0 / nfeat, bias=epsc)
    k.recip(rstd[:, :n], rs[:, :n])
    k.stt(dst, src, gain, rstd[:, :n], ALU.mult, ALU.mult)


def stage_proj(k, Bk, xT, win, lng, ncols_fm, ncols_tm, projT, tm_outs, ones, epsc):
    nfm = ncols_fm // 128
    ncol = ncols_fm + ncols_tm
    k.push()
    W = k.sb("W", [128, 16, ncol], BF16)
    for kc in range(16):
        k.dma(W[:, kc, :], win[kc * 128:(kc + 1) * 128, :], eng="pool")
    lng_sb = k.sb("lng", [128, 16])
    k.dma(lng_sb, lng)
    xb = k.sb("xb", [128, 16, 512], F32)
    hT = [k.sb("hT%d" % i, [128, 16, 512], BF16) for i in range(2)]
    sq = [k.sb("sq%d" % i, [128, 512], BF16) for i in range(2)]
    rs = k.sb("rs", [128, 512])
    rstd = k.sb("rstd", [128, 512])
    stg = [k.sb("stg%d" % i, [128, 512], BF16) for i in range(4)]
    stt_ = [[k.sb("stt%d_%d" % (j, i), [128, w], dt) for i in range(2)] for j, (_, _, w, dt, _m) in enumerate(tm_outs)]
    xv = xT.re("(kc p) t -> p kc t", p=128)
    nblk = min(NTB, T // 512)

    def load(tb):
        t0 = tb * 512
        for q4 in range(4):
            k.dma(xb[:, q4 * 4:(q4 + 1) * 4, :], xv[:, q4 * 4:(q4 + 1) * 4, t0:t0 + 512])

    def prologue(tb):
        for kc in range(16):
            k.act(sq[kc % 2], xb[:, kc, :], AF.Square)
            k.mm(Bk[0], ones, sq[kc % 2], start=(kc == 0), stop=(kc == 15))
        k.act(rs, Bk[0], AF.Sqrt, scale=1.0 / D, bias=epsc)
        k.recip(rstd, rs)
        h = hT[tb % 2]
        for kc in range(16):
            k.stt(h[:, kc, :], xb[:, kc, :], lng_sb[:, kc:kc + 1], rstd, ALU.mult, ALU.mult)

    def fm(tb, c):
        t0 = tb * 512
        h = hT[tb % 2]
        ps = Bk[1 + c % 4]
        for kc in range(16):
            k.mm(ps, W[:, kc, c * 128:(c + 1) * 128], h[:, kc, :], start=(kc == 0), stop=(kc == 15), sig=(kc == 15))
        st = stg[c % 4]
        k.cp("act" if c % 2 else "dve", st, ps)
        k.dma(projT[c * 128:(c + 1) * 128, t0:t0 + 512], st)

    def tm(tb):
        t0 = tb * 512
        h = hT[tb % 2]
        for s4 in range(4):
            tok = t0 + s4 * 128
            for j, (dv, c0, w, dt, mode) in enumerate(tm_outs):
                ps = Bk[5 + (s4 * len(tm_outs) + j) % 3]
                for kc in range(16):
                    k.mm(ps[:, :w], h[:, kc, s4 * 128:(s4 + 1) * 128], W[:, kc, ncols_fm + c0:ncols_fm + c0 + w],
                         start=(kc == 0), stop=(kc == 15), sig=(kc == 15))
                st = stt_[j][s4 % 2]
                k.cp("dve" if j % 2 == 0 else "act", st, ps[:, :w])
                if mode == "rows":
                    k.dma(dv[tok:tok + 128, c0:c0 + w], st)
                else:
                    tl = tok // 128
                    k.dma(dv[:, tl * w:(tl + 1) * w], st)

    load(0)
    prologue(0)
    for tb in range(nblk):
        if tb + 1 < nblk:
            load(tb + 1)
        half = nfm // 2
        for c in range(half):
            fm(tb, c)
        if tb + 1 < nblk:
            prologue(tb + 1)
        for c in range(half, nfm):
            fm(tb, c)
        tm(tb)
    k.pop()


def stage_memkv(k, Bk, memT, wkv, memg, kgain, ones, epsc, kmemT, vmem):
    k.push()
    wk = k.sb("wkv", [128, 16, 512], BF16)
    k.dma(wk, wkv.re("(kc p) c -> p kc c", p=128), eng="pool")
    mg = k.sb("memg", [128, 16])
    k.dma(mg, memg)
    mT = k.sb("mT", [128, 16, 256])
    k.dma(mT, memT.re("(kc p) m -> p kc m", p=128))
    mn = k.sb("mn", [128, 16, 256], BF16)
    sq = [k.sb("sqm%d" % i, [128, 256], BF16) for i in range(2)]
    rs = k.sb("rsm", [128, 256])
    rstd = k.sb("rstdm", [128, 256])
    kraw = k.sb("kraw", [128, 256])
    for kc in range(16):
        k.act(sq[kc % 2], mT[:, kc, :], AF.Square)
        k.mm(Bk[0][:, :256], ones, sq[kc % 2], start=(kc == 0), stop=(kc == 15))
    k.act(rs, Bk[0][:, :256], AF.Sqrt, scale=1.0 / D, bias=epsc)
    k.recip(rstd, rs)
    for kc in range(16):
        k.stt(mn[:, kc, :], mT[:, kc, :], mg[:, kc:kc + 1], rstd, ALU.mult, ALU.mult)
    for h in range(2):
        ps = Bk[1 + h][:, :256]
        for kc in range(16):
            k.mm(ps, wk[:, kc, h * 128:(h + 1) * 128], mn[:, kc, :], start=(kc == 0), stop=(kc == 15))
        k.cp("dve", kraw, ps)
        rms_fm(k, Bk[3], ones, epsc, kraw, kmemT[:, h, :], kgain, 256, sq[0], rs, rstd, 128)
    k.memset("pool", vmem, 1.0)
    for mc in range(2):
        ps = Bk[4 + mc][:, :256]
        for kc in range(16):
            k.mm(ps, mn[:, kc, mc * 128:(mc + 1) * 128], wk[:, kc, 256:512], start=(kc == 0), stop=(kc == 15))
        for h in range(2):
            k.cp("dve", vmem[:, mc, h, 0:128], ps[:, h * 128:(h + 1) * 128])
    k.pop()


def stage_memattn(k, Bk, projT, qchunk0, gchunk0, qgain, kmemT, vmem, ones, epsc, ident, out, orow0):
    k.push()
    qT = k.sb("mqT", [128, T], BF16)
    gT = k.sb("mgT", [128, T], BF16)
    qn2 = [k.sb("mqn%d" % i, [128, 512], BF16) for i in range(2)]
    sq2 = [k.sb("msq%d" % i, [128, 512], BF16) for i in range(2)]
    rs2 = [k.sb("mrs%d" % i, [128, 512]) for i in range(2)]
    rstd2 = [k.sb("mrstd%d" % i, [128, 512]) for i in range(2)]
    pT = [k.sb("mpT%d" % i, [128, 2, 512], BF16) for i in range(2)]
    rc = [k.sb("mrc%d" % i, [128, 1]) for i in range(2)]
    on = [k.sb("mon%d" % i, [128, 128], BF16) for i in range(2)]
    mixst = k.sb("mmix", [128, T], BF16)
    for h in range(2):
        k.dma(qT, projT[(qchunk0 + h) * 128:(qchunk0 + h + 1) * 128, :])
        k.dma(gT, projT[(gchunk0 + h) * 128:(gchunk0 + h + 1) * 128, :])
        k.act(gT, gT, AF.Silu)
        for blk in range(T // 512):
            t0 = blk * 512
            b2 = blk % 2
            qn = qn2[b2]
            rms_fm(k, Bk[0] if b2 == 0 else Bk[7], ones, epsc, qT[:, t0:t0 + 512], qn, qgain, 512, sq2[b2], rs2[b2], rstd2[b2], 128)
            p = pT[blk % 2]
            for mc in range(2):
                k.mm(Bk[1 + mc], kmemT[:, h, mc * 128:(mc + 1) * 128], qn)
                k.act(p[:, mc, :], Bk[1 + mc], AF.Exp)
            for s4 in range(4):
                po = Bk[3 + s4 % 2][:, :129]
                for mc in range(2):
                    k.mm(po, p[:, mc, s4 * 128:(s4 + 1) * 128], vmem[:, mc, h, :], start=(mc == 0), stop=(mc == 1))
                r_ = rc[s4 % 2]
                o_ = on[s4 % 2]
                k.recip(r_, po[:, 128:129])
                k.ts("dve", o_, po[:, 0:128], r_, ALU.mult)
                pt = Bk[5 + s4 % 2].bitcast(BF16)[:, :128]
                k.tr(pt, o_, ident)
                tok = t0 + s4 * 128
                k.tt("dve", mixst[:, tok:tok + 128], pt, gT[:, tok:tok + 128], ALU.mult)
        k.dma(out[orow0 + h * 128:orow0 + (h + 1) * 128, :], mixst)
    k.pop()


def stage_na(k, Bk, projT, vna, nab, nam, qgain, kgain, ones, epsc, ident, out):
    k.push()
    qT = k.sb("nqT", [128, T], BF16)
    kT = k.sb("nkT", [128, T], BF16)
    gT = k.sb("ngT", [128, T], BF16)
    qn = k.sb("nqn", [128, T], BF16)
    kn = k.sb("nkn", [128, T], BF16)
    vt = k.sb("nvt", [128, 32, 129], BF16)
    vs = k.sb("nvs", [128, 31, 129], BF16)
    bt = k.sb("nbt", [128, 8, 256])
    nm = k.sb("nnm", [128, 256])
    sq4 = [k.sb("nsq%d" % i, [128, 512], BF16) for i in range(4)]
    rs4 = [k.sb("nrs%d" % i, [128, 512]) for i in range(4)]
    rstd4 = [k.sb("nrstd%d" % i, [128, 512]) for i in range(4)]
    sT = [k.sb("nsT%d" % i, [128, 256]) for i in range(2)]
    pT = [k.sb("npT%d" % i, [128, 256], BF16) for i in range(2)]
    rc = [k.sb("nrc%d" % i, [64, 1]) for i in range(2)]
    on = [k.sb("non%d" % i, [64, 128], BF16) for i in range(2)]
    mixst = k.sb("nmix", [128, T], BF16)
    k.dma(nm, nam)
    k.memset("pool", vt, 1.0)
    k.memset("pool", vs, 1.0)
    for h in range(3):
        k.dma(qT, projT[(0 + h) * 128:(1 + h) * 128, :])
        k.dma(kT, projT[(3 + h) * 128:(4 + h) * 128, :])
        k.dma(gT, projT[(6 + h) * 128:(7 + h) * 128, :])
        k.dma(vt[:, :, 0:128], vna.re("(tl p) c -> p tl c", p=128)[:, :, h * 128:(h + 1) * 128])
        k.dma(vs[:, :, 0:128], vna[64:64 + 31 * 128, :].re("(tl p) c -> p tl c", p=128)[:, :, h * 128:(h + 1) * 128])
        k.dma(bt, nab[h].re("t p f -> p t f"))
        for t in range(8):
            k.tt("pool", bt[:, t, :], bt[:, t, :], nm, ALU.add)
        k.act(gT, gT, AF.Silu)
        for blk in range(T // 512):
            t0 = blk * 512
            b2 = blk % 2
            rms_fm(k, Bk[0 + 2 * b2], ones, epsc, qT[:, t0:t0 + 512], qn[:, t0:t0 + 512], qgain, 512, sq4[b2], rs4[b2], rstd4[b2], 128)
            rms_fm(k, Bk[1 + 2 * b2], ones, epsc, kT[:, t0:t0 + 512], kn[:, t0:t0 + 512], kgain, 512, sq4[2 + b2], rs4[2 + b2], rstd4[2 + b2], 128)
        def s1(r):
            r0 = min(max(r - 4, 0), 56)
            ty = r if r < 4 else (4 if r <= 60 else r - 56)
            ps = Bk[2 + r % 2][:, :256]
            for c in range(4):
                kt0 = (r0 + 2 * c) * 64
                k.mm(ps[:, c * 64:(c + 1) * 64], kn[:, kt0:kt0 + 128], qn[:, r * 64:(r + 1) * 64])
            k.tt("dve", sT[r % 2], ps, bt[:, ty, :], ALU.add)
            k.act(pT[r % 2], sT[r % 2], AF.Exp)

        def s2(r):
            r0 = min(max(r - 4, 0), 56)
            po = Bk[4 + r % 2][:64, :129]
            for c in range(4):
                row0 = r0 + 2 * c
                vv = vt[:, row0 // 2, :] if row0 % 2 == 0 else vs[:, (row0 - 1) // 2, :]
                k.mm(po, pT[r % 2][:, c * 64:(c + 1) * 64], vv, start=(c == 0), stop=(c == 3))
            k.recip(rc[r % 2], po[:, 128:129])
            k.ts("dve", on[r % 2], po[:, 0:128], rc[r % 2], ALU.mult)

        def s3(r):
            pt = Bk[6 + r % 2].bitcast(BF16)[:, :64]
            k.tr(pt, on[r % 2], ident[:64, :64])
            k.tt("dve", mixst[:, r * 64:(r + 1) * 64], pt, gT[:, r * 64:(r + 1) * 64], ALU.mult)

        for i in range(64 + 2):
            if i < 64:
                s1(i)
            if 0 <= i - 1 < 64:
                s2(i - 1)
            if 0 <= i - 2 < 64:
                s3(i - 2)
        k.dma(out[h * 128:(h + 1) * 128, :], mixst)
    k.pop()


def consts(k):
    ones = k.sb("ones", [128, 128], BF16)
    k.memset("dve", ones, 1.0)
    epsc = k.sb("epsc", [128, 1])
    k.memset("dve", epsc, EPS)
    identf = k.sb("identf", [128, 128])
    ident = k.sb("ident", [128, 128], BF16)
    k.memset("pool", identf, 0.0)
    k.emit("pool", lambda e: e.affine_select(out=identf.ap, in_=identf.ap, pattern=[[-1, 128]], base=0,
                                              channel_multiplier=1, compare_op=ALU.not_equal, fill=1.0),
           w=[identf], r=[identf])
    k.cp("pool", ident, identf)
    return ones, epsc, ident, identf


def build_p1(stages=("A", "M", "NA", "ME", "DN")):
    nc = bass.Bass("TRN2", target_bir_lowering=False)
    k = K(nc)
    xT = k.dram("xT", [D, T], F32, "ExternalInput")
    memT = k.dram("memT", [D, 256], F32, "ExternalInput")
    win = k.dram("win", [D, NFM1 * 128 + TM1], F32, "ExternalInput")
    wkv = k.dram("wkv", [D, 512], F32, "ExternalInput")
    lng = k.dram("lng", [128, 16], F32, "ExternalInput")
    memg = k.dram("memg", [128, 16], F32, "ExternalInput")
    hv = k.dram("hv", [128, 8], F32, "ExternalInput")
    nab = k.dram("nab", [3, 8, 128, 256], F32, "ExternalInput")
    nam = k.dram("nam", [128, 256], F32, "ExternalInput")
    dnrow = k.dram("dnrow", [128, 2, 2, 32, 3], F32, "ExternalInput")
    convw = k.dram("convw", [128, 9, 5], F32, "ExternalInput")
    onrow = k.dram("onrow", [128, 128], F32, "ExternalInput")
    out = k.dram("mixT", [1024, T], BF16, "ExternalOutput")
    projT = k.dram("projT", [NFM1 * 128, T], BF16)
    vna = k.dram("vna", [T, 384], BF16)
    ba = k.dram("ba", [128, 32 * 12], F32)
    Bk = [k.ps("bk%d" % i, [128, 512], F32) for i in range(8)]
    ones, epsc, ident, identf = consts(k)
    hvs = k.sb("hvs", [128, 8])
    k.dma(hvs, hv)
    k.ts("dve", hvs[:, 4:5], hvs[:, 0:1], 128.0 ** -0.5, ALU.mult)
    k.ts("dve", hvs[:, 5:6], hvs[:, 2:3], 128.0 ** -0.5, ALU.mult)
    kmemT = k.sb("kmemT", [128, 2, 256], BF16)
    vmem = k.sb("vmem", [128, 2, 2, 129], BF16)
    if "A" in stages:
        stage_proj(k, Bk, xT, win, lng, NFM1 * 128, TM1, projT,
                   [(vna, 0, 384, BF16, "rows"), (ba, 384, 12, F32, "ptile")], ones, epsc)
    if "M" in stages:
        stage_memkv(k, Bk, memT, wkv, memg, hvs[:, 3:4], ones, epsc, kmemT, vmem)
    k.barrier()
    if "NA" in stages:
        stage_na(k, Bk, projT, vna, nab, nam, hvs[:, 4:5], hvs[:, 1:2], ones, epsc, ident, out)
    if "ME" in stages:
        stage_memattn(k, Bk, projT, 21, 23, hvs[:, 5:6], kmemT, vmem, ones, epsc, ident, out, 768)
    if "DN" in stages:
        stage_dn(k, Bk, projT, ba, dnrow, convw, onrow, ident, ones, epsc, out)
    k.barrier()
    return nc


def na_tables(rpb_g):
    p = np.arange(128)
    c = np.arange(4)
    q = np.arange(64)
    o = (2 * c[None, :] + (p[:, None] // 64))
    kc = p % 64
    dcm = np.clip(kc[:, None] - q[None, :], -15, 15) + 15
    c0 = np.clip(q - 8, 0, 48)
    ok = (kc[:, None] >= c0[None, :]) & (kc[:, None] < c0[None, :] + 16)
    tabs = np.zeros((3, 8, 128, 4, 64), np.float32)
    rows = [0, 1, 2, 3, 30, 61, 62, 63]
    for ti, r in enumerate(rows):
        r0 = min(max(r - 4, 0), 56)
        dr = r0 + o - r + 7
        tabs[:, ti] = rpb_g[:, dr[:, :, None], dcm[:, None, :]]
    negm = np.where(ok, 0.0, -30000.0).astype(np.float32)
    negm = np.broadcast_to(negm[:, None, :], (128, 4, 64)).reshape(128, 256)
    return tabs.reshape(3, 8, 128, 256), np.ascontiguousarray(negm)


def prep_p1(inp, b, g):
    f = np.float32
    x = inp["x"][b]
    w = inp["e_w_in"][0]
    NAW = 768
    def hcols(base, h0, nh, hd=128):
        return list(range(base + h0 * hd, base + (h0 + nh) * hd))
    na_q, na_k, na_v, na_g = 0, NAW, 2 * NAW, 3 * NAW
    dn0 = 4 * NAW
    dn_q, dn_k, dn_v, dn_g = dn0, dn0 + 768, dn0 + 1536, dn0 + 2304
    ba0 = dn0 + 3072
    mq0 = ba0 + 24
    mg0 = mq0 + 512
    cols = []
    cols += hcols(na_q, 3 * g, 3) + hcols(na_k, 3 * g, 3) + hcols(na_g, 3 * g, 3)
    cols += hcols(dn_q, 3 * g, 3) + hcols(dn_k, 3 * g, 3) + hcols(dn_v, 3 * g, 3) + hcols(dn_g, 3 * g, 3)
    cols += hcols(mq0, 2 * g, 2) + hcols(mg0, 2 * g, 2)
    cols += hcols(na_v, 3 * g, 3)
    for j in range(4):
        cols += [ba0 + 6 * j + 3 * g + i for i in range(3)]
    cols = np.asarray(cols)
    assert len(cols) == NFM1 * 128 + TM1
    wkv = inp["e_w_mem_kv"][0]
    kvc = hcols(0, 2 * g, 2) + hcols(512, 2 * g, 2)
    hv = np.zeros((128, 8), f)
    hv[:, 0] = inp["na_q_norm"][0]
    hv[:, 1] = inp["na_k_norm"][0]
    hv[:, 2] = inp["e_mem_q_norm"][0]
    hv[:, 3] = inp["e_mem_k_norm"][0]
    nab, nam = na_tables(inp["na_rpb"][0][3 * g:3 * g + 3])
    dnrow = np.zeros((128, 2, 2, 32, 3), f)
    for d_ in range(2):
        dnrow[:, 0, d_] = inp["dn_dt_bias"][0][d_, 3 * g:3 * g + 3][None, None, :]
        dnrow[:, 1, d_] = inp["dn_a_log"][0][d_, 3 * g:3 * g + 3][None, None, :]
    cw = inp["dn_conv_w"][0]
    convw = np.zeros((128, 9, 5), f)
    for X in range(3):
        for hh in range(3):
            c0 = X * 768 + (3 * g + hh) * 128
            convw[:, X * 3 + hh, :] = cw[:, c0:c0 + 128].T
    onrow = np.ascontiguousarray(np.broadcast_to(inp["dn_out_norm"][0][None, :], (128, 128))).astype(f)
    return {
        "xT": np.ascontiguousarray(x.T),
        "memT": np.ascontiguousarray(inp["mem"][b].T),
        "win": np.ascontiguousarray(w[:, cols]),
        "wkv": np.ascontiguousarray(wkv[:, kvc]),
        "lng": np.ascontiguousarray(inp["e_ln_g"][0].reshape(16, 128).T),
        "memg": np.ascontiguousarray(inp["mem_norm_g"].reshape(16, 128).T),
        "hv": hv,
        "nab": nab,
        "nam": nam,
        "dnrow": dnrow,
        "convw": convw,
        "onrow": onrow,
    }


T = 4096
NT_ = 32


def tri_consts(k, identb):
    c = {}
    onesf = k.sb("onesf", [128, 128])
    k.memset("pool", onesf, 1.0)
    c["onesf"] = onesf
    def sel(name, src_val, pattern, cm, base, op, fill, dt):
        tmp = k.sb(name + "_f", [128, 128])
        k.memset("pool", tmp, src_val)
        k.emit("pool", lambda e: e.affine_select(out=tmp.ap, in_=tmp.ap, pattern=pattern, base=base,
                                                  channel_multiplier=cm, compare_op=op, fill=fill),
               w=[tmp], r=[tmp])
        if dt == F32:
            return tmp
        o = k.sb(name, [128, 128], dt)
        k.cp("pool", o, tmp)
        return o
    c["U"] = [sel("Uf", 1.0, [[1, 128]], -1, 0, ALU.is_ge, 0.0, F32),
              sel("Ub", 1.0, [[-1, 128]], 1, 0, ALU.is_ge, 0.0, F32)]
    c["NM"] = [sel("NMf", 0.0, [[-1, 128]], 1, 0, ALU.is_ge, -1e30, BF16),
               sel("NMb", 0.0, [[1, 128]], -1, 0, ALU.is_ge, -1e30, BF16)]
    c["ST"] = [sel("STf", 1.0, [[-1, 128]], 1, -1, ALU.is_ge, 0.0, BF16),
               sel("STb", 1.0, [[1, 128]], -1, -1, ALU.is_ge, 0.0, BF16)]
    return c


def dn_gates(k, Bk, ba, dnrow, tc):
    ba_sb = k.sb("ba_sb", [128, 32, 12])
    k.dma(ba_sb, ba.re("p (t c) -> p t c", c=12))
    row = k.sb("dnrow", [128, 2, 2, 32, 3])
    k.dma(row, dnrow)
    one1 = k.sb("one1", [128, 1])
    k.memset("dve", one1, 1.0)
    G = {}
    def arr(n):
        G[n] = k.sb("g_" + n, [128, 2, 32, 3])
        return G[n]
    def v4(a):
        return a.re("p d t h -> p t d h")
    bsrc = ba_sb[:, :, 0:6].re("p t (d h) -> p t d h", d=2)
    asrc = ba_sb[:, :, 6:12].re("p t (d h) -> p t d h", d=2)
    beta = arr("beta")
    k.act(v4(beta), bsrc, AF.Sigmoid)
    z = arr("z")
    k.tt("dve", v4(z), asrc, v4(row[:, 0]), ALU.add)
    k.act(z, z, AF.Exp)
    k.act(z, z, AF.Ln, bias=one1)
    A = arr("A")
    k.act(A, row[:, 1], AF.Exp)
    g = arr("g")
    k.stt(g, z, -1.0, A, ALU.mult, ALU.mult)
    gneg = arr("gneg")
    k.ts("dve", gneg, g, -1.0, ALU.mult)
    nbeta = arr("nbeta")
    k.ts("dve", nbeta, beta, -1.0, ALU.mult)
    Gc = arr("G")
    Gt = arr("Gt")
    for d in range(2):
        k.mm(Bk[6][:, :96], tc["U"][d], g[:, d].re("p t h -> p (t h)"))
        k.cp("dve", Gc[:, d].re("p t h -> p (t h)"), Bk[6][:, :96])
        k.mm(Bk[7][:, :96], tc["onesf"], g[:, d].re("p t h -> p (t h)"))
        k.cp("dve", Gt[:, d].re("p t h -> p (t h)"), Bk[7][:, :96])
    eG = arr("eG")
    k.act(eG, Gc, AF.Exp)
    beG = arr("beG")
    k.tt("dve", beG, beta, eG, ALU.mult)
    ekd = arr("ekd")
    k.tt("dve", ekd, Gt, Gc, ALU.subtract)
    k.act(ekd, ekd, AF.Exp)
    gend = arr("gend")
    k.act(gend, Gt, AF.Exp)
    return G


def dn_prep_head(k, Bk, projT, h, convw, ident, ones, epsc, qn, kn, vf, colq, colk):
    k.push()
    xp = k.sb("xp", [128, T + 4], BF16)
    dg = k.sb("dg", [128, 5, 128], BF16)
    xc = [k.sb("xc%d" % i, [128, 512]) for i in range(2)]
    sq2 = [k.sb("dsq%d" % i, [128, 512], BF16) for i in range(2)]
    rs2 = [k.sb("drs%d" % i, [128, 512]) for i in range(2)]
    rstd2 = [k.sb("drstd%d" % i, [128, 512]) for i in range(2)]
    k.memset("pool", xp[:, 0:2], 0.0)
    k.memset("pool", xp[:, T + 2:T + 4], 0.0)
    for X, (chunk0, dst) in enumerate(((9, qn), (12, kn), (15, vf))):
        k.dma(xp[:, 2:T + 2], projT[(chunk0 + h) * 128:(chunk0 + h + 1) * 128, :])
        for j in range(5):
            k.ts("dve", dg[:, j, :], ident, convw[:, X * 3 + h, j:j + 1], ALU.mult)
        for blk in range(T // 512):
            t0 = blk * 512
            ps = Bk[6 + blk % 2]
            for j in range(5):
                k.mm(ps, dg[:, j, :], xp[:, t0 + j:t0 + j + 512], start=(j == 0), stop=(j == 4))
            if X == 2:
                k.act(dst[:, t0:t0 + 512], ps, AF.Silu)
            else:
                b2 = blk % 2
                x_ = xc[b2]
                sq, rs, rstd = sq2[b2], rs2[b2], rstd2[b2]
                pb = Bk[5 - b2]
                k.act(x_, ps, AF.Silu)
                k.act(sq, x_, AF.Square)
                k.mm(pb, ones, sq)
                k.act(rs, pb, AF.Sqrt, scale=1.0, bias=epsc)
                k.recip(rstd, rs)
                k.stt(dst[:, t0:t0 + 512], x_, colq if X == 0 else colk, rstd, ALU.mult, ALU.mult)
    k.pop()


def dn_chain(k, B, h, d, G, tc, ident, qn, kn, vf, oacc, cidx=0):
    nm = "c%d%d" % (h, d)
    def bt(n, dt=BF16):
        return k.sb(nm + n, [128, 128], dt)
    D, Ds, qkm = bt("D"), bt("Ds"), bt("qkm")
    N = bt("N")
    TQ = k.sb(nm + "TQ", [128, 256], BF16)
    NT, qkT = TQ[:, 0:128], TQ[:, 128:256]
    XX = [k.sb(nm + "XX%d" % i, [128, 256], BF16) for i in range(2)]
    P = [bt("P0"), bt("P1")]
    kbe, kdec, vb, wT, vnew, Sb = bt("kbe"), bt("kdec"), bt("vb"), bt("wT"), bt("vnew"), bt("Sb")
    gnb, u, tmp, S = bt("gnb", F32), bt("u", F32), bt("tmp", F32), bt("S", F32)
    k.memset("pool", S, 0.0)
    k.memset("pool", Sb, 0.0)
    eA = "act" if cidx % 2 == 0 else "dve"
    eB = "dve" if cidx % 2 == 0 else "act"
    s = [B[:, i * 128:(i + 1) * 128] for i in range(4)]
    Bb = B.bitcast(BF16)
    sb_ = [Bb[:, i * 256:i * 256 + 128] for i in range(4)]
    yield
    order = range(NT_) if d == 0 else range(NT_ - 1, -1, -1)
    for n in order:
        tl = slice(n * 128, (n + 1) * 128)
        col = lambda a: a[:, d, n, h:h + 1]
        k.ts("dve", gnb, tc["onesf"], col(G["gneg"]), ALU.mult)
        k.mm(s[0], gnb, tc["U"][d], start=True, stop=False)
        k.mm(s[0], ident, tc["NM"][d], start=False, stop=True)
        k.act(D, s[0], AF.Exp, bias=col(G["G"]))
        k.tt("pool", Ds, D, tc["ST"][d], ALU.mult)
        yield
        k.mm(s[1], kn[:, tl], kn[:, tl])
        k.stt(N, s[1], col(G["nbeta"]), Ds, ALU.mult, ALU.mult)
        k.mm(s[2], qn[:, tl], kn[:, tl])
        k.tt("dve", qkm, s[2], D, ALU.mult)
        yield
        k.tr(sb_[2], N, ident)
        k.tr(Bb[:, 2 * 256 + 128:2 * 256 + 256], qkm, ident)
        k.cp("act", TQ, Bb[:, 2 * 256:2 * 256 + 256])
        yield
        k.tr(sb_[1], kn[:, tl], ident)
        k.tr(sb_[2], vf[:, tl], ident)
        k.ts("dve", kbe, sb_[1], col(G["beG"]), ALU.mult)
        k.ts("dve", kdec, sb_[1], col(G["ekd"]), ALU.mult)
        k.ts("dve", vb, sb_[2], col(G["beta"]), ALU.mult)
        yield
        Xc, XTc = N, NT
        k.tt("pool", P[0], NT, ident, ALU.add)
        Pc = P[0]
        pi = 1
        for l in range(6):
            eng = eA if l % 2 == 0 else eB
            xx = XX[l % 2]
            Xn, XTn = xx[:, 0:128], xx[:, 128:256]
            k.mm(s[0], XTc, Xc)
            if l < 5:
                k.mm(s[1], Xc, XTc)
                k.cp(eng, xx, B[:, 0:256])
            else:
                k.cp(eng, Xn, s[0])
            Xc, XTc = Xn, XTn
            yield
            k.mm(s[2], Xc, Pc, start=True, stop=False)
            k.mm(s[2], ident, Pc, start=False, stop=True)
            Pn = P[pi]
            pi ^= 1
            k.cp(eB if l % 2 == 0 else eA, Pn, s[2])
            Pc = Pn
            yield
        TT = Pc
        k.mm(s[0], TT, vb)
        k.mm(s[1], kbe, TT)
        k.cp("act", u, s[0])
        k.cp("act", wT, s[1])
        yield
        k.mm(s[2], wT, Sb)
        k.tt("dve", vnew, u, s[2], ALU.subtract)
        yield
        k.mm(s[3], qn[:, tl], Sb)
        k.stt(tmp, s[3], col(G["eG"]), oacc[:, n, :], ALU.mult, ALU.add)
        k.mm(s[0], qkT, vnew)
        k.tt("dve", oacc[:, n, :], s[0], tmp, ALU.add)
        k.mm(s[1], kdec, vnew)
        k.stt(S, S, col(G["gend"]), s[1], ALU.mult, ALU.add)
        k.cp("pool", Sb, S)
        yield


def dn_out_head(k, Bk, projT, h, oacc, onrow, ident, epsc, out, orow0):
    k.push()
    gT = k.sb("dgT", [128, T], BF16)
    mixst = k.sb("dmix", [128, T], BF16)
    junk = k.sb("djunk", [128, 128])
    ss = k.sb("dss", [128, 32])
    rs = k.sb("drs2", [128, 32])
    rstd = k.sb("drstd2", [128, 32])
    on = [k.sb("don%d" % i, [128, 128], BF16) for i in range(2)]
    k.dma(gT, projT[(18 + h) * 128:(19 + h) * 128, :])
    k.act(gT, gT, AF.Silu)
    for n in range(NT_):
        k.act(junk, oacc[:, n, :], AF.Square, accum_out=ss[:, n:n + 1])
    k.act(rs, ss, AF.Sqrt, scale=1.0 / 128, bias=epsc)
    k.recip(rstd, rs)
    for n in range(NT_):
        o_ = on[n % 2]
        k.stt(o_, oacc[:, n, :], rstd[:, n:n + 1], onrow, ALU.mult, ALU.mult)
        pt = Bk[6 + n % 2].bitcast(BF16)[:, :128]
        k.tr(pt, o_, ident)
        k.tt("dve", mixst[:, n * 128:(n + 1) * 128], pt, gT[:, n * 128:(n + 1) * 128], ALU.mult)
    k.dma(out[orow0 + h * 128:orow0 + (h + 1) * 128, :], mixst)
    k.pop()


def stage_dn(k, Bk, projT, ba, dnrow, convw_d, onrow_d, ident, ones, epsc, out, heads=(0, 1, 2)):
    k.push()
    tc = tri_consts(k, ident)
    G = dn_gates(k, Bk, ba, dnrow, tc)
    convw = k.sb("convw", [128, 9, 5])
    k.dma(convw, convw_d)
    onrow = k.sb("onrow", [128, 128])
    k.dma(onrow, onrow_d)
    colq = k.sb("colq", [128, 1])
    colk = k.sb("colk", [128, 1])
    k.memset("dve", colq, 128.0 ** -0.5)
    k.memset("dve", colk, 1.0)
    nh = len(heads)
    qn = [k.sb("dqn%d" % h, [128, T], BF16) for h in range(nh)]
    kn = [k.sb("dkn%d" % h, [128, T], BF16) for h in range(nh)]
    vf = [k.sb("dvf%d" % h, [128, T], BF16) for h in range(nh)]
    oacc = [k.sb("doacc%d" % h, [128, 32, 128]) for h in range(nh)]
    for o_ in oacc:
        k.memset("pool", o_, 0.0)
    for i, h in enumerate(heads):
        dn_prep_head(k, Bk, projT, h, convw, ident, ones, epsc, qn[i], kn[i], vf[i], colq, colk)
    k.barrier()
    k.push()
    chains = []
    for i, h in enumerate(heads):
        for d in range(2):
            chains.append(dn_chain(k, Bk[len(chains)], h, d, G, tc, ident, qn[i], kn[i], vf[i], oacc[i], cidx=len(chains)))
    active = list(chains)
    while active:
        nxt = []
        for c in active:
            try:
                next(c)
                nxt.append(c)
            except StopIteration:
                pass
        active = nxt
    k.pop()
    for i, h in enumerate(heads):
        dn_out_head(k, Bk, projT, h, oacc[i], onrow, ident, epsc, out, 384)
    k.pop()


import math
T = 4096
D = 2048
EPS = 1e-6
NFM3 = 22
LAM_INIT = 0.8 - 0.6 * math.exp(-0.3 * 1)


def build_p2():
    nc = bass.Bass("TRN2", target_bir_lowering=False)
    k = K(nc)
    TT = 2048
    mixT = k.dram("mixT", [D, TT], BF16, "ExternalInput")
    xT = k.dram("xT", [D, TT], F32, "ExternalInput")
    wout = k.dram("wout", [D, D], F32, "ExternalInput")
    yT = k.dram("yT", [D, TT], F32, "ExternalOutput")
    Bk = [k.ps("bk%d" % i, [128, 512], F32) for i in range(8)]
    Wo = k.sb("Wo", [128, 16, D], BF16)
    for kc in range(16):
        k.dma(Wo[:, kc, :], wout[kc * 128:(kc + 1) * 128, :], eng="pool")
    mb = [k.sb("mb%d" % i, [128, 16, 512], BF16) for i in range(2)]
    xb = [k.sb("xb%d" % i, [128, 16, 512], F32) for i in range(2)]
    st = [k.sb("st%d" % i, [128, 512], F32) for i in range(4)]
    mv = mixT.re("(kc p) t -> p kc t", p=128)
    xv = xT.re("(kc p) t -> p kc t", p=128)
    for tb in range(TT // 512):
        t0 = tb * 512
        m_, x_ = mb[tb % 2], xb[tb % 2]
        for q4 in range(4):
            k.dma(m_[:, q4 * 4:(q4 + 1) * 4, :], mv[:, q4 * 4:(q4 + 1) * 4, t0:t0 + 512])
            k.dma(x_[:, q4 * 4:(q4 + 1) * 4, :], xv[:, q4 * 4:(q4 + 1) * 4, t0:t0 + 512])
        for dc in range(16):
            ps = Bk[dc % 4]
            for kc in range(16):
                k.mm(ps, Wo[:, kc, dc * 128:(dc + 1) * 128], m_[:, kc, :], start=(kc == 0), stop=(kc == 15), sig=(kc == 15))
            s_ = st[dc % 4]
            k.tt("dve", s_, ps, x_[:, dc, :], ALU.add)
            k.dma(yT[dc * 128:(dc + 1) * 128, t0:t0 + 512], s_)
    k.barrier()
    return nc


def mix_perm():
    idx = []
    for g in range(2):
        for h in range(3):
            idx += list(range((3 * g + h) * 128, (3 * g + h + 1) * 128))
        for h in range(3):
            idx += list(range(768 + (3 * g + h) * 128, 768 + (3 * g + h + 1) * 128))
        for h in range(2):
            idx += list(range(1536 + (2 * g + h) * 128, 1536 + (2 * g + h + 1) * 128))
    return np.asarray(idx)


def mix_perm_odd():
    idx = []
    for g in range(2):
        for h in range(3):
            idx += list(range((3 * g + h) * 256, (3 * g + h + 1) * 256))
        for h in range(2):
            idx += list(range(1536 + (2 * g + h) * 128, 1536 + (2 * g + h + 1) * 128))
    return np.asarray(idx)


def rope_tables(k):
    C = k.sb("ropeC", [128, T])
    S = k.sb("ropeS", [128, T])
    k.push()
    pos = k.sb("pos", [128, T])
    k.emit("pool", lambda e: e.iota(pos.ap, [[1, T]], channel_multiplier=0, allow_small_or_imprecise_dtypes=True),
           w=[pos], r=[])
    pidx = k.sb("pidx", [128, 1])
    for half in range(2):
        v = pidx[half * 64:(half + 1) * 64, :]
        k.emit("pool", lambda e, v=v: e.iota(v.ap, [[0, 1]], channel_multiplier=1, allow_small_or_imprecise_dtypes=True),
               w=[v], r=[])
    inv = k.sb("inv", [128, 1])
    k.act(inv, pidx, AF.Exp, scale=-math.log(10000.0) / 64.0)
    ang = k.sb("ang", [128, T])
    k.ts("dve", ang, pos, inv, ALU.mult)
    y = k.sb("ropey", [128, T])
    ni = k.sb("ropeni", [128, T], mybir.dt.int32)
    npi = k.sb("npi", [128, 1])
    k.memset("dve", npi, -math.pi)
    c1 = 6.28125
    c2 = 2.0 * math.pi - c1
    for tab, shift in ((S, 0.0), (C, math.pi / 2)):
        k.ts("dve", y, ang, shift, ALU.add, 1.0 / (2 * math.pi), ALU.mult)
        k.cp("dve", ni, y)
        k.cp("dve", y, ni)
        k.stt(tab, y, -c1, ang, ALU.mult, ALU.add)
        k.stt(tab, y, -c2, tab, ALU.mult, ALU.add)
        if shift:
            k.ts("dve", tab, tab, shift, ALU.add)
        k.ts("dve", y, tab, math.pi, ALU.is_gt)
        k.stt(tab, y, -2 * math.pi, tab, ALU.mult, ALU.add)
        k.ts("dve", y, tab, -math.pi, ALU.is_lt)
        k.stt(tab, y, 2 * math.pi, tab, ALU.mult, ALU.add)
        k.act(tab, tab, AF.Sin)
    k.ts("dve", S[0:64, :], S[0:64, :], -1.0, ALU.mult)
    k.pop()
    return C, S


def stage_diff(k, Bk, projT, vdf, hvs, lamrep_d, subrow_d, ones, epsc, ident, identf, out, heads=(0, 1, 2)):
    k.push()
    C, S = rope_tables(k)
    swp = k.sb("swp", [128, 128], BF16)
    k.cp("pool", swp[:, 0:64], ident[:, 64:128])
    k.cp("pool", swp[:, 64:128], ident[:, 0:64])
    lamrep = k.sb("lamrep", [128, 4, 128])
    k.dma(lamrep, lamrep_d)
    lt = k.sb("lt", [128, 2, 128])
    le = k.sb("le", [128, 2])
    k.tt("dve", lt[:, 0, :], lamrep[:, 0, :], lamrep[:, 1, :], ALU.mult)
    k.tt("dve", lt[:, 1, :], lamrep[:, 2, :], lamrep[:, 3, :], ALU.mult)
    k.emit("dve", lambda e: e.tensor_reduce(out=le.ap, in_=lt.ap, axis=AX.X, op=ALU.add), w=[le], r=[lt])
    k.act(le, le, AF.Exp)
    nlam = k.sb("nlam", [128, 1])
    k.tt("dve", nlam, le[:, 1:2], le[:, 0:1], ALU.subtract)
    k.ts("dve", nlam, nlam, -LAM_INIT, ALU.add)
    subrow = k.sb("subrow", [128, 256])
    k.dma(subrow, subrow_d)
    k.ts("dve", subrow, subrow, 1.0 - LAM_INIT, ALU.mult)
    raw = k.sb("fraw", [128, T], BF16)
    qk = [k.sb("fqk%d" % i, [128, T], BF16) for i in range(4)]
    va = k.sb("fva", [128, 32, 257], BF16)
    gT = [k.sb("fg%d" % i, [128, T], BF16) for i in range(2)]
    mixst = [k.sb("fmix%d" % i, [128, T], BF16) for i in range(2)]
    sq2 = [k.sb("fsq%d" % i, [128, 512], BF16) for i in range(2)]
    rs2 = [k.sb("frs%d" % i, [128, 512]) for i in range(2)]
    rstd2 = [k.sb("frstd%d" % i, [128, 512]) for i in range(2)]
    xn2 = [k.sb("fxn%d" % i, [128, 512], BF16) for i in range(2)]
    t12 = [k.sb("ft1%d" % i, [128, 512]) for i in range(2)]
    t22 = [k.sb("ft2%d" % i, [128, 512]) for i in range(2)]
    pT = [k.sb("fpT%d" % i, [128, 512], BF16) for i in range(4)]
    accs = [[k.sb("facc%d%d" % (m, s_), [128, 257]) for s_ in range(2)] for m in range(2)]
    rc = k.sb("frc", [128, 4])
    o1 = k.sb("fo1", [128, 256])
    o = k.sb("fo", [128, 256])
    junk = k.sb("fjunk", [128, 256])
    ss = k.sb("fss", [128, 2])
    on = [k.sb("fon%d" % i, [128, 256], BF16) for i in range(2)]
    k.memset("pool", va, 1.0)
    mhalf = k.sb("mhalf", [128, 1])
    k.memset("pool", mhalf, -0.5)
    for h in heads:
        for i in range(4):
            chunk = (0 if i < 2 else 6) + 2 * h + (i % 2)
            gain = hvs[:, 4:5] if i < 2 else hvs[:, 1:2]
            k.dma(raw, projT[chunk * 128:(chunk + 1) * 128, :])
            for blk in range(T // 512):
                t0 = blk * 512
                b2 = blk % 2
                xn, t1, t2 = xn2[b2], t12[b2], t22[b2]
                rms_fm(k, Bk[6 - 2 * b2], ones, epsc, raw[:, t0:t0 + 512], xn, gain, 512, sq2[b2], rs2[b2], rstd2[b2], 128)
                k.mm(Bk[7 - 2 * b2], swp, xn)
                k.tt("pool", t1, xn, C[:, t0:t0 + 512], ALU.mult)
                k.tt("dve", t2, Bk[7 - 2 * b2], S[:, t0:t0 + 512], ALU.mult)
                k.tt("dve", qk[i][:, t0:t0 + 512], t1, t2, ALU.add)
        k.dma(va[:, :, 0:256], vdf.re("(tl p) c -> p tl c", p=128)[:, :, h * 256:(h + 1) * 256])
        for e in range(2):
            k.dma(gT[e], projT[(12 + 2 * h + e) * 128:(13 + 2 * h + e) * 128, :])
            k.act(gT[e], gT[e], AF.Silu)
        def epilogue(q0):
            for s_ in range(2):
                a0, a1 = accs[0][s_], accs[1][s_]
                k.recip(rc[:, 0:1], a0[:, 256:257])
                yield
                k.ts("dve", o1, a0[:, 0:256], rc[:, 0:1], ALU.mult)
                yield
                k.recip(rc[:, 1:2], a1[:, 256:257])
                k.tt("dve", rc[:, 2:3], rc[:, 1:2], nlam, ALU.mult)
                yield
                k.stt(o, a1[:, 0:256], rc[:, 2:3], o1, ALU.mult, ALU.add)
                yield
                k.tt("pool", junk, o, o, ALU.mult)
                yield
                k.emit("dve", lambda e_: e_.tensor_reduce(out=ss[:, 0:1].ap, in_=junk.ap, axis=AX.X, op=ALU.add),
                       w=[ss], r=[junk])
                k.ts("dve", ss[:, 1:2], ss[:, 0:1], 1.0 / 256, ALU.mult, EPS, ALU.add)
                yield
                yield
                k.tt("pool", rc[:, 3:4], ss[:, 1:2], mhalf, ALU.pow)
                yield
                o_ = on[s_]
                k.stt(o_, o, rc[:, 3:4], subrow, ALU.mult, ALU.mult)
                yield
                yield
                tok = q0 + s_ * 128
                for e in range(2):
                    pt = Bk[7].bitcast(BF16)[:, e * 128:(e + 1) * 128]
                    k.tr(pt, o_[:, e * 128:(e + 1) * 128], ident)
                    yield
                    k.tt("dve", mixst[e][:, tok:tok + 128], pt, gT[e][:, tok:tok + 128], ALU.mult)
                    yield

        pend = iter(())
        for qb in range(T // 256):
            q0 = qb * 256
            acc = [[Bk[2 + 2 * m + s_][:, :257] for s_ in range(2)] for m in range(2)]
            stb = [Bk[0], Bk[1], Bk[6]]

            def qk_(kc):
                ps = stb[kc % 3]
                for m in range(2):
                    k.mm(ps[:, m * 256:(m + 1) * 256], qk[2 + m][:, kc * 128:(kc + 1) * 128], qk[m][:, q0:q0 + 256])

            def pv_(kc):
                p_ = pT[kc % 4]
                for m in range(2):
                    for s_ in range(2):
                        k.mm(acc[m][s_], p_[:, m * 256 + s_ * 128:m * 256 + (s_ + 1) * 128], va[:, kc, :],
                             start=(kc == 0), stop=(kc == 31), sig=(kc == 31 or True))
            qk_(0)
            qk_(1)
            for kc in range(32):
                k.act(pT[kc % 4], stb[kc % 3], AF.Exp)
                if kc + 2 < 32:
                    qk_(kc + 2)
                pv_(kc)
                next(pend, None)
            for m in range(2):
                for s_ in range(2):
                    k.cp("dve", accs[m][s_], acc[m][s_])
            pend = epilogue(q0)
        for _ in pend:
            pass
        for e in range(2):
            k.dma(out[(2 * h + e) * 128:(2 * h + e + 1) * 128, :], mixst[e])
    k.pop()


def build_p3(stages=("A", "M", "DF", "ME")):
    nc = bass.Bass("TRN2", target_bir_lowering=False)
    k = K(nc)
    xT = k.dram("xT", [D, T], F32, "ExternalInput")
    memT = k.dram("memT", [D, 256], F32, "ExternalInput")
    win = k.dram("win", [D, NFM3 * 128 + 768], F32, "ExternalInput")
    wkv = k.dram("wkv", [D, 512], F32, "ExternalInput")
    lng = k.dram("lng", [128, 16], F32, "ExternalInput")
    memg = k.dram("memg", [128, 16], F32, "ExternalInput")
    hv = k.dram("hv", [128, 8], F32, "ExternalInput")
    lamrep = k.dram("lamrep", [128, 4, 128], F32, "ExternalInput")
    subrow = k.dram("subrow", [128, 256], F32, "ExternalInput")
    out = k.dram("mixT", [1024, T], BF16, "ExternalOutput")
    projT = k.dram("projT", [NFM3 * 128, T], BF16)
    vdf = k.dram("vdf", [T, 768], BF16)
    Bk = [k.ps("bk%d" % i, [128, 512], F32) for i in range(8)]
    ones, epsc, ident, identf = consts(k)
    hvs = k.sb("hvs", [128, 8])
    k.dma(hvs, hv)
    k.ts("dve", hvs[:, 4:5], hvs[:, 0:1], 128.0 ** -0.5, ALU.mult)
    k.ts("dve", hvs[:, 5:6], hvs[:, 2:3], 128.0 ** -0.5, ALU.mult)
    kmemT = k.sb("kmemT", [128, 2, 256], BF16)
    vmem = k.sb("vmem", [128, 2, 2, 129], BF16)
    if "A" in stages:
        stage_proj(k, Bk, xT, win, lng, NFM3 * 128, 768, projT,
                      [(vdf, 0, 384, BF16, "rows"), (vdf, 384, 384, BF16, "rows")], ones, epsc)
    if "M" in stages:
        stage_memkv(k, Bk, memT, wkv, memg, hvs[:, 3:4], ones, epsc, kmemT, vmem)
    k.barrier()
    if "DF" in stages:
        stage_diff(k, Bk, projT, vdf, hvs, lamrep, subrow, ones, epsc, ident, identf, out)
    if "ME" in stages:
        stage_memattn(k, Bk, projT, 18, 20, hvs[:, 5:6], kmemT, vmem, ones, epsc, ident, out, 768)
    k.barrier()
    return nc


def prep_p3(inp, x1T_b, b, g):
    f = np.float32
    w = inp["o_w_in"][0]
    DW = 1536
    cols = []
    for base in (0, DW):
        for h in range(3):
            cols += list(range(base + (3 * g + h) * 256, base + (3 * g + h + 1) * 256))
    for h in range(3):
        cols += list(range(3 * DW + (3 * g + h) * 256, 3 * DW + (3 * g + h + 1) * 256))
    mq0 = 4 * DW
    mg0 = mq0 + 512
    cols += list(range(mq0 + 2 * g * 128, mq0 + (2 * g + 2) * 128))
    cols += list(range(mg0 + 2 * g * 128, mg0 + (2 * g + 2) * 128))
    for h in range(3):
        cols += list(range(2 * DW + (3 * g + h) * 256, 2 * DW + (3 * g + h + 1) * 256))
    cols = np.asarray(cols)
    assert len(cols) == NFM3 * 128 + 768
    wkv = inp["o_w_mem_kv"][0]
    kvc = list(range(2 * g * 128, (2 * g + 2) * 128)) + list(range(512 + 2 * g * 128, 512 + (2 * g + 2) * 128))
    hv = np.zeros((128, 8), f)
    hv[:, 0] = inp["df_q_norm"][0]
    hv[:, 1] = inp["df_k_norm"][0]
    hv[:, 2] = inp["o_mem_q_norm"][0]
    hv[:, 3] = inp["o_mem_k_norm"][0]
    return {
        "xT": x1T_b,
        "memT": np.ascontiguousarray(inp["mem"][b].T),
        "win": np.ascontiguousarray(w[:, cols]),
        "wkv": np.ascontiguousarray(wkv[:, kvc]),
        "lng": np.ascontiguousarray(inp["o_ln_g"][0].reshape(16, 128).T),
        "memg": np.ascontiguousarray(inp["mem_norm_g"].reshape(16, 128).T),
        "hv": hv,
        "lamrep": np.ascontiguousarray(np.broadcast_to(inp["df_lambda"][0][None], (128, 4, 128))).astype(f),
        "subrow": np.ascontiguousarray(np.broadcast_to(inp["df_subln"][0][None], (128, 256))).astype(f),
    }


T = 4096
D = 2048
PAIRS = [[0, 1], [2, 3], [4, 5], [6, 7]]


def stage_wout(k, Bk, mixall, xsrc, wout, ydst, ntok=T):
    k.push()
    Wo = k.sb("Wo", [128, 16, D], BF16)
    for kc in range(16):
        k.dma(Wo[:, kc, :], wout[kc * 128:(kc + 1) * 128, :], eng="pool")
    mb = [k.sb("mb%d" % i, [128, 16, 512], BF16) for i in range(2)]
    xb = [k.sb("xwb%d" % i, [128, 16, 512], F32) for i in range(2)]
    st = [k.sb("wst%d" % i, [128, 512], F32) for i in range(4)]
    mv = mixall.re("(kc p) t -> p kc t", p=128)
    xv = xsrc.re("(kc p) t -> p kc t", p=128)
    for tb in range(ntok // 512):
        t0 = tb * 512
        m_, x_ = mb[tb % 2], xb[tb % 2]
        for q4 in range(4):
            k.dma(m_[:, q4 * 4:(q4 + 1) * 4, :], mv[:, q4 * 4:(q4 + 1) * 4, t0:t0 + 512])
            k.dma(x_[:, q4 * 4:(q4 + 1) * 4, :], xv[:, q4 * 4:(q4 + 1) * 4, t0:t0 + 512])
        for dc in range(16):
            ps = Bk[dc % 4]
            for kc in range(16):
                k.mm(ps, Wo[:, kc, dc * 128:(dc + 1) * 128], m_[:, kc, :], start=(kc == 0), stop=(kc == 15), sig=(kc == 15))
            s_ = st[dc % 4]
            k.tt("dve", s_, ps, x_[:, dc, :], ALU.add)
            k.dma(ydst[dc * 128:(dc + 1) * 128, t0:t0 + 512], s_)
    k.pop()


def build_fused():
    nc = bass.Bass("TRN2", target_bir_lowering=False)
    k = K(nc)
    EI = "ExternalInput"
    xT = k.dram("xT", [D, T], F32, EI)
    memT = k.dram("memT", [D, 256], F32, EI)
    memg = k.dram("memg", [128, 16], F32, EI)
    win0 = k.dram("win0", [D, NFM1 * 128 + TM1], F32, EI)
    wkv0 = k.dram("wkv0", [D, 512], F32, EI)
    lng0 = k.dram("lng0", [128, 16], F32, EI)
    hv0 = k.dram("hv0", [128, 8], F32, EI)
    nab = k.dram("nab", [3, 8, 128, 256], F32, EI)
    nam = k.dram("nam", [128, 256], F32, EI)
    dnrow = k.dram("dnrow", [128, 2, 2, 32, 3], F32, EI)
    convw = k.dram("convw", [128, 9, 5], F32, EI)
    onrow = k.dram("onrow", [128, 128], F32, EI)
    wout0 = k.dram("wout0", [D, D], F32, EI)
    win1 = k.dram("win1", [D, NFM3 * 128 + 768], F32, EI)
    wkv1 = k.dram("wkv1", [D, 512], F32, EI)
    lng1 = k.dram("lng1", [128, 16], F32, EI)
    hv1 = k.dram("hv1", [128, 8], F32, EI)
    lamrep = k.dram("lamrep", [128, 4, 128], F32, EI)
    subrow = k.dram("subrow", [128, 256], F32, EI)
    wout1 = k.dram("wout1", [D, D], F32, EI)
    yT = k.dram("yT", [D, T], F32, "ExternalOutput")
    projT = k.dram("projT", [NFM1 * 128, T], BF16)
    vna = k.dram("vna", [T, 384], BF16)
    ba = k.dram("ba", [128, 32 * 12], F32)
    vdf = k.dram("vdf", [T, 768], BF16)
    mixloc_f = k.dram("mixloc", [1024, T // 2], F32)
    mixall_f = k.dram("mixall", [2048, T // 2], F32)
    mixloc = mixloc_f.bitcast(BF16)
    mixall = mixall_f.bitcast(BF16)

    def exchange():
        for j in range(4):
            k.allgather(mixall_f[j * 512:(j + 1) * 512, :], mixloc_f[j * 256:(j + 1) * 256, :], PAIRS)
    x1T = k.dram("x1T", [D, T], F32)
    Bk = [k.ps("bk%d" % i, [128, 512], F32) for i in range(8)]
    ones, epsc, ident, identf = consts(k)

    def hvload(hv):
        hvs = k.sb("hvs", [128, 8])
        k.dma(hvs, hv)
        k.ts("dve", hvs[:, 4:5], hvs[:, 0:1], 128.0 ** -0.5, ALU.mult)
        k.ts("dve", hvs[:, 5:6], hvs[:, 2:3], 128.0 ** -0.5, ALU.mult)
        return hvs

    k.push()
    hvs = hvload(hv0)
    kmemT = k.sb("kmemT", [128, 2, 256], BF16)
    vmem = k.sb("vmem", [128, 2, 2, 129], BF16)
    stage_proj(k, Bk, xT, win0, lng0, NFM1 * 128, TM1, projT,
                  [(vna, 0, 384, BF16, "rows"), (ba, 384, 12, F32, "ptile")], ones, epsc)
    stage_memkv(k, Bk, memT, wkv0, memg, hvs[:, 3:4], ones, epsc, kmemT, vmem)
    k.barrier()
    stage_na(k, Bk, projT, vna, nab, nam, hvs[:, 4:5], hvs[:, 1:2], ones, epsc, ident, mixloc)
    stage_memattn(k, Bk, projT, 21, 23, hvs[:, 5:6], kmemT, vmem, ones, epsc, ident, mixloc, 768)
    stage_dn(k, Bk, projT, ba, dnrow, convw, onrow, ident, ones, epsc, mixloc)
    k.pop()
    exchange()
    stage_wout(k, Bk, mixall, xT, wout0, x1T)
    k.barrier()
    k.push()
    hvs = hvload(hv1)
    kmemT = k.sb("kmemT", [128, 2, 256], BF16)
    vmem = k.sb("vmem", [128, 2, 2, 129], BF16)
    stage_proj(k, Bk, x1T, win1, lng1, NFM3 * 128, 768, projT,
                  [(vdf, 0, 384, BF16, "rows"), (vdf, 384, 384, BF16, "rows")], ones, epsc)
    stage_memkv(k, Bk, memT, wkv1, memg, hvs[:, 3:4], ones, epsc, kmemT, vmem)
    k.barrier()
    stage_diff(k, Bk, projT, vdf, hvs, lamrep, subrow, ones, epsc, ident, identf, mixloc)
    stage_memattn(k, Bk, projT, 18, 20, hvs[:, 5:6], kmemT, vmem, ones, epsc, ident, mixloc, 768)
    k.pop()
    exchange()
    stage_wout(k, Bk, mixall, x1T, wout1, yT)
    k.barrier()
    return nc


def fused_rowmap(perm_groups):
    idx = []
    for j in range(4):
        for r in range(2):
            idx += list(perm_groups[r][j * 256:(j + 1) * 256])
    return np.asarray(idx)


def prep_fused(inp, b, g):
    a = prep_p1(inp, b, g)
    c = prep_p3(inp, None, b, g)
    m = {
        "xT": a["xT"], "memT": a["memT"], "memg": a["memg"],
        "win0": a["win"], "wkv0": a["wkv"], "lng0": a["lng"], "hv0": a["hv"],
        "nab": a["nab"], "nam": a["nam"], "dnrow": a["dnrow"], "convw": a["convw"], "onrow": a["onrow"],
        "win1": c["win"], "wkv1": c["wkv"], "lng1": c["lng"], "hv1": c["hv"],
        "lamrep": c["lamrep"], "subrow": c["subrow"],
    }
    pe = mix_perm()
    po = mix_perm_odd()
    m["wout0"] = np.ascontiguousarray(inp["e_w_out"][0][fused_rowmap([pe[:1024], pe[1024:]]), :])
    m["wout1"] = np.ascontiguousarray(inp["o_w_out"][0][fused_rowmap([po[:1024], po[1024:]]), :])
    return m


def kernel(**inputs):
    inp = {k_: np.asarray(v_) for k_, v_ in inputs.items()}
    nc = build_fused()
    maps = [prep_fused(inp, c // 2, c % 2) for c in range(8)]
    res = run_bass_kernel_spmd(nc, maps, core_ids=list(range(8)))
    out = np.stack([np.ascontiguousarray(np.asarray(res.results[2 * b]["yT"]).T) for b in range(4)], axis=0)
    return out.astype(np.float32)
```

```python
import numpy as np
import concourse.bass as bass
import concourse.mybir as mybir
from concourse.bass_utils import run_bass_kernel_spmd

F32 = mybir.dt.float32
BF16 = mybir.dt.bfloat16
ALU = mybir.AluOpType
AF = mybir.ActivationFunctionType
AX = mybir.AxisListType


class Trk:
    __slots__ = ("name", "w", "r", "dsem", "dcnt", "excl")

    def __init__(self, name, excl=False):
        self.name = name
        self.excl = excl
        self.w = None
        self.r = {}
        self.dsem = None
        self.dcnt = 0


class V:
    __slots__ = ("trk", "ap")

    def __init__(self, trk, ap):
        self.trk = trk
        self.ap = ap

    def __getitem__(self, idx):
        return V(self.trk, self.ap[idx])

    def sub(self, name, idx=None):
        return V(Trk(name), self.ap if idx is None else self.ap[idx])

    def bitcast(self, dt):
        return V(self.trk, self.ap.bitcast(dt))

    def re(self, pat, **kw):
        return V(self.trk, self.ap.rearrange(pat, **kw))


class K:
    def __init__(self, nc):
        self.nc = nc
        self.E = {"pe": nc.tensor, "act": nc.scalar, "dve": nc.vector, "pool": nc.gpsimd, "sp": nc.sync}
        self.sems = {}
        self.cnt = {}
        for e in ("pe", "act", "dve", "pool"):
            self.sems[e] = nc.alloc_semaphore("s_" + e)
            self.cnt[e] = 0
        self.waited = {}
        self.dma_trks = []
        self.nsem = 4
        self.uid = 0
        self.guards = []
        self.marks = []
        self.free_dsems = []
        self.dtot = {}
        self.cc_sem = None
        self.cc_cnt = 0

    def sb(self, name, shape, dt=F32):
        self.uid += 1
        g = self.nc.sbuf_tensor("%s_%d" % (name, self.uid), list(shape), dt)
        t = g.__enter__()
        v = V(Trk(name), t[:])
        self.guards.append((g, v))
        return v

    def push(self):
        self.marks.append(len(self.guards))

    def pop(self):
        self.barrier()
        m = self.marks.pop()
        while len(self.guards) > m:
            g, v = self.guards.pop()
            if v.trk.dsem is not None:
                self.free_dsems.append(v.trk.dsem)
                v.trk.dsem = None
            g.__exit__(None, None, None)

    def ps(self, name, shape, dt=F32):
        self.uid += 1
        t = self.nc.alloc_psum_tensor("%s_%d" % (name, self.uid), list(shape), dt)
        return V(Trk(name, excl=True), t[:])

    def dram(self, name, shape, dt=F32, kind="Internal"):
        t = self.nc.dram_tensor(name, list(shape), dt, kind=kind)
        return V(None, t.ap())

    def _dsem(self, trk):
        if trk.dsem is None:
            if self.free_dsems:
                trk.dsem = self.free_dsems.pop()
            else:
                self.nsem += 1
                key = "d%d" % self.nsem
                self.sems[key] = self.nc.alloc_semaphore("d_%d" % self.nsem)
                self.dtot[key] = 0
                trk.dsem = key
        return trk.dsem

    def _deps(self, eng, w, r):
        need = {}

        def add(dep):
            if dep is None:
                return
            k, v = dep
            if k == eng and eng == "pe":
                return
            if need.get(k, 0) < v:
                need[k] = v
        for x in r:
            if x is not None and x.trk is not None:
                add(x.trk.w)
                if x.trk.excl:
                    for k, v in x.trk.r.items():
                        if k != eng:
                            add((k, v))
        for x in w:
            if x is not None and x.trk is not None:
                add(x.trk.w)
                for k, v in x.trk.r.items():
                    add((k, v))
        return need

    def _wait(self, eng, need):
        E = self.E[eng]
        for k, v in need.items():
            if self.waited.get((eng, k), 0) >= v:
                continue
            self.waited[(eng, k)] = v
            E.wait_ge(self.sems[k], v)

    def emit(self, eng, fn, w=(), r=(), sig=True):
        need = self._deps(eng, w, r)
        self._wait(eng, need)
        ins = fn(self.E[eng])
        n = self.cnt[eng] + 1
        if sig:
            self.cnt[eng] = n
            ins.then_inc(self.sems[eng], 1)
        for x in r:
            if x is not None and x.trk is not None:
                x.trk.r[eng] = n
        for x in w:
            if x is not None and x.trk is not None:
                x.trk.w = (eng, n)
                x.trk.r = {}
        return ins

    def dma(self, out, in_, eng="sp", **kw):
        own = out if out.trk is not None else in_
        assert own.trk is not None
        sk = self._dsem(own.trk)
        need = {}
        if in_.trk is not None and in_.trk.w is not None:
            need[in_.trk.w[0]] = in_.trk.w[1]
        if out.trk is not None:
            if out.trk.w is not None and out.trk.w[0] != sk:
                need[out.trk.w[0]] = max(need.get(out.trk.w[0], 0), out.trk.w[1])
            for k2, v2 in out.trk.r.items():
                need[k2] = max(need.get(k2, 0), v2)
        need.pop(None, None)
        self._wait(eng, need)
        ins = self.E[eng].dma_start(out=out.ap, in_=in_.ap, **kw)
        self.dtot[sk] += 1
        val = 16 * self.dtot[sk]
        ins.then_inc(self.sems[sk], 16)
        if out.trk is not None:
            out.trk.w = (sk, val)
            out.trk.r = {}
        if in_.trk is not None:
            in_.trk.r[sk] = val
        return ins

    def barrier(self):
        tot = {}
        for e in ("pe", "act", "dve", "pool"):
            if self.cnt[e] > 0:
                tot[e] = self.cnt[e]
        for key, n in self.dtot.items():
            if n > 0:
                tot[key] = 16 * n
        if self.cc_cnt:
            tot["cc"] = self.cc_cnt
        for e in ("pe", "act", "dve", "pool", "sp"):
            self._wait(e, dict(tot))

    def allgather(self, out, in_, groups, inc=1):
        self.barrier()
        if self.cc_sem is None:
            self.cc_sem = self.nc.alloc_semaphore("cc")
            self.sems["cc"] = self.cc_sem
        ins = self.E["pool"].collective_compute("AllGather", ALU.bypass, replica_groups=groups,
                                                ins=[in_.ap.opt()], outs=[out.ap.opt()])
        ins.then_inc(self.cc_sem, inc)
        self.cc_cnt += inc
        self.barrier()

    def mm(self, out, lhsT, rhs, start=True, stop=True, sig=None, **kw):
        if sig is None:
            sig = True
        return self.emit("pe", lambda e: e.matmul(out.ap, lhsT.ap, rhs.ap, start=start, stop=stop, **kw),
                         w=[out], r=[lhsT, rhs], sig=sig)

    def tr(self, out, in_, ident, sig=True):
        return self.emit("pe", lambda e: e.transpose(out.ap, in_.ap, ident.ap), w=[out], r=[in_, ident], sig=sig)

    def act(self, out, in_, func, bias=None, scale=None, accum_out=None, eng="act"):
        kw = {}
        r = [in_]
        if bias is not None:
            if isinstance(bias, V):
                kw["bias"] = bias.ap
                r.append(bias)
            else:
                kw["bias"] = bias
        if scale is not None:
            if isinstance(scale, V):
                kw["scale"] = scale.ap
                r.append(scale)
            else:
                kw["scale"] = scale
        w = [out]
        if accum_out is not None:
            kw["accum_out"] = accum_out.ap
            w.append(accum_out)
        return self.emit("act", lambda e: e.activation(out.ap, in_.ap, func, **kw), w=w, r=r)

    def tt(self, eng, out, a, b, op):
        return self.emit(eng, lambda e: e.tensor_tensor(out=out.ap, in0=a.ap, in1=b.ap, op=op), w=[out], r=[a, b])

    def ts(self, eng, out, a, s1, op0, s2=None, op1=None, accum_out=None):
        r = [a]
        s1a = s1.ap if isinstance(s1, V) else s1
        s2a = s2.ap if isinstance(s2, V) else s2
        if isinstance(s1, V):
            r.append(s1)
        if isinstance(s2, V):
            r.append(s2)
        kw = {}
        w = [out]
        if op1 is not None:
            kw["op1"] = op1
        if accum_out is not None:
            kw["accum_out"] = accum_out.ap
            w.append(accum_out)
        return self.emit(eng, lambda e: e.tensor_scalar(out=out.ap, in0=a.ap, scalar1=s1a, scalar2=s2a, op0=op0, **kw),
                         w=w, r=r)

    def stt(self, out, a, s, b, op0, op1, eng="dve"):
        r = [a, b]
        sa = s.ap if isinstance(s, V) else s
        if isinstance(s, V):
            r.append(s)
        return self.emit(eng, lambda e: e.scalar_tensor_tensor(out=out.ap, in0=a.ap, scalar=sa, in1=b.ap, op0=op0, op1=op1),
                         w=[out], r=r)

    def cp(self, eng, out, in_):
        if eng == "act":
            return self.emit("act", lambda e: e.activation(out.ap, in_.ap, AF.Copy), w=[out], r=[in_])
        return self.emit(eng, lambda e: e.tensor_copy(out=out.ap, in_=in_.ap), w=[out], r=[in_])

    def memset(self, eng, out, val):
        return self.emit(eng, lambda e: e.memset(out.ap, val), w=[out], r=[])

    def recip(self, out, in_, eng="dve"):
        return self.emit(eng, lambda e: e.reciprocal(out=out.ap, in_=in_.ap), w=[out], r=[in_])


T = 4096
D = 2048
EPS = 1e-6
NTB = 8
DBG = 99
NFM1 = 25
TM1 = 396


def rms_fm(k, B, ones, epsc, src, dst, gain, n, sq, rs, rstd, nfeat):
    k.act(sq[:, :n], src, AF.Square)
    k.mm(B[:, :n], ones, sq[:, :n])
    k.act(rs[:, :n], B[:, :n], AF.Sqrt, scale=1.0 / nfeat, bias=epsc)
    k.recip(rstd[:, :n], rs[:, :n])
    k.stt(dst, src, gain, rstd[:, :n], ALU.mult, ALU.mult)


def stage_proj(k, Bk, xT, win, lng, ncols_fm, ncols_tm, projT, tm_outs, ones, epsc):
    nfm = ncols_fm // 128
    ncol = ncols_fm + ncols_tm
    k.push()
    W = k.sb("W", [128, 16, ncol], BF16)
    for kc in range(16):
        k.dma(W[:, kc, :], win[kc * 128:(kc + 1) * 128, :], eng="pool")
    lng_sb = k.sb("lng", [128, 16])
    k.dma(lng_sb, lng)
    xb = k.sb("xb", [128, 16, 512], F32)
    hT = [k.sb("hT%d" % i, [128, 16, 512], BF16) for i in range(2)]
    sq = [k.sb("sq%d" % i, [128, 512], BF16) for i in range(2)]
    rs = k.sb("rs", [128, 512])
    rstd = k.sb("rstd", [128, 512])
    stg = [k.sb("stg%d" % i, [128, 512], BF16) for i in range(4)]
    stt_ = [[k.sb("stt%d_%d" % (j, i), [128, w], dt) for i in range(2)] for j, (_, _, w, dt, _m) in enumerate(tm_outs)]
    xv = xT.re("(kc p) t -> p kc t", p=128)
    nblk = min(NTB, T // 512)

    def load(tb):
        t0 = tb * 512
        for q4 in range(4):
            k.dma(xb[:, q4 * 4:(q4 + 1) * 4, :], xv[:, q4 * 4:(q4 + 1) * 4, t0:t0 + 512])

    def prologue(tb):
        for kc in range(16):
            k.act(sq[kc % 2], xb[:, kc, :], AF.Square)
            k.mm(Bk[0], ones, sq[kc % 2], start=(kc == 0), stop=(kc == 15))
        k.act(rs, Bk[0], AF.Sqrt, scale=1.0 / D, bias=epsc)
        k.recip(rstd, rs)
        h = hT[tb % 2]
        for kc in range(16):
            k.stt(h[:, kc, :], xb[:, kc, :], lng_sb[:, kc:kc + 1], rstd, ALU.mult, ALU.mult)

    def fm(tb, c):
        t0 = tb * 512
        h = hT[tb % 2]
        ps = Bk[1 + c % 4]
        for kc in range(16):
            k.mm(ps, W[:, kc, c * 128:(c + 1) * 128], h[:, kc, :], start=(kc == 0), stop=(kc == 15), sig=(kc == 15))
        st = stg[c % 4]
        k.cp("act" if c % 2 else "dve", st, ps)
        k.dma(projT[c * 128:(c + 1) * 128, t0:t0 + 512], st)

    def tm(tb):
        t0 = tb * 512
        h = hT[tb % 2]
        for s4 in range(4):
            tok = t0 + s4 * 128
            for j, (dv, c0, w, dt, mode) in enumerate(tm_outs):
                ps = Bk[5 + (s4 * len(tm_outs) + j) % 3]
                for kc in range(16):
                    k.mm(ps[:, :w], h[:, kc, s4 * 128:(s4 + 1) * 128], W[:, kc, ncols_fm + c0:ncols_fm + c0 + w],
                         start=(kc == 0), stop=(kc == 15), sig=(kc == 15))
                st = stt_[j][s4 % 2]
                k.cp("dve" if j % 2 == 0 else "act", st, ps[:, :w])
                if mode == "rows":
                    k.dma(dv[tok:tok + 128, c0:c0 + w], st)
                else:
                    tl = tok // 128
                    k.dma(dv[:, tl * w:(tl + 1) * w], st)

    load(0)
    prologue(0)
    for tb in range(nblk):
        if tb + 1 < nblk:
            load(tb + 1)
        half = nfm // 2
        for c in range(half):
            fm(tb, c)
        if tb + 1 < nblk:
            prologue(tb + 1)
        for c in range(half, nfm):
            fm(tb, c)
        tm(tb)
    k.pop()


def stage_memkv(k, Bk, memT, wkv, memg, kgain, ones, epsc, kmemT, vmem):
    k.push()
    wk = k.sb("wkv", [128, 16, 512], BF16)
    k.dma(wk, wkv.re("(kc p) c -> p kc c", p=128), eng="pool")
    mg = k.sb("memg", [128, 16])
    k.dma(mg, memg)
    mT = k.sb("mT", [128, 16, 256])
    k.dma(mT, memT.re("(kc p) m -> p kc m", p=128))
    mn = k.sb("mn", [128, 16, 256], BF16)
    sq = [k.sb("sqm%d" % i, [128, 256], BF16) for i in range(2)]
    rs = k.sb("rsm", [128, 256])
    rstd = k.sb("rstdm", [128, 256])
    kraw = k.sb("kraw", [128, 256])
    for kc in range(16):
        k.act(sq[kc % 2], mT[:, kc, :], AF.Square)
        k.mm(Bk[0][:, :256], ones, sq[kc % 2], start=(kc == 0), stop=(kc == 15))
    k.act(rs, Bk[0][:, :256], AF.Sqrt, scale=1.0 / D, bias=epsc)
    k.recip(rstd, rs)
    for kc in range(16):
        k.stt(mn[:, kc, :], mT[:, kc, :], mg[:, kc:kc + 1], rstd, ALU.mult, ALU.mult)
    for h in range(2):
        ps = Bk[1 + h][:, :256]
        for kc in range(16):
            k.mm(ps, wk[:, kc, h * 128:(h + 1) * 128], mn[:, kc, :], start=(kc == 0), stop=(kc == 15))
        k.cp("dve", kraw, ps)
        rms_fm(k, Bk[3], ones, epsc, kraw, kmemT[:, h, :], kgain, 256, sq[0], rs, rstd, 128)
    k.memset("pool", vmem, 1.0)
    for mc in range(2):
        ps = Bk[4 + mc][:, :256]
        for kc in range(16):
            k.mm(ps, mn[:, kc, mc * 128:(mc + 1) * 128], wk[:, kc, 256:512], start=(kc == 0), stop=(kc == 15))
        for h in range(2):
            k.cp("dve", vmem[:, mc, h, 0:128], ps[:, h * 128:(h + 1) * 128])
    k.pop()


def stage_memattn(k, Bk, projT, qchunk0, gchunk0, qgain, kmemT, vmem, ones, epsc, ident, out, orow0):
    k.push()
    qT = k.sb("mqT", [128, T], BF16)
    gT = k.sb("mgT", [128, T], BF16)
    qnF = k.sb("mqnF", [128, T], BF16)
    sq2 = [k.sb("msq%d" % i, [128, 512], BF16) for i in range(2)]
    rs2 = [k.sb("mrs%d" % i, [128, 512]) for i in range(2)]
    rstd2 = [k.sb("mrstd%d" % i, [128, 512]) for i in range(2)]
    pT = [k.sb("mpT%d" % i, [128, 2, 512], BF16) for i in range(2)]
    rc = [k.sb("mrc%d" % i, [128, 1]) for i in range(2)]
    on = [k.sb("mon%d" % i, [128, 128], BF16) for i in range(2)]
    mixst = k.sb("mmix", [128, T], BF16)
    for h in range(2):
        k.dma(qT, projT[(qchunk0 + h) * 128:(qchunk0 + h + 1) * 128, :])
        k.dma(gT, projT[(gchunk0 + h) * 128:(gchunk0 + h + 1) * 128, :])
        k.act(gT, gT, AF.Silu)
        for blk in range(T // 512):
            t0 = blk * 512
            b2 = blk % 2
            rms_fm(k, Bk[0] if b2 == 0 else Bk[7], ones, epsc, qT[:, t0:t0 + 512], qnF[:, t0:t0 + 512], qgain, 512, sq2[b2], rs2[b2], rstd2[b2], 128)
        for blk in range(T // 512):
            t0 = blk * 512
            qn = qnF[:, t0:t0 + 512]
            p = pT[blk % 2]
            for mc in range(2):
                k.mm(Bk[1 + mc], kmemT[:, h, mc * 128:(mc + 1) * 128], qn)
                k.act(p[:, mc, :], Bk[1 + mc], AF.Exp)
            for s4 in range(4):
                po = Bk[3 + s4 % 2][:, :129]
                for mc in range(2):
                    k.mm(po, p[:, mc, s4 * 128:(s4 + 1) * 128], vmem[:, mc, h, :], start=(mc == 0), stop=(mc == 1))
                r_ = rc[s4 % 2]
                o_ = on[s4 % 2]
                k.recip(r_, po[:, 128:129])
                k.ts("dve", o_, po[:, 0:128], r_, ALU.mult)
                pt = Bk[5 + s4 % 2].bitcast(BF16)[:, :128]
                k.tr(pt, o_, ident)
                tok = t0 + s4 * 128
                k.tt("dve", mixst[:, tok:tok + 128], pt, gT[:, tok:tok + 128], ALU.mult)
        k.dma(out[orow0 + h * 128:orow0 + (h + 1) * 128, :], mixst)
    k.pop()


def stage_na(k, Bk, projT, vna, nab, nam, qgain, kgain, ones, epsc, ident, out):
    k.push()
    qT = k.sb("nqT", [128, T], BF16)
    kT = k.sb("nkT", [128, T], BF16)
    gT = k.sb("ngT", [128, T], BF16)
    qn = k.sb("nqn", [128, T], BF16)
    kn = k.sb("nkn", [128, T], BF16)
    vt = k.sb("nvt", [128, 32, 129], BF16)
    vs = k.sb("nvs", [128, 31, 129], BF16)
    bt = k.sb("nbt", [128, 8, 256])
    nm = k.sb("nnm", [128, 256])
    sq4 = [k.sb("nsq%d" % i, [128, 512], BF16) for i in range(4)]
    rs4 = [k.sb("nrs%d" % i, [128, 512]) for i in range(4)]
    rstd4 = [k.sb("nrstd%d" % i, [128, 512]) for i in range(4)]
    sT = [k.sb("nsT%d" % i, [128, 256]) for i in range(2)]
    pT = [k.sb("npT%d" % i, [128, 256], BF16) for i in range(2)]
    rc = [k.sb("nrc%d" % i, [64, 1]) for i in range(2)]
    on = [k.sb("non%d" % i, [64, 128], BF16) for i in range(2)]
    mixst = k.sb("nmix", [128, T], BF16)
    k.dma(nm, nam)
    k.memset("pool", vt, 1.0)
    k.memset("pool", vs, 1.0)
    for h in range(3):
        k.dma(qT, projT[(0 + h) * 128:(1 + h) * 128, :])
        k.dma(kT, projT[(3 + h) * 128:(4 + h) * 128, :])
        k.dma(gT, projT[(6 + h) * 128:(7 + h) * 128, :])
        k.dma(vt[:, :, 0:128], vna.re("(tl p) c -> p tl c", p=128)[:, :, h * 128:(h + 1) * 128])
        k.dma(vs[:, :, 0:128], vna[64:64 + 31 * 128, :].re("(tl p) c -> p tl c", p=128)[:, :, h * 128:(h + 1) * 128])
        k.dma(bt, nab[h].re("t p f -> p t f"))
        for t in range(8):
            k.tt("pool", bt[:, t, :], bt[:, t, :], nm, ALU.add)
        k.act(gT, gT, AF.Silu)
        for blk in range(T // 512):
            t0 = blk * 512
            b2 = blk % 2
            rms_fm(k, Bk[0 + 2 * b2], ones, epsc, qT[:, t0:t0 + 512], qn[:, t0:t0 + 512], qgain, 512, sq4[b2], rs4[b2], rstd4[b2], 128)
            rms_fm(k, Bk[1 + 2 * b2], ones, epsc, kT[:, t0:t0 + 512], kn[:, t0:t0 + 512], kgain, 512, sq4[2 + b2], rs4[2 + b2], rstd4[2 + b2], 128)
        def s1(r):
            r0 = min(max(r - 4, 0), 56)
            ty = r if r < 4 else (4 if r <= 60 else r - 56)
            ps = Bk[2 + r % 2][:, :256]
            for c in range(4):
                kt0 = (r0 + 2 * c) * 64
                k.mm(ps[:, c * 64:(c + 1) * 64], kn[:, kt0:kt0 + 128], qn[:, r * 64:(r + 1) * 64])
            k.tt("dve", sT[r % 2], ps, bt[:, ty, :], ALU.add)
            k.act(pT[r % 2], sT[r % 2], AF.Exp)

        def s2(r):
            r0 = min(max(r - 4, 0), 56)
            po = Bk[4 + r % 2][:64, :129]
            for c in range(4):
                row0 = r0 + 2 * c
                vv = vt[:, row0 // 2, :] if row0 % 2 == 0 else vs[:, (row0 - 1) // 2, :]
                k.mm(po, pT[r % 2][:, c * 64:(c + 1) * 64], vv, start=(c == 0), stop=(c == 3))
            k.recip(rc[r % 2], po[:, 128:129])
            k.ts("dve", on[r % 2], po[:, 0:128], rc[r % 2], ALU.mult)

        def s3(r):
            pt = Bk[6 + r % 2].bitcast(BF16)[:, :64]
            k.tr(pt, on[r % 2], ident[:64, :64])
            k.tt("dve", mixst[:, r * 64:(r + 1) * 64], pt, gT[:, r * 64:(r + 1) * 64], ALU.mult)

        for i in range(64 + 2):
            if i < 64:
                s1(i)
            if 0 <= i - 1 < 64:
                s2(i - 1)
            if 0 <= i - 2 < 64:
                s3(i - 2)
        k.dma(out[h * 128:(h + 1) * 128, :], mixst)
    k.pop()


def consts(k):
    ones = k.sb("ones", [128, 128], BF16)
    k.memset("dve", ones, 1.0)
    epsc = k.sb("epsc", [128, 1])
    k.memset("dve", epsc, EPS)
    identf = k.sb("identf", [128, 128])
    ident = k.sb("ident", [128, 128], BF16)
    k.memset("pool", identf, 0.0)
    k.emit("pool", lambda e: e.affine_select(out=identf.ap, in_=identf.ap, pattern=[[-1, 128]], base=0,
                                              channel_multiplier=1, compare_op=ALU.not_equal, fill=1.0),
           w=[identf], r=[identf])
    k.cp("pool", ident, identf)
    return ones, epsc, ident, identf


def build_p1(stages=("A", "M", "NA", "ME", "DN")):
    nc = bass.Bass("TRN2", target_bir_lowering=False)
    k = K(nc)
    xT = k.dram("xT", [D, T], F32, "ExternalInput")
    memT = k.dram("memT", [D, 256], F32, "ExternalInput")
    win = k.dram("win", [D, NFM1 * 128 + TM1], F32, "ExternalInput")
    wkv = k.dram("wkv", [D, 512], F32, "ExternalInput")
    lng = k.dram("lng", [128, 16], F32, "ExternalInput")
    memg = k.dram("memg", [128, 16], F32, "ExternalInput")
    hv = k.dram("hv", [128, 8], F32, "ExternalInput")
    nab = k.dram("nab", [3, 8, 128, 256], F32, "ExternalInput")
    nam = k.dram("nam", [128, 256], F32, "ExternalInput")
    dnrow = k.dram("dnrow", [128, 2, 2, 32, 3], F32, "ExternalInput")
    convw = k.dram("convw", [128, 9, 5], F32, "ExternalInput")
    onrow = k.dram("onrow", [128, 128], F32, "ExternalInput")
    out = k.dram("mixT", [1024, T], BF16, "ExternalOutput")
    projT = k.dram("projT", [NFM1 * 128, T], BF16)
    vna = k.dram("vna", [T, 384], BF16)
    ba = k.dram("ba", [128, 32 * 12], F32)
    Bk = [k.ps("bk%d" % i, [128, 512], F32) for i in range(8)]
    ones, epsc, ident, identf = consts(k)
    hvs = k.sb("hvs", [128, 8])
    k.dma(hvs, hv)
    k.ts("dve", hvs[:, 4:5], hvs[:, 0:1], 128.0 ** -0.5, ALU.mult)
    k.ts("dve", hvs[:, 5:6], hvs[:, 2:3], 128.0 ** -0.5, ALU.mult)
    kmemT = k.sb("kmemT", [128, 2, 256], BF16)
    vmem = k.sb("vmem", [128, 2, 2, 129], BF16)
    if "A" in stages:
        stage_proj(k, Bk, xT, win, lng, NFM1 * 128, TM1, projT,
                   [(vna, 0, 384, BF16, "rows"), (ba, 384, 12, F32, "ptile")], ones, epsc)
    if "M" in stages:
        stage_memkv(k, Bk, memT, wkv, memg, hvs[:, 3:4], ones, epsc, kmemT, vmem)
    k.barrier()
    if "NA" in stages:
        stage_na(k, Bk, projT, vna, nab, nam, hvs[:, 4:5], hvs[:, 1:2], ones, epsc, ident, out)
    if "ME" in stages:
        stage_memattn(k, Bk, projT, 21, 23, hvs[:, 5:6], kmemT, vmem, ones, epsc, ident, out, 768)
    if "DN" in stages:
        stage_dn(k, Bk, projT, ba, dnrow, convw, onrow, ident, ones, epsc, out)
    k.barrier()
    return nc


def na_tables(rpb_g):
    p = np.arange(128)
    c = np.arange(4)
    q = np.arange(64)
    o = (2 * c[None, :] + (p[:, None] // 64))
    kc = p % 64
    dcm = np.clip(kc[:, None] - q[None, :], -15, 15) + 15
    c0 = np.clip(q - 8, 0, 48)
    ok = (kc[:, None] >= c0[None, :]) & (kc[:, None] < c0[None, :] + 16)
    tabs = np.zeros((3, 8, 128, 4, 64), np.float32)
    rows = [0, 1, 2, 3, 30, 61, 62, 63]
    for ti, r in enumerate(rows):
        r0 = min(max(r - 4, 0), 56)
        dr = r0 + o - r + 7
        tabs[:, ti] = rpb_g[:, dr[:, :, None], dcm[:, None, :]]
    negm = np.where(ok, 0.0, -30000.0).astype(np.float32)
    negm = np.broadcast_to(negm[:, None, :], (128, 4, 64)).reshape(128, 256)
    return tabs.reshape(3, 8, 128, 256), np.ascontiguousarray(negm)


def prep_p1(inp, b, g):
    f = np.float32
    x = inp["x"][b]
    w = inp["e_w_in"][0]
    NAW = 768
    def hcols(base, h0, nh, hd=128):
        return list(range(base + h0 * hd, base + (h0 + nh) * hd))
    na_q, na_k, na_v, na_g = 0, NAW, 2 * NAW, 3 * NAW
    dn0 = 4 * NAW
    dn_q, dn_k, dn_v, dn_g = dn0, dn0 + 768, dn0 + 1536, dn0 + 2304
    ba0 = dn0 + 3072
    mq0 = ba0 + 24
    mg0 = mq0 + 512
    cols = []
    cols += hcols(na_q, 3 * g, 3) + hcols(na_k, 3 * g, 3) + hcols(na_g, 3 * g, 3)
    cols += hcols(dn_q, 3 * g, 3) + hcols(dn_k, 3 * g, 3) + hcols(dn_v, 3 * g, 3) + hcols(dn_g, 3 * g, 3)
    cols += hcols(mq0, 2 * g, 2) + hcols(mg0, 2 * g, 2)
    cols += hcols(na_v, 3 * g, 3)
    for j in range(4):
        cols += [ba0 + 6 * j + 3 * g + i for i in range(3)]
    cols = np.asarray(cols)
    assert len(cols) == NFM1 * 128 + TM1
    wkv = inp["e_w_mem_kv"][0]
    kvc = hcols(0, 2 * g, 2) + hcols(512, 2 * g, 2)
    hv = np.zeros((128, 8), f)
    hv[:, 0] = inp["na_q_norm"][0]
    hv[:, 1] = inp["na_k_norm"][0]
    hv[:, 2] = inp["e_mem_q_norm"][0]
    hv[:, 3] = inp["e_mem_k_norm"][0]
    nab, nam = na_tables(inp["na_rpb"][0][3 * g:3 * g + 3])
    dnrow = np.zeros((128, 2, 2, 32, 3), f)
    for d_ in range(2):
        dnrow[:, 0, d_] = inp["dn_dt_bias"][0][d_, 3 * g:3 * g + 3][None, None, :]
        dnrow[:, 1, d_] = inp["dn_a_log"][0][d_, 3 * g:3 * g + 3][None, None, :]
    cw = inp["dn_conv_w"][0]
    convw = np.zeros((128, 9, 5), f)
    for X in range(3):
        for hh in range(3):
            c0 = X * 768 + (3 * g + hh) * 128
            convw[:, X * 3 + hh, :] = cw[:, c0:c0 + 128].T
    onrow = np.ascontiguousarray(np.broadcast_to(inp["dn_out_norm"][0][None, :], (128, 128))).astype(f)
    return {
        "xT": np.ascontiguousarray(x.T),
        "memT": np.ascontiguousarray(inp["mem"][b].T),
        "win": np.ascontiguousarray(w[:, cols]),
        "wkv": np.ascontiguousarray(wkv[:, kvc]),
        "lng": np.ascontiguousarray(inp["e_ln_g"][0].reshape(16, 128).T),
        "memg": np.ascontiguousarray(inp["mem_norm_g"].reshape(16, 128).T),
        "hv": hv,
        "nab": nab,
        "nam": nam,
        "dnrow": dnrow,
        "convw": convw,
        "onrow": onrow,
    }


T = 4096
NT_ = 32


def tri_consts(k, identb):
    c = {}
    onesf = k.sb("onesf", [128, 128])
    k.memset("pool", onesf, 1.0)
    c["onesf"] = onesf
    def sel(name, src_val, pattern, cm, base, op, fill, dt):
        tmp = k.sb(name + "_f", [128, 128])
        k.memset("pool", tmp, src_val)
        k.emit("pool", lambda e: e.affine_select(out=tmp.ap, in_=tmp.ap, pattern=pattern, base=base,
                                                  channel_multiplier=cm, compare_op=op, fill=fill),
               w=[tmp], r=[tmp])
        if dt == F32:
            return tmp
        o = k.sb(name, [128, 128], dt)
        k.cp("pool", o, tmp)
        return o
    c["U"] = [sel("Uf", 1.0, [[1, 128]], -1, 0, ALU.is_ge, 0.0, F32),
              sel("Ub", 1.0, [[-1, 128]], 1, 0, ALU.is_ge, 0.0, F32)]
    c["NM"] = [sel("NMf", 0.0, [[-1, 128]], 1, 0, ALU.is_ge, -1e30, BF16),
               sel("NMb", 0.0, [[1, 128]], -1, 0, ALU.is_ge, -1e30, BF16)]
    c["ST"] = [sel("STf", 1.0, [[-1, 128]], 1, -1, ALU.is_ge, 0.0, BF16),
               sel("STb", 1.0, [[1, 128]], -1, -1, ALU.is_ge, 0.0, BF16)]
    return c


def dn_gates(k, Bk, ba, dnrow, tc):
    ba_sb = k.sb("ba_sb", [128, 32, 12])
    k.dma(ba_sb, ba.re("p (t c) -> p t c", c=12))
    row = k.sb("dnrow", [128, 2, 2, 32, 3])
    k.dma(row, dnrow)
    one1 = k.sb("one1", [128, 1])
    k.memset("dve", one1, 1.0)
    G = {}
    def arr(n):
        G[n] = k.sb("g_" + n, [128, 2, 32, 3])
        return G[n]
    def v4(a):
        return a.re("p d t h -> p t d h")
    bsrc = ba_sb[:, :, 0:6].re("p t (d h) -> p t d h", d=2)
    asrc = ba_sb[:, :, 6:12].re("p t (d h) -> p t d h", d=2)
    beta = arr("beta")
    k.act(v4(beta), bsrc, AF.Sigmoid)
    z = arr("z")
    k.tt("dve", v4(z), asrc, v4(row[:, 0]), ALU.add)
    k.act(z, z, AF.Exp)
    k.act(z, z, AF.Ln, bias=one1)
    A = arr("A")
    k.act(A, row[:, 1], AF.Exp)
    g = arr("g")
    k.stt(g, z, -1.0, A, ALU.mult, ALU.mult)
    gneg = arr("gneg")
    k.ts("dve", gneg, g, -1.0, ALU.mult)
    nbeta = arr("nbeta")
    k.ts("dve", nbeta, beta, -1.0, ALU.mult)
    Gc = arr("G")
    Gt = arr("Gt")
    for d in range(2):
        k.mm(Bk[6][:, :96], tc["U"][d], g[:, d].re("p t h -> p (t h)"))
        k.cp("dve", Gc[:, d].re("p t h -> p (t h)"), Bk[6][:, :96])
        k.mm(Bk[7][:, :96], tc["onesf"], g[:, d].re("p t h -> p (t h)"))
        k.cp("dve", Gt[:, d].re("p t h -> p (t h)"), Bk[7][:, :96])
    eG = arr("eG")
    k.act(eG, Gc, AF.Exp)
    beG = arr("beG")
    k.tt("dve", beG, beta, eG, ALU.mult)
    ekd = arr("ekd")
    k.tt("dve", ekd, Gt, Gc, ALU.subtract)
    k.act(ekd, ekd, AF.Exp)
    gend = arr("gend")
    k.act(gend, Gt, AF.Exp)
    return G


def dn_prep_head(k, Bk, projT, h, convw, ident, ones, epsc, qn, kn, vf, colq, colk):
    k.push()
    xp = k.sb("xp", [128, T + 4], BF16)
    dg = k.sb("dg", [128, 5, 128], BF16)
    xcF = k.sb("xcF", [128, T])
    sq2 = [k.sb("dsq%d" % i, [128, 512], BF16) for i in range(2)]
    rs2 = [k.sb("drs%d" % i, [128, 512]) for i in range(2)]
    rstd2 = [k.sb("drstd%d" % i, [128, 512]) for i in range(2)]
    k.memset("pool", xp[:, 0:2], 0.0)
    k.memset("pool", xp[:, T + 2:T + 4], 0.0)
    for X, (chunk0, dst) in enumerate(((9, qn), (12, kn), (15, vf))):
        k.dma(xp[:, 2:T + 2], projT[(chunk0 + h) * 128:(chunk0 + h + 1) * 128, :])
        for j in range(5):
            k.ts("dve", dg[:, j, :], ident, convw[:, X * 3 + h, j:j + 1], ALU.mult)
        for blk in range(T // 512):
            t0 = blk * 512
            ps = Bk[6 + blk % 2]
            for j in range(5):
                k.mm(ps, dg[:, j, :], xp[:, t0 + j:t0 + j + 512], start=(j == 0), stop=(j == 4))
            k.act((dst if X == 2 else xcF)[:, t0:t0 + 512], ps, AF.Silu)
        if X == 2:
            continue
        for blk in range(T // 512):
            t0 = blk * 512
            b2 = blk % 2
            x_ = xcF[:, t0:t0 + 512]
            sq, rs, rstd = sq2[b2], rs2[b2], rstd2[b2]
            pb = Bk[5 - b2]
            k.act(sq, x_, AF.Square)
            k.mm(pb, ones, sq)
            k.act(rs, pb, AF.Sqrt, scale=1.0, bias=epsc)
            k.recip(rstd, rs)
            k.stt(dst[:, t0:t0 + 512], x_, colq if X == 0 else colk, rstd, ALU.mult, ALU.mult)
    k.pop()


def dn_chain(k, B, h, d, G, tc, ident, qn, kn, vf, oacc, cidx=0):
    nm = "c%d%d" % (h, d)
    def bt(n, dt=BF16):
        return k.sb(nm + n, [128, 128], dt)
    D, Ds, qkm = bt("D"), bt("Ds"), bt("qkm")
    N = bt("N")
    TQ = k.sb(nm + "TQ", [128, 256], BF16)
    NT, qkT = TQ[:, 0:128], TQ[:, 128:256]
    XX = [k.sb(nm + "XX%d" % i, [128, 256], BF16) for i in range(2)]
    P = [bt("P0"), bt("P1")]
    kbe, kdec, vb, wT, vnew, Sb = bt("kbe"), bt("kdec"), bt("vb"), bt("wT"), bt("vnew"), bt("Sb")
    gnb, u, tmp, S = bt("gnb", F32), bt("u", F32), bt("tmp", F32), bt("S", F32)
    k.memset("pool", S, 0.0)
    k.memset("pool", Sb, 0.0)
    eA = "act" if cidx % 2 == 0 else "dve"
    eB = "dve" if cidx % 2 == 0 else "act"
    s = [B[:, i * 128:(i + 1) * 128] for i in range(4)]
    Bb = B.bitcast(BF16)
    sb_ = [Bb[:, i * 256:i * 256 + 128] for i in range(4)]
    yield
    order = range(NT_) if d == 0 else range(NT_ - 1, -1, -1)
    for n in order:
        tl = slice(n * 128, (n + 1) * 128)
        col = lambda a: a[:, d, n, h:h + 1]
        k.ts("dve", gnb, tc["onesf"], col(G["gneg"]), ALU.mult)
        k.mm(s[0], gnb, tc["U"][d], start=True, stop=False)
        k.mm(s[0], ident, tc["NM"][d], start=False, stop=True)
        k.act(D, s[0], AF.Exp, bias=col(G["G"]))
        k.tt("pool", Ds, D, tc["ST"][d], ALU.mult)
        yield
        k.mm(s[1], kn[:, tl], kn[:, tl])
        k.stt(N, s[1], col(G["nbeta"]), Ds, ALU.mult, ALU.mult)
        k.mm(s[2], qn[:, tl], kn[:, tl])
        k.tt("dve", qkm, s[2], D, ALU.mult)
        yield
        k.tr(sb_[2], N, ident)
        k.tr(Bb[:, 2 * 256 + 128:2 * 256 + 256], qkm, ident)
        k.cp("act", TQ, Bb[:, 2 * 256:2 * 256 + 256])
        yield
        k.tr(sb_[1], kn[:, tl], ident)
        k.tr(sb_[2], vf[:, tl], ident)
        k.ts("dve", kbe, sb_[1], col(G["beG"]), ALU.mult)
        k.ts("dve", kdec, sb_[1], col(G["ekd"]), ALU.mult)
        k.ts("dve", vb, sb_[2], col(G["beta"]), ALU.mult)
        yield
        Xc, XTc = N, NT
        k.tt("pool", P[0], NT, ident, ALU.add)
        Pc = P[0]
        pi = 1
        for l in range(6):
            eng = eA if l % 2 == 0 else eB
            xx = XX[l % 2]
            Xn, XTn = xx[:, 0:128], xx[:, 128:256]
            k.mm(s[0], XTc, Xc)
            if l < 5:
                k.mm(s[1], Xc, XTc)
                k.cp(eng, xx, B[:, 0:256])
            else:
                k.cp(eng, Xn, s[0])
            Xc, XTc = Xn, XTn
            yield
            k.mm(s[2], Xc, Pc, start=True, stop=False)
            k.mm(s[2], ident, Pc, start=False, stop=True)
            Pn = P[pi]
            pi ^= 1
            k.cp(eB if l % 2 == 0 else eA, Pn, s[2])
            Pc = Pn
            yield
        TT = Pc
        k.mm(s[0], TT, vb)
        k.mm(s[1], kbe, TT)
        k.cp("act", u, s[0])
        k.cp("act", wT, s[1])
        yield
        k.mm(s[2], wT, Sb)
        k.tt("dve", vnew, u, s[2], ALU.subtract)
        yield
        k.mm(s[3], qn[:, tl], Sb)
        k.stt(tmp, s[3], col(G["eG"]), oacc[:, n, :], ALU.mult, ALU.add)
        k.mm(s[0], qkT, vnew)
        k.tt("dve", oacc[:, n, :], s[0], tmp, ALU.add)
        k.mm(s[1], kdec, vnew)
        k.stt(S, S, col(G["gend"]), s[1], ALU.mult, ALU.add)
        k.cp("pool", Sb, S)
        yield


def dn_out_head(k, Bk, projT, h, oacc, onrow, ident, epsc, out, orow0):
    k.push()
    gT = k.sb("dgT", [128, T], BF16)
    mixst = k.sb("dmix", [128, T], BF16)
    junk = k.sb("djunk", [128, 128])
    ss = k.sb("dss", [128, 32])
    rs = k.sb("drs2", [128, 32])
    rstd = k.sb("drstd2", [128, 32])
    on = [k.sb("don%d" % i, [128, 128], BF16) for i in range(2)]
    k.dma(gT, projT[(18 + h) * 128:(19 + h) * 128, :])
    k.act(gT, gT, AF.Silu)
    for n in range(NT_):
        k.act(junk, oacc[:, n, :], AF.Square, accum_out=ss[:, n:n + 1])
    k.act(rs, ss, AF.Sqrt, scale=1.0 / 128, bias=epsc)
    k.recip(rstd, rs)
    for n in range(NT_):
        o_ = on[n % 2]
        k.stt(o_, oacc[:, n, :], rstd[:, n:n + 1], onrow, ALU.mult, ALU.mult)
        pt = Bk[6 + n % 2].bitcast(BF16)[:, :128]
        k.tr(pt, o_, ident)
        k.tt("dve", mixst[:, n * 128:(n + 1) * 128], pt, gT[:, n * 128:(n + 1) * 128], ALU.mult)
    k.dma(out[orow0 + h * 128:orow0 + (h + 1) * 128, :], mixst)
    k.pop()


def stage_dn(k, Bk, projT, ba, dnrow, convw_d, onrow_d, ident, ones, epsc, out, heads=(0, 1, 2)):
    k.push()
    tc = tri_consts(k, ident)
    G = dn_gates(k, Bk, ba, dnrow, tc)
    convw = k.sb("convw", [128, 9, 5])
    k.dma(convw, convw_d)
    onrow = k.sb("onrow", [128, 128])
    k.dma(onrow, onrow_d)
    colq = k.sb("colq", [128, 1])
    colk = k.sb("colk", [128, 1])
    k.memset("dve", colq, 128.0 ** -0.5)
    k.memset("dve", colk, 1.0)
    nh = len(heads)
    qn = [k.sb("dqn%d" % h, [128, T], BF16) for h in range(nh)]
    kn = [k.sb("dkn%d" % h, [128, T], BF16) for h in range(nh)]
    vf = [k.sb("dvf%d" % h, [128, T], BF16) for h in range(nh)]
    oacc = [k.sb("doacc%d" % h, [128, 32, 128]) for h in range(nh)]
    for o_ in oacc:
        k.memset("pool", o_, 0.0)
    for i, h in enumerate(heads):
        dn_prep_head(k, Bk, projT, h, convw, ident, ones, epsc, qn[i], kn[i], vf[i], colq, colk)
    k.barrier()
    k.push()
    chains = []
    for i, h in enumerate(heads):
        for d in range(2):
            chains.append(dn_chain(k, Bk[len(chains)], h, d, G, tc, ident, qn[i], kn[i], vf[i], oacc[i], cidx=len(chains)))
    active = list(chains)
    while active:
        nxt = []
        for c in active:
            try:
                next(c)
                nxt.append(c)
            except StopIteration:
                pass
        active = nxt
    k.pop()
    for i, h in enumerate(heads):
        dn_out_head(k, Bk, projT, h, oacc[i], onrow, ident, epsc, out, 384)
    k.pop()


import math
T = 4096
D = 2048
EPS = 1e-6
NFM3 = 22
LAM_INIT = 0.8 - 0.6 * math.exp(-0.3 * 1)


def build_p2():
    nc = bass.Bass("TRN2", target_bir_lowering=False)
    k = K(nc)
    TT = 2048
    mixT = k.dram("mixT", [D, TT], BF16, "ExternalInput")
    xT = k.dram("xT", [D, TT], F32, "ExternalInput")
    wout = k.dram("wout", [D, D], F32, "ExternalInput")
    yT = k.dram("yT", [D, TT], F32, "ExternalOutput")
    Bk = [k.ps("bk%d" % i, [128, 512], F32) for i in range(8)]
    Wo = k.sb("Wo", [128, 16, D], BF16)
    for kc in range(16):
        k.dma(Wo[:, kc, :], wout[kc * 128:(kc + 1) * 128, :], eng="pool")
    mb = [k.sb("mb%d" % i, [128, 16, 512], BF16) for i in range(2)]
    xb = [k.sb("xb%d" % i, [128, 16, 512], F32) for i in range(2)]
    st = [k.sb("st%d" % i, [128, 512], F32) for i in range(4)]
    mv = mixT.re("(kc p) t -> p kc t", p=128)
    xv = xT.re("(kc p) t -> p kc t", p=128)
    for tb in range(TT // 512):
        t0 = tb * 512
        m_, x_ = mb[tb % 2], xb[tb % 2]
        for q4 in range(4):
            k.dma(m_[:, q4 * 4:(q4 + 1) * 4, :], mv[:, q4 * 4:(q4 + 1) * 4, t0:t0 + 512])
            k.dma(x_[:, q4 * 4:(q4 + 1) * 4, :], xv[:, q4 * 4:(q4 + 1) * 4, t0:t0 + 512])
        for dc in range(16):
            ps = Bk[dc % 4]
            for kc in range(16):
                k.mm(ps, Wo[:, kc, dc * 128:(dc + 1) * 128], m_[:, kc, :], start=(kc == 0), stop=(kc == 15), sig=(kc == 15))
            s_ = st[dc % 4]
            k.tt("dve", s_, ps, x_[:, dc, :], ALU.add)
            k.dma(yT[dc * 128:(dc + 1) * 128, t0:t0 + 512], s_)
    k.barrier()
    return nc


def mix_perm():
    idx = []
    for g in range(2):
        for h in range(3):
            idx += list(range((3 * g + h) * 128, (3 * g + h + 1) * 128))
        for h in range(3):
            idx += list(range(768 + (3 * g + h) * 128, 768 + (3 * g + h + 1) * 128))
        for h in range(2):
            idx += list(range(1536 + (2 * g + h) * 128, 1536 + (2 * g + h + 1) * 128))
    return np.asarray(idx)


def mix_perm_odd():
    idx = []
    for g in range(2):
        for h in range(3):
            idx += list(range((3 * g + h) * 256, (3 * g + h + 1) * 256))
        for h in range(2):
            idx += list(range(1536 + (2 * g + h) * 128, 1536 + (2 * g + h + 1) * 128))
    return np.asarray(idx)


def rope_tables(k):
    C = k.sb("ropeC", [128, T])
    S = k.sb("ropeS", [128, T])
    k.push()
    pos = k.sb("pos", [128, T])
    k.emit("pool", lambda e: e.iota(pos.ap, [[1, T]], channel_multiplier=0, allow_small_or_imprecise_dtypes=True),
           w=[pos], r=[])
    pidx = k.sb("pidx", [128, 1])
    for half in range(2):
        v = pidx[half * 64:(half + 1) * 64, :]
        k.emit("pool", lambda e, v=v: e.iota(v.ap, [[0, 1]], channel_multiplier=1, allow_small_or_imprecise_dtypes=True),
               w=[v], r=[])
    inv = k.sb("inv", [128, 1])
    k.act(inv, pidx, AF.Exp, scale=-math.log(10000.0) / 64.0)
    ang = k.sb("ang", [128, T])
    k.ts("dve", ang, pos, inv, ALU.mult)
    y = k.sb("ropey", [128, T])
    ni = k.sb("ropeni", [128, T], mybir.dt.int32)
    npi = k.sb("npi", [128, 1])
    k.memset("dve", npi, -math.pi)
    c1 = 6.28125
    c2 = 2.0 * math.pi - c1
    for tab, shift in ((S, 0.0), (C, math.pi / 2)):
        k.ts("dve", y, ang, shift, ALU.add, 1.0 / (2 * math.pi), ALU.mult)
        k.cp("dve", ni, y)
        k.cp("dve", y, ni)
        k.stt(tab, y, -c1, ang, ALU.mult, ALU.add)
        k.stt(tab, y, -c2, tab, ALU.mult, ALU.add)
        if shift:
            k.ts("dve", tab, tab, shift, ALU.add)
        k.ts("dve", y, tab, math.pi, ALU.is_gt)
        k.stt(tab, y, -2 * math.pi, tab, ALU.mult, ALU.add)
        k.ts("dve", y, tab, -math.pi, ALU.is_lt)
        k.stt(tab, y, 2 * math.pi, tab, ALU.mult, ALU.add)
        k.act(tab, tab, AF.Sin)
    k.ts("dve", S[0:64, :], S[0:64, :], -1.0, ALU.mult)
    k.pop()
    return C, S


def stage_diff(k, Bk, projT, vdf, hvs, lamrep_d, subrow_d, ones, epsc, ident, identf, out, heads=(0, 1, 2)):
    k.push()
    C, S = rope_tables(k)
    swp = k.sb("swp", [128, 128], BF16)
    k.cp("pool", swp[:, 0:64], ident[:, 64:128])
    k.cp("pool", swp[:, 64:128], ident[:, 0:64])
    lamrep = k.sb("lamrep", [128, 4, 128])
    k.dma(lamrep, lamrep_d)
    lt = k.sb("lt", [128, 2, 128])
    le = k.sb("le", [128, 2])
    k.tt("dve", lt[:, 0, :], lamrep[:, 0, :], lamrep[:, 1, :], ALU.mult)
    k.tt("dve", lt[:, 1, :], lamrep[:, 2, :], lamrep[:, 3, :], ALU.mult)
    k.emit("dve", lambda e: e.tensor_reduce(out=le.ap, in_=lt.ap, axis=AX.X, op=ALU.add), w=[le], r=[lt])
    k.act(le, le, AF.Exp)
    nlam = k.sb("nlam", [128, 1])
    k.tt("dve", nlam, le[:, 1:2], le[:, 0:1], ALU.subtract)
    k.ts("dve", nlam, nlam, -LAM_INIT, ALU.add)
    subrow = k.sb("subrow", [128, 256])
    k.dma(subrow, subrow_d)
    k.ts("dve", subrow, subrow, 1.0 - LAM_INIT, ALU.mult)
    raw = k.sb("fraw", [128, T], BF16)
    qk = [k.sb("fqk%d" % i, [128, T], BF16) for i in range(4)]
    va = k.sb("fva", [128, 32, 257], BF16)
    gT = [k.sb("fg%d" % i, [128, T], BF16) for i in range(2)]
    mixst = [k.sb("fmix%d" % i, [128, T], BF16) for i in range(2)]
    sq2 = [k.sb("fsq%d" % i, [128, 512], BF16) for i in range(2)]
    rs2 = [k.sb("frs%d" % i, [128, 512]) for i in range(2)]
    rstd2 = [k.sb("frstd%d" % i, [128, 512]) for i in range(2)]
    xn2 = [k.sb("fxn%d" % i, [128, 512], BF16) for i in range(2)]
    t12 = [k.sb("ft1%d" % i, [128, 512]) for i in range(2)]
    t22 = [k.sb("ft2%d" % i, [128, 512]) for i in range(2)]
    pT = [k.sb("fpT%d" % i, [128, 512], BF16) for i in range(4)]
    accs = [[k.sb("facc%d%d" % (m, s_), [128, 257]) for s_ in range(2)] for m in range(2)]
    rc = k.sb("frc", [128, 4])
    o1 = k.sb("fo1", [128, 256])
    o = k.sb("fo", [128, 256])
    junk = k.sb("fjunk", [128, 256])
    ss = k.sb("fss", [128, 2])
    on = [k.sb("fon%d" % i, [128, 256], BF16) for i in range(2)]
    k.memset("pool", va, 1.0)
    mhalf = k.sb("mhalf", [128, 1])
    k.memset("pool", mhalf, -0.5)
    for h in heads:
        for i in range(4):
            chunk = (0 if i < 2 else 6) + 2 * h + (i % 2)
            gain = hvs[:, 4:5] if i < 2 else hvs[:, 1:2]
            k.dma(raw, projT[chunk * 128:(chunk + 1) * 128, :])
            for blk in range(T // 512):
                t0 = blk * 512
                b2 = blk % 2
                xn, t1, t2 = xn2[b2], t12[b2], t22[b2]
                rms_fm(k, Bk[6 - 2 * b2], ones, epsc, raw[:, t0:t0 + 512], xn, gain, 512, sq2[b2], rs2[b2], rstd2[b2], 128)
                k.mm(Bk[7 - 2 * b2], swp, xn)
                k.tt("pool", t1, xn, C[:, t0:t0 + 512], ALU.mult)
                k.tt("dve", t2, Bk[7 - 2 * b2], S[:, t0:t0 + 512], ALU.mult)
                k.tt("dve", qk[i][:, t0:t0 + 512], t1, t2, ALU.add)
        k.dma(va[:, :, 0:256], vdf.re("(tl p) c -> p tl c", p=128)[:, :, h * 256:(h + 1) * 256])
        for e in range(2):
            k.dma(gT[e], projT[(12 + 2 * h + e) * 128:(13 + 2 * h + e) * 128, :])
            k.act(gT[e], gT[e], AF.Silu)
        def epilogue(q0):
            for s_ in range(2):
                a0, a1 = accs[0][s_], accs[1][s_]
                k.recip(rc[:, 0:1], a0[:, 256:257])
                yield
                k.ts("dve", o1, a0[:, 0:256], rc[:, 0:1], ALU.mult)
                yield
                k.recip(rc[:, 1:2], a1[:, 256:257])
                k.tt("dve", rc[:, 2:3], rc[:, 1:2], nlam, ALU.mult)
                yield
                k.stt(o, a1[:, 0:256], rc[:, 2:3], o1, ALU.mult, ALU.add)
                yield
                k.tt("pool", junk, o, o, ALU.mult)
                yield
                k.emit("dve", lambda e_: e_.tensor_reduce(out=ss[:, 0:1].ap, in_=junk.ap, axis=AX.X, op=ALU.add),
                       w=[ss], r=[junk])
                k.ts("dve", ss[:, 1:2], ss[:, 0:1], 1.0 / 256, ALU.mult, EPS, ALU.add)
                yield
                yield
                k.tt("pool", rc[:, 3:4], ss[:, 1:2], mhalf, ALU.pow)
                yield
                o_ = on[s_]
                k.stt(o_, o, rc[:, 3:4], subrow, ALU.mult, ALU.mult)
                yield
                yield
                tok = q0 + s_ * 128
                for e in range(2):
                    pt = Bk[7].bitcast(BF16)[:, e * 128:(e + 1) * 128]
                    k.tr(pt, o_[:, e * 128:(e + 1) * 128], ident)
                    yield
                    k.tt("dve", mixst[e][:, tok:tok + 128], pt, gT[e][:, tok:tok + 128], ALU.mult)
                    yield

        pend = iter(())
        for qb in range(T // 256):
            q0 = qb * 256
            acc = [[Bk[2 + 2 * m + s_][:, :257] for s_ in range(2)] for m in range(2)]
            stb = [Bk[0], Bk[1], Bk[6]]

            def qk_(kc):
                ps = stb[kc % 3]
                for m in range(2):
                    k.mm(ps[:, m * 256:(m + 1) * 256], qk[2 + m][:, kc * 128:(kc + 1) * 128], qk[m][:, q0:q0 + 256])

            def pv_(kc):
                p_ = pT[kc % 4]
                for m in range(2):
                    for s_ in range(2):
                        k.mm(acc[m][s_], p_[:, m * 256 + s_ * 128:m * 256 + (s_ + 1) * 128], va[:, kc, :],
                             start=(kc == 0), stop=(kc == 31), sig=(kc == 31 or True))
            qk_(0)
            qk_(1)
            for kc in range(32):
                k.act(pT[kc % 4], stb[kc % 3], AF.Exp)
                if kc + 2 < 32:
                    qk_(kc + 2)
                pv_(kc)
                next(pend, None)
            for m in range(2):
                for s_ in range(2):
                    k.cp("dve", accs[m][s_], acc[m][s_])
            pend = epilogue(q0)
        for _ in pend:
            pass
        for e in range(2):
            k.dma(out[(2 * h + e) * 128:(2 * h + e + 1) * 128, :], mixst[e])
    k.pop()


def build_p3(stages=("A", "M", "DF", "ME")):
    nc = bass.Bass("TRN2", target_bir_lowering=False)
    k = K(nc)
    xT = k.dram("xT", [D, T], F32, "ExternalInput")
    memT = k.dram("memT", [D, 256], F32, "ExternalInput")
    win = k.dram("win", [D, NFM3 * 128 + 768], F32, "ExternalInput")
    wkv = k.dram("wkv", [D, 512], F32, "ExternalInput")
    lng = k.dram("lng", [128, 16], F32, "ExternalInput")
    memg = k.dram("memg", [128, 16], F32, "ExternalInput")
    hv = k.dram("hv", [128, 8], F32, "ExternalInput")
    lamrep = k.dram("lamrep", [128, 4, 128], F32, "ExternalInput")
    subrow = k.dram("subrow", [128, 256], F32, "ExternalInput")
    out = k.dram("mixT", [1024, T], BF16, "ExternalOutput")
    projT = k.dram("projT", [NFM3 * 128, T], BF16)
    vdf = k.dram("vdf", [T, 768], BF16)
    Bk = [k.ps("bk%d" % i, [128, 512], F32) for i in range(8)]
    ones, epsc, ident, identf = consts(k)
    hvs = k.sb("hvs", [128, 8])
    k.dma(hvs, hv)
    k.ts("dve", hvs[:, 4:5], hvs[:, 0:1], 128.0 ** -0.5, ALU.mult)
    k.ts("dve", hvs[:, 5:6], hvs[:, 2:3], 128.0 ** -0.5, ALU.mult)
    kmemT = k.sb("kmemT", [128, 2, 256], BF16)
    vmem = k.sb("vmem", [128, 2, 2, 129], BF16)
    if "A" in stages:
        stage_proj(k, Bk, xT, win, lng, NFM3 * 128, 768, projT,
                      [(vdf, 0, 384, BF16, "rows"), (vdf, 384, 384, BF16, "rows")], ones, epsc)
    if "M" in stages:
        stage_memkv(k, Bk, memT, wkv, memg, hvs[:, 3:4], ones, epsc, kmemT, vmem)
    k.barrier()
    if "DF" in stages:
        stage_diff(k, Bk, projT, vdf, hvs, lamrep, subrow, ones, epsc, ident, identf, out)
    if "ME" in stages:
        stage_memattn(k, Bk, projT, 18, 20, hvs[:, 5:6], kmemT, vmem, ones, epsc, ident, out, 768)
    k.barrier()
    return nc


def prep_p3(inp, x1T_b, b, g):
    f = np.float32
    w = inp["o_w_in"][0]
    DW = 1536
    cols = []
    for base in (0, DW):
        for h in range(3):
            cols += list(range(base + (3 * g + h) * 256, base + (3 * g + h + 1) * 256))
    for h in range(3):
        cols += list(range(3 * DW + (3 * g + h) * 256, 3 * DW + (3 * g + h + 1) * 256))
    mq0 = 4 * DW
    mg0 = mq0 + 512
    cols += list(range(mq0 + 2 * g * 128, mq0 + (2 * g + 2) * 128))
    cols += list(range(mg0 + 2 * g * 128, mg0 + (2 * g + 2) * 128))
    for h in range(3):
        cols += list(range(2 * DW + (3 * g + h) * 256, 2 * DW + (3 * g + h + 1) * 256))
    cols = np.asarray(cols)
    assert len(cols) == NFM3 * 128 + 768
    wkv = inp["o_w_mem_kv"][0]
    kvc = list(range(2 * g * 128, (2 * g + 2) * 128)) + list(range(512 + 2 * g * 128, 512 + (2 * g + 2) * 128))
    hv = np.zeros((128, 8), f)
    hv[:, 0] = inp["df_q_norm"][0]
    hv[:, 1] = inp["df_k_norm"][0]
    hv[:, 2] = inp["o_mem_q_norm"][0]
    hv[:, 3] = inp["o_mem_k_norm"][0]
    return {
        "xT": x1T_b,
        "memT": np.ascontiguousarray(inp["mem"][b].T),
        "win": np.ascontiguousarray(w[:, cols]),
        "wkv": np.ascontiguousarray(wkv[:, kvc]),
        "lng": np.ascontiguousarray(inp["o_ln_g"][0].reshape(16, 128).T),
        "memg": np.ascontiguousarray(inp["mem_norm_g"].reshape(16, 128).T),
        "hv": hv,
        "lamrep": np.ascontiguousarray(np.broadcast_to(inp["df_lambda"][0][None], (128, 4, 128))).astype(f),
        "subrow": np.ascontiguousarray(np.broadcast_to(inp["df_subln"][0][None], (128, 256))).astype(f),
    }


T = 4096
D = 2048
PAIRS = [[0, 1], [2, 3], [4, 5], [6, 7]]


def stage_wout(k, Bk, mixall, xsrc, wout, ydst, ntok=T):
    k.push()
    Wo = k.sb("Wo", [128, 16, D], BF16)
    for kc in range(16):
        k.dma(Wo[:, kc, :], wout[kc * 128:(kc + 1) * 128, :], eng="pool")
    mb = [k.sb("mb%d" % i, [128, 16, 512], BF16) for i in range(2)]
    xb = [k.sb("xwb%d" % i, [128, 16, 512], F32) for i in range(2)]
    st = [k.sb("wst%d" % i, [128, 512], F32) for i in range(4)]
    mv = mixall.re("(kc p) t -> p kc t", p=128)
    xv = xsrc.re("(kc p) t -> p kc t", p=128)
    for tb in range(ntok // 512):
        t0 = tb * 512
        m_, x_ = mb[tb % 2], xb[tb % 2]
        for q4 in range(4):
            k.dma(m_[:, q4 * 4:(q4 + 1) * 4, :], mv[:, q4 * 4:(q4 + 1) * 4, t0:t0 + 512])
            k.dma(x_[:, q4 * 4:(q4 + 1) * 4, :], xv[:, q4 * 4:(q4 + 1) * 4, t0:t0 + 512])
        for dc in range(16):
            ps = Bk[dc % 4]
            for kc in range(16):
                k.mm(ps, Wo[:, kc, dc * 128:(dc + 1) * 128], m_[:, kc, :], start=(kc == 0), stop=(kc == 15), sig=(kc == 15))
            s_ = st[dc % 4]
            k.tt("dve", s_, ps, x_[:, dc, :], ALU.add)
            k.dma(ydst[dc * 128:(dc + 1) * 128, t0:t0 + 512], s_)
    k.pop()


def build_fused():
    nc = bass.Bass("TRN2", target_bir_lowering=False)
    k = K(nc)
    EI = "ExternalInput"
    xT = k.dram("xT", [D, T], F32, EI)
    memT = k.dram("memT", [D, 256], F32, EI)
    memg = k.dram("memg", [128, 16], F32, EI)
    win0 = k.dram("win0", [D, NFM1 * 128 + TM1], F32, EI)
    wkv0 = k.dram("wkv0", [D, 512], F32, EI)
    lng0 = k.dram("lng0", [128, 16], F32, EI)
    hv0 = k.dram("hv0", [128, 8], F32, EI)
    nab = k.dram("nab", [3, 8, 128, 256], F32, EI)
    nam = k.dram("nam", [128, 256], F32, EI)
    dnrow = k.dram("dnrow", [128, 2, 2, 32, 3], F32, EI)
    convw = k.dram("convw", [128, 9, 5], F32, EI)
    onrow = k.dram("onrow", [128, 128], F32, EI)
    wout0 = k.dram("wout0", [D, D], F32, EI)
    win1 = k.dram("win1", [D, NFM3 * 128 + 768], F32, EI)
    wkv1 = k.dram("wkv1", [D, 512], F32, EI)
    lng1 = k.dram("lng1", [128, 16], F32, EI)
    hv1 = k.dram("hv1", [128, 8], F32, EI)
    lamrep = k.dram("lamrep", [128, 4, 128], F32, EI)
    subrow = k.dram("subrow", [128, 256], F32, EI)
    wout1 = k.dram("wout1", [D, D], F32, EI)
    yT = k.dram("yT", [D, T], F32, "ExternalOutput")
    projT = k.dram("projT", [NFM1 * 128, T], BF16)
    vna = k.dram("vna", [T, 384], BF16)
    ba = k.dram("ba", [128, 32 * 12], F32)
    vdf = k.dram("vdf", [T, 768], BF16)
    mixloc_f = k.dram("mixloc", [1024, T // 2], F32)
    mixall_f = k.dram("mixall", [2048, T // 2], F32)
    mixloc = mixloc_f.bitcast(BF16)
    mixall = mixall_f.bitcast(BF16)

    def exchange():
        for j in range(4):
            k.allgather(mixall_f[j * 512:(j + 1) * 512, :], mixloc_f[j * 256:(j + 1) * 256, :], PAIRS)
    x1T = k.dram("x1T", [D, T], F32)
    Bk = [k.ps("bk%d" % i, [128, 512], F32) for i in range(8)]
    ones, epsc, ident, identf = consts(k)

    def hvload(hv):
        hvs = k.sb("hvs", [128, 8])
        k.dma(hvs, hv)
        k.ts("dve", hvs[:, 4:5], hvs[:, 0:1], 128.0 ** -0.5, ALU.mult)
        k.ts("dve", hvs[:, 5:6], hvs[:, 2:3], 128.0 ** -0.5, ALU.mult)
        return hvs

    k.push()
    hvs = hvload(hv0)
    kmemT = k.sb("kmemT", [128, 2, 256], BF16)
    vmem = k.sb("vmem", [128, 2, 2, 129], BF16)
    stage_proj(k, Bk, xT, win0, lng0, NFM1 * 128, TM1, projT,
                  [(vna, 0, 384, BF16, "rows"), (ba, 384, 12, F32, "ptile")], ones, epsc)
    stage_memkv(k, Bk, memT, wkv0, memg, hvs[:, 3:4], ones, epsc, kmemT, vmem)
    k.barrier()
    stage_na(k, Bk, projT, vna, nab, nam, hvs[:, 4:5], hvs[:, 1:2], ones, epsc, ident, mixloc)
    stage_memattn(k, Bk, projT, 21, 23, hvs[:, 5:6], kmemT, vmem, ones, epsc, ident, mixloc, 768)
    stage_dn(k, Bk, projT, ba, dnrow, convw, onrow, ident, ones, epsc, mixloc)
    k.pop()
    exchange()
    stage_wout(k, Bk, mixall, xT, wout0, x1T)
    k.barrier()
    k.push()
    hvs = hvload(hv1)
    kmemT = k.sb("kmemT", [128, 2, 256], BF16)
    vmem = k.sb("vmem", [128, 2, 2, 129], BF16)
    stage_proj(k, Bk, x1T, win1, lng1, NFM3 * 128, 768, projT,
                  [(vdf, 0, 384, BF16, "rows"), (vdf, 384, 384, BF16, "rows")], ones, epsc)
    stage_memkv(k, Bk, memT, wkv1, memg, hvs[:, 3:4], ones, epsc, kmemT, vmem)
    k.barrier()
    stage_diff(k, Bk, projT, vdf, hvs, lamrep, subrow, ones, epsc, ident, identf, mixloc)
    stage_memattn(k, Bk, projT, 18, 20, hvs[:, 5:6], kmemT, vmem, ones, epsc, ident, mixloc, 768)
    k.pop()
    exchange()
    stage_wout(k, Bk, mixall, x1T, wout1, yT)
    k.barrier()
    return nc


def fused_rowmap(perm_groups):
    idx = []
    for j in range(4):
        for r in range(2):
            idx += list(perm_groups[r][j * 256:(j + 1) * 256])
    return np.asarray(idx)


def prep_fused(inp, b, g):
    a = prep_p1(inp, b, g)
    c = prep_p3(inp, None, b, g)
    m = {
        "xT": a["xT"], "memT": a["memT"], "memg": a["memg"],
        "win0": a["win"], "wkv0": a["wkv"], "lng0": a["lng"], "hv0": a["hv"],
        "nab": a["nab"], "nam": a["nam"], "dnrow": a["dnrow"], "convw": a["convw"], "onrow": a["onrow"],
        "win1": c["win"], "wkv1": c["wkv"], "lng1": c["lng"], "hv1": c["hv"],
        "lamrep": c["lamrep"], "subrow": c["subrow"],
    }
    pe = mix_perm()
    po = mix_perm_odd()
    m["wout0"] = np.ascontiguousarray(inp["e_w_out"][0][fused_rowmap([pe[:1024], pe[1024:]]), :])
    m["wout1"] = np.ascontiguousarray(inp["o_w_out"][0][fused_rowmap([po[:1024], po[1024:]]), :])
    return m


def kernel(**inputs):
    inp = {k_: np.asarray(v_) for k_, v_ in inputs.items()}
    nc = build_fused()
    maps = [prep_fused(inp, c // 2, c % 2) for c in range(8)]
    res = run_bass_kernel_spmd(nc, maps, core_ids=list(range(8)))
    out = np.stack([np.ascontiguousarray(np.asarray(res.results[2 * b]["yT"]).T) for b in range(4)], axis=0)
    return out.astype(np.float32)
```

```python
import numpy as np
import concourse.bass as bass
import concourse.mybir as mybir
from concourse.bass_utils import run_bass_kernel_spmd

F32 = mybir.dt.float32
BF16 = mybir.dt.bfloat16
ALU = mybir.AluOpType
AF = mybir.ActivationFunctionType
AX = mybir.AxisListType


class Trk:
    __slots__ = ("name", "w", "r", "dsem", "dcnt", "excl")

    def __init__(self, name, excl=False):
        self.name = name
        self.excl = excl
        self.w = None
        self.r = {}
        self.dsem = None
        self.dcnt = 0


class V:
    __slots__ = ("trk", "ap")

    def __init__(self, trk, ap):
        self.trk = trk
        self.ap = ap

    def __getitem__(self, idx):
        return V(self.trk, self.ap[idx])

    def sub(self, name, idx=None):
        return V(Trk(name), self.ap if idx is None else self.ap[idx])

    def bitcast(self, dt):
        return V(self.trk, self.ap.bitcast(dt))

    def re(self, pat, **kw):
        return V(self.trk, self.ap.rearrange(pat, **kw))


class K:
    def __init__(self, nc):
        self.nc = nc
        self.E = {"pe": nc.tensor, "act": nc.scalar, "dve": nc.vector, "pool": nc.gpsimd, "sp": nc.sync}
        self.sems = {}
        self.cnt = {}
        for e in ("pe", "act", "dve", "pool"):
            self.sems[e] = nc.alloc_semaphore("s_" + e)
            self.cnt[e] = 0
        self.waited = {}
        self.dma_trks = []
        self.nsem = 4
        self.uid = 0
        self.guards = []
        self.marks = []
        self.free_dsems = []
        self.dtot = {}
        self.cc_sem = None
        self.cc_cnt = 0

    def sb(self, name, shape, dt=F32):
        self.uid += 1
        g = self.nc.sbuf_tensor("%s_%d" % (name, self.uid), list(shape), dt)
        t = g.__enter__()
        v = V(Trk(name), t[:])
        self.guards.append((g, v))
        return v

    def push(self):
        self.marks.append(len(self.guards))

    def pop(self):
        self.barrier()
        m = self.marks.pop()
        while len(self.guards) > m:
            g, v = self.guards.pop()
            if v.trk.dsem is not None:
                self.free_dsems.append(v.trk.dsem)
                v.trk.dsem = None
            g.__exit__(None, None, None)

    def ps(self, name, shape, dt=F32):
        self.uid += 1
        t = self.nc.alloc_psum_tensor("%s_%d" % (name, self.uid), list(shape), dt)
        return V(Trk(name, excl=True), t[:])

    def dram(self, name, shape, dt=F32, kind="Internal"):
        t = self.nc.dram_tensor(name, list(shape), dt, kind=kind)
        return V(None, t.ap())

    def _dsem(self, trk):
        if trk.dsem is None:
            if self.free_dsems:
                trk.dsem = self.free_dsems.pop()
            else:
                self.nsem += 1
                key = "d%d" % self.nsem
                self.sems[key] = self.nc.alloc_semaphore("d_%d" % self.nsem)
                self.dtot[key] = 0
                trk.dsem = key
        return trk.dsem

    def _deps(self, eng, w, r):
        need = {}

        def add(dep):
            if dep is None:
                return
            k, v = dep
            if k == eng and eng == "pe":
                return
            if need.get(k, 0) < v:
                need[k] = v
        for x in r:
            if x is not None and x.trk is not None:
                add(x.trk.w)
                if x.trk.excl:
                    for k, v in x.trk.r.items():
                        if k != eng:
                            add((k, v))
        for x in w:
            if x is not None and x.trk is not None:
                add(x.trk.w)
                for k, v in x.trk.r.items():
                    add((k, v))
        return need

    def _wait(self, eng, need):
        E = self.E[eng]
        for k, v in need.items():
            if self.waited.get((eng, k), 0) >= v:
                continue
            self.waited[(eng, k)] = v
            E.wait_ge(self.sems[k], v)

    def emit(self, eng, fn, w=(), r=(), sig=True):
        need = self._deps(eng, w, r)
        self._wait(eng, need)
        ins = fn(self.E[eng])
        n = self.cnt[eng] + 1
        if sig:
            self.cnt[eng] = n
            ins.then_inc(self.sems[eng], 1)
        for x in r:
            if x is not None and x.trk is not None:
                x.trk.r[eng] = n
        for x in w:
            if x is not None and x.trk is not None:
                x.trk.w = (eng, n)
                x.trk.r = {}
        return ins

    def dma(self, out, in_, eng="sp", **kw):
        own = out if out.trk is not None else in_
        assert own.trk is not None
        sk = self._dsem(own.trk)
        need = {}
        if in_.trk is not None and in_.trk.w is not None:
            need[in_.trk.w[0]] = in_.trk.w[1]
        if out.trk is not None:
            if out.trk.w is not None and out.trk.w[0] != sk:
                need[out.trk.w[0]] = max(need.get(out.trk.w[0], 0), out.trk.w[1])
            for k2, v2 in out.trk.r.items():
                need[k2] = max(need.get(k2, 0), v2)
        need.pop(None, None)
        self._wait(eng, need)
        ins = self.E[eng].dma_start(out=out.ap, in_=in_.ap, **kw)
        self.dtot[sk] += 1
        val = 16 * self.dtot[sk]
        ins.then_inc(self.sems[sk], 16)
        if out.trk is not None:
            out.trk.w = (sk, val)
            out.trk.r = {}
        if in_.trk is not None:
            in_.trk.r[sk] = val
        return ins

    def barrier(self):
        tot = {}
        for e in ("pe", "act", "dve", "pool"):
            if self.cnt[e] > 0:
                tot[e] = self.cnt[e]
        for key, n in self.dtot.items():
            if n > 0:
                tot[key] = 16 * n
        if self.cc_cnt:
            tot["cc"] = self.cc_cnt
        for e in ("pe", "act", "dve", "pool", "sp"):
            self._wait(e, dict(tot))

    def allgather(self, out, in_, groups, inc=1):
        self.barrier()
        if self.cc_sem is None:
            self.cc_sem = self.nc.alloc_semaphore("cc")
            self.sems["cc"] = self.cc_sem
        ins = self.E["pool"].collective_compute("AllGather", ALU.bypass, replica_groups=groups,
                                                ins=[in_.ap.opt()], outs=[out.ap.opt()])
        ins.then_inc(self.cc_sem, inc)
        self.cc_cnt += inc
        self.barrier()

    def mm(self, out, lhsT, rhs, start=True, stop=True, sig=None, **kw):
        if sig is None:
            sig = True
        return self.emit("pe", lambda e: e.matmul(out.ap, lhsT.ap, rhs.ap, start=start, stop=stop, **kw),
                         w=[out], r=[lhsT, rhs], sig=sig)

    def tr(self, out, in_, ident, sig=True):
        return self.emit("pe", lambda e: e.transpose(out.ap, in_.ap, ident.ap), w=[out], r=[in_, ident], sig=sig)

    def act(self, out, in_, func, bias=None, scale=None, accum_out=None, eng="act"):
        kw = {}
        r = [in_]
        if bias is not None:
            if isinstance(bias, V):
                kw["bias"] = bias.ap
                r.append(bias)
            else:
                kw["bias"] = bias
        if scale is not None:
            if isinstance(scale, V):
                kw["scale"] = scale.ap
                r.append(scale)
            else:
                kw["scale"] = scale
        w = [out]
        if accum_out is not None:
            kw["accum_out"] = accum_out.ap
            w.append(accum_out)
        return self.emit("act", lambda e: e.activation(out.ap, in_.ap, func, **kw), w=w, r=r)

    def tt(self, eng, out, a, b, op):
        return self.emit(eng, lambda e: e.tensor_tensor(out=out.ap, in0=a.ap, in1=b.ap, op=op), w=[out], r=[a, b])

    def ts(self, eng, out, a, s1, op0, s2=None, op1=None, accum_out=None):
        r = [a]
        s1a = s1.ap if isinstance(s1, V) else s1
        s2a = s2.ap if isinstance(s2, V) else s2
        if isinstance(s1, V):
            r.append(s1)
        if isinstance(s2, V):
            r.append(s2)
        kw = {}
        w = [out]
        if op1 is not None:
            kw["op1"] = op1
        if accum_out is not None:
            kw["accum_out"] = accum_out.ap
            w.append(accum_out)
        return self.emit(eng, lambda e: e.tensor_scalar(out=out.ap, in0=a.ap, scalar1=s1a, scalar2=s2a, op0=op0, **kw),
                         w=w, r=r)

    def stt(self, out, a, s, b, op0, op1, eng="dve"):
        r = [a, b]
        sa = s.ap if isinstance(s, V) else s
        if isinstance(s, V):
            r.append(s)
        return self.emit(eng, lambda e: e.scalar_tensor_tensor(out=out.ap, in0=a.ap, scalar=sa, in1=b.ap, op0=op0, op1=op1),
                         w=[out], r=r)

    def cp(self, eng, out, in_):
        if eng == "act":
            return self.emit("act", lambda e: e.activation(out.ap, in_.ap, AF.Copy), w=[out], r=[in_])
        return self.emit(eng, lambda e: e.tensor_copy(out=out.ap, in_=in_.ap), w=[out], r=[in_])

    def memset(self, eng, out, val):
        return self.emit(eng, lambda e: e.memset(out.ap, val), w=[out], r=[])

    def recip(self, out, in_, eng="dve"):
        return self.emit(eng, lambda e: e.reciprocal(out=out.ap, in_=in_.ap), w=[out], r=[in_])


T = 4096
D = 2048
EPS = 1e-6
NTB = 8
DBG = 99
NFM1 = 25
TM1 = 396


def rms_fm(k, B, ones, epsc, src, dst, gain, n, sq, rs, rstd, nfeat):
    k.act(sq[:, :n], src, AF.Square)
    k.mm(B[:, :n], ones, sq[:, :n])
    k.act(rs[:, :n], B[:, :n], AF.Sqrt, scale=1.0 / nfeat, bias=epsc)
    k.recip(rstd[:, :n], rs[:, :n])
    k.stt(dst, src, gain, rstd[:, :n], ALU.mult, ALU.mult)


def stage_proj(k, Bk, xT, win, lng, ncols_fm, ncols_tm, projT, tm_outs, ones, epsc):
    nfm = ncols_fm // 128
    ncol = ncols_fm + ncols_tm
    k.push()
    W = k.sb("W", [128, 16, ncol], BF16)
    for kc in range(16):
        k.dma(W[:, kc, :], win[kc * 128:(kc + 1) * 128, :], eng="pool")
    lng_sb = k.sb("lng", [128, 16])
    k.dma(lng_sb, lng)
    xb = k.sb("xb", [128, 16, 512], F32)
    hT = [k.sb("hT%d" % i, [128, 16, 512], BF16) for i in range(2)]
    sq = [k.sb("sq%d" % i, [128, 512], BF16) for i in range(2)]
    rs = k.sb("rs", [128, 512])
    rstd = k.sb("rstd", [128, 512])
    stg = [k.sb("stg%d" % i, [128, 512], BF16) for i in range(4)]
    stt_ = [[k.sb("stt%d_%d" % (j, i), [128, w], dt) for i in range(2)] for j, (_, _, w, dt, _m) in enumerate(tm_outs)]
    xv = xT.re("(kc p) t -> p kc t", p=128)
    nblk = min(NTB, T // 512)

    def load(tb):
        t0 = tb * 512
        for q4 in range(4):
            k.dma(xb[:, q4 * 4:(q4 + 1) * 4, :], xv[:, q4 * 4:(q4 + 1) * 4, t0:t0 + 512])

    def prologue(tb):
        for kc in range(16):
            k.act(sq[kc % 2], xb[:, kc, :], AF.Square)
            k.mm(Bk[0], ones, sq[kc % 2], start=(kc == 0), stop=(kc == 15))
        k.act(rs, Bk[0], AF.Sqrt, scale=1.0 / D, bias=epsc)
        k.recip(rstd, rs)
        h = hT[tb % 2]
        for kc in range(16):
            k.stt(h[:, kc, :], xb[:, kc, :], lng_sb[:, kc:kc + 1], rstd, ALU.mult, ALU.mult)

    def fm(tb, c):
        t0 = tb * 512
        h = hT[tb % 2]
        ps = Bk[1 + c % 4]
        for kc in range(16):
            k.mm(ps, W[:, kc, c * 128:(c + 1) * 128], h[:, kc, :], start=(kc == 0), stop=(kc == 15), sig=(kc == 15))
        st = stg[c % 4]
        k.cp("act" if c % 2 else "dve", st, ps)
        k.dma(projT[c * 128:(c + 1) * 128, t0:t0 + 512], st)

    def tm(tb):
        t0 = tb * 512
        h = hT[tb % 2]
        for s4 in range(4):
            tok = t0 + s4 * 128
            for j, (dv, c0, w, dt, mode) in enumerate(tm_outs):
                ps = Bk[5 + (s4 * len(tm_outs) + j) % 3]
                for kc in range(16):
                    k.mm(ps[:, :w], h[:, kc, s4 * 128:(s4 + 1) * 128], W[:, kc, ncols_fm + c0:ncols_fm + c0 + w],
                         start=(kc == 0), stop=(kc == 15), sig=(kc == 15))
                st = stt_[j][s4 % 2]
                k.cp("dve" if j % 2 == 0 else "act", st, ps[:, :w])
                if mode == "rows":
                    k.dma(dv[tok:tok + 128, c0:c0 + w], st)
                else:
                    tl = tok // 128
                    k.dma(dv[:, tl * w:(tl + 1) * w], st)

    load(0)
    prologue(0)
    for tb in range(nblk):
        if tb + 1 < nblk:
            load(tb + 1)
        half = nfm // 2
        for c in range(half):
            fm(tb, c)
        if tb + 1 < nblk:
            prologue(tb + 1)
        for c in range(half, nfm):
            fm(tb, c)
        tm(tb)
    k.pop()


def stage_memkv(k, Bk, memT, wkv, memg, kgain, ones, epsc, kmemT, vmem):
    k.push()
    wk = k.sb("wkv", [128, 16, 512], BF16)
    k.dma(wk, wkv.re("(kc p) c -> p kc c", p=128), eng="pool")
    mg = k.sb("memg", [128, 16])
    k.dma(mg, memg)
    mT = k.sb("mT", [128, 16, 256])
    k.dma(mT, memT.re("(kc p) m -> p kc m", p=128))
    mn = k.sb("mn", [128, 16, 256], BF16)
    sq = [k.sb("sqm%d" % i, [128, 256], BF16) for i in range(2)]
    rs = k.sb("rsm", [128, 256])
    rstd = k.sb("rstdm", [128, 256])
    kraw = k.sb("kraw", [128, 256])
    for kc in range(16):
        k.act(sq[kc % 2], mT[:, kc, :], AF.Square)
        k.mm(Bk[0][:, :256], ones, sq[kc % 2], start=(kc == 0), stop=(kc == 15))
    k.act(rs, Bk[0][:, :256], AF.Sqrt, scale=1.0 / D, bias=epsc)
    k.recip(rstd, rs)
    for kc in range(16):
        k.stt(mn[:, kc, :], mT[:, kc, :], mg[:, kc:kc + 1], rstd, ALU.mult, ALU.mult)
    for h in range(2):
        ps = Bk[1 + h][:, :256]
        for kc in range(16):
            k.mm(ps, wk[:, kc, h * 128:(h + 1) * 128], mn[:, kc, :], start=(kc == 0), stop=(kc == 15))
        k.cp("dve", kraw, ps)
        rms_fm(k, Bk[3], ones, epsc, kraw, kmemT[:, h, :], kgain, 256, sq[0], rs, rstd, 128)
    k.memset("pool", vmem, 1.0)
    for mc in range(2):
        ps = Bk[4 + mc][:, :256]
        for kc in range(16):
            k.mm(ps, mn[:, kc, mc * 128:(mc + 1) * 128], wk[:, kc, 256:512], start=(kc == 0), stop=(kc == 15))
        for h in range(2):
            k.cp("dve", vmem[:, mc, h, 0:128], ps[:, h * 128:(h + 1) * 128])
    k.pop()


def stage_memattn(k, Bk, projT, qchunk0, gchunk0, qgain, kmemT, vmem, ones, epsc, ident, out, orow0):
    k.push()
    qT = k.sb("mqT", [128, T], BF16)
    gT = k.sb("mgT", [128, T], BF16)
    qnF = k.sb("mqnF", [128, T], BF16)
    sq2 = [k.sb("msq%d" % i, [128, 512], BF16) for i in range(2)]
    rs2 = [k.sb("mrs%d" % i, [128, 512]) for i in range(2)]
    rstd2 = [k.sb("mrstd%d" % i, [128, 512]) for i in range(2)]
    pT = [k.sb("mpT%d" % i, [128, 2, 512], BF16) for i in range(2)]
    rc = [k.sb("mrc%d" % i, [128, 1]) for i in range(2)]
    on = [k.sb("mon%d" % i, [128, 128], BF16) for i in range(2)]
    mixst = k.sb("mmix", [128, T], BF16)
    for h in range(2):
        k.dma(qT, projT[(qchunk0 + h) * 128:(qchunk0 + h + 1) * 128, :])
        k.dma(gT, projT[(gchunk0 + h) * 128:(gchunk0 + h + 1) * 128, :])
        k.act(gT, gT, AF.Silu)
        for blk in range(T // 512):
            t0 = blk * 512
            b2 = blk % 2
            rms_fm(k, Bk[0] if b2 == 0 else Bk[7], ones, epsc, qT[:, t0:t0 + 512], qnF[:, t0:t0 + 512], qgain, 512, sq2[b2], rs2[b2], rstd2[b2], 128)
        for blk in range(T // 512):
            t0 = blk * 512
            qn = qnF[:, t0:t0 + 512]
            p = pT[blk % 2]
            for mc in range(2):
                k.mm(Bk[1 + mc], kmemT[:, h, mc * 128:(mc + 1) * 128], qn)
                k.act(p[:, mc, :], Bk[1 + mc], AF.Exp)
            for s4 in range(4):
                po = Bk[3 + s4 % 2][:, :129]
                for mc in range(2):
                    k.mm(po, p[:, mc, s4 * 128:(s4 + 1) * 128], vmem[:, mc, h, :], start=(mc == 0), stop=(mc == 1))
                r_ = rc[s4 % 2]
                o_ = on[s4 % 2]
                k.recip(r_, po[:, 128:129])
                k.ts("dve", o_, po[:, 0:128], r_, ALU.mult)
                pt = Bk[5 + s4 % 2].bitcast(BF16)[:, :128]
                k.tr(pt, o_, ident)
                tok = t0 + s4 * 128
                k.tt("dve", mixst[:, tok:tok + 128], pt, gT[:, tok:tok + 128], ALU.mult)
        k.dma(out[orow0 + h * 128:orow0 + (h + 1) * 128, :], mixst)
    k.pop()


def stage_na(k, Bk, projT, vna, nab, nam, qgain, kgain, ones, epsc, ident, out):
    k.push()
    qT = k.sb("nqT", [128, T], BF16)
    kT = k.sb("nkT", [128, T], BF16)
    gT = k.sb("ngT", [128, T], BF16)
    qn = k.sb("nqn", [128, T], BF16)
    kn = k.sb("nkn", [128, T], BF16)
    vt = k.sb("nvt", [128, 32, 129], BF16)
    vs = k.sb("nvs", [128, 31, 129], BF16)
    bt = k.sb("nbt", [128, 8, 256])
    nm = k.sb("nnm", [128, 256])
    sq4 = [k.sb("nsq%d" % i, [128, 512], BF16) for i in range(4)]
    rs4 = [k.sb("nrs%d" % i, [128, 512]) for i in range(4)]
    rstd4 = [k.sb("nrstd%d" % i, [128, 512]) for i in range(4)]
    sT = [k.sb("nsT%d" % i, [128, 256]) for i in range(2)]
    pT = [k.sb("npT%d" % i, [128, 256], BF16) for i in range(2)]
    rc = [k.sb("nrc%d" % i, [64, 1]) for i in range(2)]
    on = [k.sb("non%d" % i, [64, 128], BF16) for i in range(2)]
    mixst = k.sb("nmix", [128, T], BF16)
    k.dma(nm, nam)
    k.memset("pool", vt, 1.0)
    k.memset("pool", vs, 1.0)
    for h in range(3):
        k.dma(qT, projT[(0 + h) * 128:(1 + h) * 128, :])
        k.dma(kT, projT[(3 + h) * 128:(4 + h) * 128, :])
        k.dma(gT, projT[(6 + h) * 128:(7 + h) * 128, :])
        k.dma(vt[:, :, 0:128], vna.re("(tl p) c -> p tl c", p=128)[:, :, h * 128:(h + 1) * 128])
        k.dma(vs[:, :, 0:128], vna[64:64 + 31 * 128, :].re("(tl p) c -> p tl c", p=128)[:, :, h * 128:(h + 1) * 128])
        k.dma(bt, nab[h].re("t p f -> p t f"))
        for t in range(8):
            k.tt("pool", bt[:, t, :], bt[:, t, :], nm, ALU.add)
        k.act(gT, gT, AF.Silu)
        for blk in range(T // 512):
            t0 = blk * 512
            b2 = blk % 2
            rms_fm(k, Bk[0 + 2 * b2], ones, epsc, qT[:, t0:t0 + 512], qn[:, t0:t0 + 512], qgain, 512, sq4[b2], rs4[b2], rstd4[b2], 128)
            rms_fm(k, Bk[1 + 2 * b2], ones, epsc, kT[:, t0:t0 + 512], kn[:, t0:t0 + 512], kgain, 512, sq4[2 + b2], rs4[2 + b2], rstd4[2 + b2], 128)
        def s1(r):
            r0 = min(max(r - 4, 0), 56)
            ty = r if r < 4 else (4 if r <= 60 else r - 56)
            ps = Bk[2 + r % 2][:, :256]
            for c in range(4):
                kt0 = (r0 + 2 * c) * 64
                k.mm(ps[:, c * 64:(c + 1) * 64], kn[:, kt0:kt0 + 128], qn[:, r * 64:(r + 1) * 64])
            k.tt("dve", sT[r % 2], ps, bt[:, ty, :], ALU.add)
            k.act(pT[r % 2], sT[r % 2], AF.Exp)

        def s2(r):
            r0 = min(max(r - 4, 0), 56)
            po = Bk[4 + r % 2][:64, :129]
            for c in range(4):
                row0 = r0 + 2 * c
                vv = vt[:, row0 // 2, :] if row0 % 2 == 0 else vs[:, (row0 - 1) // 2, :]
                k.mm(po, pT[r % 2][:, c * 64:(c + 1) * 64], vv, start=(c == 0), stop=(c == 3))
            k.recip(rc[r % 2], po[:, 128:129])
            k.ts("dve", on[r % 2], po[:, 0:128], rc[r % 2], ALU.mult)

        def s3(r):
            pt = Bk[6 + r % 2].bitcast(BF16)[:, :64]
            k.tr(pt, on[r % 2], ident[:64, :64])
            k.tt("dve", mixst[:, r * 64:(r + 1) * 64], pt, gT[:, r * 64:(r + 1) * 64], ALU.mult)

        for i in range(64 + 2):
            if i < 64:
                s1(i)
            if 0 <= i - 1 < 64:
                s2(i - 1)
            if 0 <= i - 2 < 64:
                s3(i - 2)
        k.dma(out[h * 128:(h + 1) * 128, :], mixst)
    k.pop()


def consts(k):
    ones = k.sb("ones", [128, 128], BF16)
    k.memset("dve", ones, 1.0)
    epsc = k.sb("epsc", [128, 1])
    k.memset("dve", epsc, EPS)
    identf = k.sb("identf", [128, 128])
    ident = k.sb("ident", [128, 128], BF16)
    k.memset("pool", identf, 0.0)
    k.emit("pool", lambda e: e.affine_select(out=identf.ap, in_=identf.ap, pattern=[[-1, 128]], base=0,
                                              channel_multiplier=1, compare_op=ALU.not_equal, fill=1.0),
           w=[identf], r=[identf])
    k.cp("pool", ident, identf)
    return ones, epsc, ident, identf


def build_p1(stages=("A", "M", "NA", "ME", "DN")):
    nc = bass.Bass("TRN2", target_bir_lowering=False)
    k = K(nc)
    xT = k.dram("xT", [D, T], F32, "ExternalInput")
    memT = k.dram("memT", [D, 256], F32, "ExternalInput")
    win = k.dram("win", [D, NFM1 * 128 + TM1], F32, "ExternalInput")
    wkv = k.dram("wkv", [D, 512], F32, "ExternalInput")
    lng = k.dram("lng", [128, 16], F32, "ExternalInput")
    memg = k.dram("memg", [128, 16], F32, "ExternalInput")
    hv = k.dram("hv", [128, 8], F32, "ExternalInput")
    nab = k.dram("nab", [3, 8, 128, 256], F32, "ExternalInput")
    nam = k.dram("nam", [128, 256], F32, "ExternalInput")
    dnrow = k.dram("dnrow", [128, 2, 2, 32, 3], F32, "ExternalInput")
    convw = k.dram("convw", [128, 9, 5], F32, "ExternalInput")
    onrow = k.dram("onrow", [128, 128], F32, "ExternalInput")
    out = k.dram("mixT", [1024, T], BF16, "ExternalOutput")
    projT = k.dram("projT", [NFM1 * 128, T], BF16)
    vna = k.dram("vna", [T, 384], BF16)
    ba = k.dram("ba", [128, 32 * 12], F32)
    Bk = [k.ps("bk%d" % i, [128, 512], F32) for i in range(8)]
    ones, epsc, ident, identf = consts(k)
    hvs = k.sb("hvs", [128, 8])
    k.dma(hvs, hv)
    k.ts("dve", hvs[:, 4:5], hvs[:, 0:1], 128.0 ** -0.5, ALU.mult)
    k.ts("dve", hvs[:, 5:6], hvs[:, 2:3], 128.0 ** -0.5, ALU.mult)
    kmemT = k.sb("kmemT", [128, 2, 256], BF16)
    vmem = k.sb("vmem", [128, 2, 2, 129], BF16)
    if "A" in stages:
        stage_proj(k, Bk, xT, win, lng, NFM1 * 128, TM1, projT,
                   [(vna, 0, 384, BF16, "rows"), (ba, 384, 12, F32, "ptile")], ones, epsc)
    if "M" in stages:
        stage_memkv(k, Bk, memT, wkv, memg, hvs[:, 3:4], ones, epsc, kmemT, vmem)
    k.barrier()
    if "NA" in stages:
        stage_na(k, Bk, projT, vna, nab, nam, hvs[:, 4:5], hvs[:, 1:2], ones, epsc, ident, out)
    if "ME" in stages:
        stage_memattn(k, Bk, projT, 21, 23, hvs[:, 5:6], kmemT, vmem, ones, epsc, ident, out, 768)
    if "DN" in stages:
        stage_dn(k, Bk, projT, ba, dnrow, convw, onrow, ident, ones, epsc, out)
    k.barrier()
    return nc


def na_tables(rpb_g):
    p = np.arange(128)
    c = np.arange(4)
    q = np.arange(64)
    o = (2 * c[None, :] + (p[:, None] // 64))
    kc = p % 64
    dcm = np.clip(kc[:, None] - q[None, :], -15, 15) + 15
    c0 = np.clip(q - 8, 0, 48)
    ok = (kc[:, None] >= c0[None, :]) & (kc[:, None] < c0[None, :] + 16)
    tabs = np.zeros((3, 8, 128, 4, 64), np.float32)
    rows = [0, 1, 2, 3, 30, 61, 62, 63]
    for ti, r in enumerate(rows):
        r0 = min(max(r - 4, 0), 56)
        dr = r0 + o - r + 7
        tabs[:, ti] = rpb_g[:, dr[:, :, None], dcm[:, None, :]]
    negm = np.where(ok, 0.0, -30000.0).astype(np.float32)
    negm = np.broadcast_to(negm[:, None, :], (128, 4, 64)).reshape(128, 256)
    return tabs.reshape(3, 8, 128, 256), np.ascontiguousarray(negm)


def prep_p1(inp, b, g):
    f = np.float32
    x = inp["x"][b]
    w = inp["e_w_in"][0]
    NAW = 768
    def hcols(base, h0, nh, hd=128):
        return list(range(base + h0 * hd, base + (h0 + nh) * hd))
    na_q, na_k, na_v, na_g = 0, NAW, 2 * NAW, 3 * NAW
    dn0 = 4 * NAW
    dn_q, dn_k, dn_v, dn_g = dn0, dn0 + 768, dn0 + 1536, dn0 + 2304
    ba0 = dn0 + 3072
    mq0 = ba0 + 24
    mg0 = mq0 + 512
    cols = []
    cols += hcols(na_q, 3 * g, 3) + hcols(na_k, 3 * g, 3) + hcols(na_g, 3 * g, 3)
    cols += hcols(dn_q, 3 * g, 3) + hcols(dn_k, 3 * g, 3) + hcols(dn_v, 3 * g, 3) + hcols(dn_g, 3 * g, 3)
    cols += hcols(mq0, 2 * g, 2) + hcols(mg0, 2 * g, 2)
    cols += hcols(na_v, 3 * g, 3)
    for j in range(4):
        cols += [ba0 + 6 * j + 3 * g + i for i in range(3)]
    cols = np.asarray(cols)
    assert len(cols) == NFM1 * 128 + TM1
    wkv = inp["e_w_mem_kv"][0]
    kvc = hcols(0, 2 * g, 2) + hcols(512, 2 * g, 2)
    hv = np.zeros((128, 8), f)
    hv[:, 0] = inp["na_q_norm"][0]
    hv[:, 1] = inp["na_k_norm"][0]
    hv[:, 2] = inp["e_mem_q_norm"][0]
    hv[:, 3] = inp["e_mem_k_norm"][0]
    nab, nam = na_tables(inp["na_rpb"][0][3 * g:3 * g + 3])
    dnrow = np.zeros((128, 2, 2, 32, 3), f)
    for d_ in range(2):
        dnrow[:, 0, d_] = inp["dn_dt_bias"][0][d_, 3 * g:3 * g + 3][None, None, :]
        dnrow[:, 1, d_] = inp["dn_a_log"][0][d_, 3 * g:3 * g + 3][None, None, :]
    cw = inp["dn_conv_w"][0]
    convw = np.zeros((128, 9, 5), f)
    for X in range(3):
        for hh in range(3):
            c0 = X * 768 + (3 * g + hh) * 128
            convw[:, X * 3 + hh, :] = cw[:, c0:c0 + 128].T
    onrow = np.ascontiguousarray(np.broadcast_to(inp["dn_out_norm"][0][None, :], (128, 128))).astype(f)
    return {
        "xT": np.ascontiguousarray(x.T),
        "memT": np.ascontiguousarray(inp["mem"][b].T),
        "win": np.ascontiguousarray(w[:, cols]),
        "wkv": np.ascontiguousarray(wkv[:, kvc]),
        "lng": np.ascontiguousarray(inp["e_ln_g"][0].reshape(16, 128).T),
        "memg": np.ascontiguousarray(inp["mem_norm_g"].reshape(16, 128).T),
        "hv": hv,
        "nab": nab,
        "nam": nam,
        "dnrow": dnrow,
        "convw": convw,
        "onrow": onrow,
    }


T = 4096
NT_ = 32


def tri_consts(k, identb):
    c = {}
    onesf = k.sb("onesf", [128, 128])
    k.memset("pool", onesf, 1.0)
    c["onesf"] = onesf
    def sel(name, src_val, pattern, cm, base, op, fill, dt):
        tmp = k.sb(name + "_f", [128, 128])
        k.memset("pool", tmp, src_val)
        k.emit("pool", lambda e: e.affine_select(out=tmp.ap, in_=tmp.ap, pattern=pattern, base=base,
                                                  channel_multiplier=cm, compare_op=op, fill=fill),
               w=[tmp], r=[tmp])
        if dt == F32:
            return tmp
        o = k.sb(name, [128, 128], dt)
        k.cp("pool", o, tmp)
        return o
    c["U"] = [sel("Uf", 1.0, [[1, 128]], -1, 0, ALU.is_ge, 0.0, F32),
              sel("Ub", 1.0, [[-1, 128]], 1, 0, ALU.is_ge, 0.0, F32)]
    c["NM"] = [sel("NMf", 0.0, [[-1, 128]], 1, 0, ALU.is_ge, -1e30, BF16),
               sel("NMb", 0.0, [[1, 128]], -1, 0, ALU.is_ge, -1e30, BF16)]
    c["ST"] = [sel("STf", 1.0, [[-1, 128]], 1, -1, ALU.is_ge, 0.0, BF16),
               sel("STb", 1.0, [[1, 128]], -1, -1, ALU.is_ge, 0.0, BF16)]
    return c


def dn_gates(k, Bk, ba, dnrow, tc):
    ba_sb = k.sb("ba_sb", [128, 32, 12])
    k.dma(ba_sb, ba.re("p (t c) -> p t c", c=12))
    row = k.sb("dnrow", [128, 2, 2, 32, 3])
    k.dma(row, dnrow)
    one1 = k.sb("one1", [128, 1])
    k.memset("dve", one1, 1.0)
    G = {}
    def arr(n):
        G[n] = k.sb("g_" + n, [128, 2, 32, 3])
        return G[n]
    def v4(a):
        return a.re("p d t h -> p t d h")
    bsrc = ba_sb[:, :, 0:6].re("p t (d h) -> p t d h", d=2)
    asrc = ba_sb[:, :, 6:12].re("p t (d h) -> p t d h", d=2)
    beta = arr("beta")
    k.act(v4(beta), bsrc, AF.Sigmoid)
    z = arr("z")
    k.tt("dve", v4(z), asrc, v4(row[:, 0]), ALU.add)
    k.act(z, z, AF.Exp)
    k.act(z, z, AF.Ln, bias=one1)
    A = arr("A")
    k.act(A, row[:, 1], AF.Exp)
    g = arr("g")
    k.stt(g, z, -1.0, A, ALU.mult, ALU.mult)
    gneg = arr("gneg")
    k.ts("dve", gneg, g, -1.0, ALU.mult)
    nbeta = arr("nbeta")
    k.ts("dve", nbeta, beta, -1.0, ALU.mult)
    Gc = arr("G")
    Gt = arr("Gt")
    for d in range(2):
        k.mm(Bk[6][:, :96], tc["U"][d], g[:, d].re("p t h -> p (t h)"))
        k.cp("dve", Gc[:, d].re("p t h -> p (t h)"), Bk[6][:, :96])
        k.mm(Bk[7][:, :96], tc["onesf"], g[:, d].re("p t h -> p (t h)"))
        k.cp("dve", Gt[:, d].re("p t h -> p (t h)"), Bk[7][:, :96])
    eG = arr("eG")
    k.act(eG, Gc, AF.Exp)
    beG = arr("beG")
    k.tt("dve", beG, beta, eG, ALU.mult)
    ekd = arr("ekd")
    k.tt("dve", ekd, Gt, Gc, ALU.subtract)
    k.act(ekd, ekd, AF.Exp)
    gend = arr("gend")
    k.act(gend, Gt, AF.Exp)
    return G


def dn_prep_head(k, Bk, projT, h, convw, ident, ones, epsc, qn, kn, vf, colq, colk):
    k.push()
    xp = k.sb("xp", [128, T + 4], BF16)
    dg = k.sb("dg", [128, 5, 128], BF16)
    xcF = k.sb("xcF", [128, T])
    sq2 = [k.sb("dsq%d" % i, [128, 512], BF16) for i in range(2)]
    rs2 = [k.sb("drs%d" % i, [128, 512]) for i in range(2)]
    rstd2 = [k.sb("drstd%d" % i, [128, 512]) for i in range(2)]
    k.memset("pool", xp[:, 0:2], 0.0)
    k.memset("pool", xp[:, T + 2:T + 4], 0.0)
    for X, (chunk0, dst) in enumerate(((9, qn), (12, kn), (15, vf))):
        k.dma(xp[:, 2:T + 2], projT[(chunk0 + h) * 128:(chunk0 + h + 1) * 128, :])
        for j in range(5):
            k.ts("dve", dg[:, j, :], ident, convw[:, X * 3 + h, j:j + 1], ALU.mult)
        for blk in range(T // 512):
            t0 = blk * 512
            ps = Bk[6 + blk % 2]
            for j in range(5):
                k.mm(ps, dg[:, j, :], xp[:, t0 + j:t0 + j + 512], start=(j == 0), stop=(j == 4))
            k.act((dst if X == 2 else xcF)[:, t0:t0 + 512], ps, AF.Silu)
        if X == 2:
            continue
        for blk in range(T // 512):
            t0 = blk * 512
            b2 = blk % 2
            x_ = xcF[:, t0:t0 + 512]
            sq, rs, rstd = sq2[b2], rs2[b2], rstd2[b2]
            pb = Bk[5 - b2]
            k.act(sq, x_, AF.Square)
            k.mm(pb, ones, sq)
            k.act(rs, pb, AF.Sqrt, scale=1.0, bias=epsc)
            k.recip(rstd, rs)
            k.stt(dst[:, t0:t0 + 512], x_, colq if X == 0 else colk, rstd, ALU.mult, ALU.mult)
    k.pop()


def dn_chain(k, B, h, d, G, tc, ident, qn, kn, vf, oacc, cidx=0):
    nm = "c%d%d" % (h, d)
    def bt(n, dt=BF16):
        return k.sb(nm + n, [128, 128], dt)
    D, Ds, qkm = bt("D"), bt("Ds"), bt("qkm")
    N = bt("N")
    TQ = k.sb(nm + "TQ", [128, 256], BF16)
    NT, qkT = TQ[:, 0:128], TQ[:, 128:256]
    XX = [k.sb(nm + "XX%d" % i, [128, 256], BF16) for i in range(2)]
    P = [bt("P0"), bt("P1")]
    kbe, kdec, vb, wT, vnew, Sb = bt("kbe"), bt("kdec"), bt("vb"), bt("wT"), bt("vnew"), bt("Sb")
    gnb, u, tmp, S = bt("gnb", F32), bt("u", F32), bt("tmp", F32), bt("S", F32)
    k.memset("pool", S, 0.0)
    k.memset("pool", Sb, 0.0)
    eA = "act" if cidx % 2 == 0 else "dve"
    eB = "dve" if cidx % 2 == 0 else "act"
    s = [B[:, i * 128:(i + 1) * 128] for i in range(4)]
    Bb = B.bitcast(BF16)
    sb_ = [Bb[:, i * 256:i * 256 + 128] for i in range(4)]
    yield
    order = range(NT_) if d == 0 else range(NT_ - 1, -1, -1)
    for n in order:
        tl = slice(n * 128, (n + 1) * 128)
        col = lambda a: a[:, d, n, h:h + 1]
        k.ts("dve", gnb, tc["onesf"], col(G["gneg"]), ALU.mult)
        k.mm(s[0], gnb, tc["U"][d], start=True, stop=False)
        k.mm(s[0], ident, tc["NM"][d], start=False, stop=True)
        k.act(D, s[0], AF.Exp, bias=col(G["G"]))
        k.tt("pool", Ds, D, tc["ST"][d], ALU.mult)
        yield
        k.mm(s[1], kn[:, tl], kn[:, tl])
        k.stt(N, s[1], col(G["nbeta"]), Ds, ALU.mult, ALU.mult)
        k.mm(s[2], qn[:, tl], kn[:, tl])
        k.tt("dve", qkm, s[2], D, ALU.mult)
        yield
        k.tr(sb_[2], N, ident)
        k.tr(Bb[:, 2 * 256 + 128:2 * 256 + 256], qkm, ident)
        k.cp("act", TQ, Bb[:, 2 * 256:2 * 256 + 256])
        yield
        k.tr(sb_[1], kn[:, tl], ident)
        k.tr(sb_[2], vf[:, tl], ident)
        k.ts("dve", kbe, sb_[1], col(G["beG"]), ALU.mult)
        k.ts("dve", kdec, sb_[1], col(G["ekd"]), ALU.mult)
        k.ts("dve", vb, sb_[2], col(G["beta"]), ALU.mult)
        yield
        Xc, XTc = N, NT
        k.tt("pool", P[0], NT, ident, ALU.add)
        Pc = P[0]
        pi = 1
        for l in range(6):
            eng = eA if l % 2 == 0 else eB
            xx = XX[l % 2]
            Xn, XTn = xx[:, 0:128], xx[:, 128:256]
            k.mm(s[0], XTc, Xc)
            if l < 5:
                k.mm(s[1], Xc, XTc)
                k.cp(eng, xx, B[:, 0:256])
            else:
                k.cp(eng, Xn, s[0])
            Xc, XTc = Xn, XTn
            yield
            k.mm(s[2], Xc, Pc, start=True, stop=False)
            k.mm(s[2], ident, Pc, start=False, stop=True)
            Pn = P[pi]
            pi ^= 1
            k.cp(eB if l % 2 == 0 else eA, Pn, s[2])
            Pc = Pn
            yield
        TT = Pc
        k.mm(s[0], TT, vb)
        k.mm(s[1], kbe, TT)
        k.cp("act", u, s[0])
        k.cp("act", wT, s[1])
        yield
        k.mm(s[2], wT, Sb)
        k.tt("dve", vnew, u, s[2], ALU.subtract)
        yield
        k.mm(s[3], qn[:, tl], Sb)
        k.stt(tmp, s[3], col(G["eG"]), oacc[:, n, :], ALU.mult, ALU.add)
        k.mm(s[0], qkT, vnew)
        k.tt("dve", oacc[:, n, :], s[0], tmp, ALU.add)
        k.mm(s[1], kdec, vnew)
        k.stt(S, S, col(G["gend"]), s[1], ALU.mult, ALU.add)
        k.cp("pool", Sb, S)
        yield


def dn_out_head(k, Bk, projT, h, oacc, onrow, ident, epsc, out, orow0):
    k.push()
    gT = k.sb("dgT", [128, T], BF16)
    mixst = k.sb("dmix", [128, T], BF16)
    junk = k.sb("djunk", [128, 128])
    ss = k.sb("dss", [128, 32])
    rs = k.sb("drs2", [128, 32])
    rstd = k.sb("drstd2", [128, 32])
    on = [k.sb("don%d" % i, [128, 128], BF16) for i in range(2)]
    k.dma(gT, projT[(18 + h) * 128:(19 + h) * 128, :])
    k.act(gT, gT, AF.Silu)
    for n in range(NT_):
        k.act(junk, oacc[:, n, :], AF.Square, accum_out=ss[:, n:n + 1])
    k.act(rs, ss, AF.Sqrt, scale=1.0 / 128, bias=epsc)
    k.recip(rstd, rs)
    for n in range(NT_):
        o_ = on[n % 2]
        k.stt(o_, oacc[:, n, :], rstd[:, n:n + 1], onrow, ALU.mult, ALU.mult)
        pt = Bk[6 + n % 2].bitcast(BF16)[:, :128]
        k.tr(pt, o_, ident)
        k.tt("dve", mixst[:, n * 128:(n + 1) * 128], pt, gT[:, n * 128:(n + 1) * 128], ALU.mult)
    k.dma(out[orow0 + h * 128:orow0 + (h + 1) * 128, :], mixst)
    k.pop()


def stage_dn(k, Bk, projT, ba, dnrow, convw_d, onrow_d, ident, ones, epsc, out, heads=(0, 1, 2)):
    k.push()
    tc = tri_consts(k, ident)
    G = dn_gates(k, Bk, ba, dnrow, tc)
    convw = k.sb("convw", [128, 9, 5])
    k.dma(convw, convw_d)
    onrow = k.sb("onrow", [128, 128])
    k.dma(onrow, onrow_d)
    colq = k.sb("colq", [128, 1])
    colk = k.sb("colk", [128, 1])
    k.memset("dve", colq, 128.0 ** -0.5)
    k.memset("dve", colk, 1.0)
    nh = len(heads)
    qn = [k.sb("dqn%d" % h, [128, T], BF16) for h in range(nh)]
    kn = [k.sb("dkn%d" % h, [128, T], BF16) for h in range(nh)]
    vf = [k.sb("dvf%d" % h, [128, T], BF16) for h in range(nh)]
    oacc = [k.sb("doacc%d" % h, [128, 32, 128]) for h in range(nh)]
    for o_ in oacc:
        k.memset("pool", o_, 0.0)
    for i, h in enumerate(heads):
        dn_prep_head(k, Bk, projT, h, convw, ident, ones, epsc, qn[i], kn[i], vf[i], colq, colk)
    k.barrier()
    k.push()
    chains = []
    for i, h in enumerate(heads):
        for d in range(2):
            chains.append(dn_chain(k, Bk[len(chains)], h, d, G, tc, ident, qn[i], kn[i], vf[i], oacc[i], cidx=len(chains)))
    active = list(chains)
    while active:
        nxt = []
        for c in active:
            try:
                next(c)
                nxt.append(c)
            except StopIteration:
                pass
        active = nxt
    k.pop()
    for i, h in enumerate(heads):
        dn_out_head(k, Bk, projT, h, oacc[i], onrow, ident, epsc, out, 384)
    k.pop()


import math
T = 4096
D = 2048
EPS = 1e-6
NFM3 = 22
LAM_INIT = 0.8 - 0.6 * math.exp(-0.3 * 1)


def build_p2():
    nc = bass.Bass("TRN2", target_bir_lowering=False)
    k = K(nc)
    TT = 2048
    mixT = k.dram("mixT", [D, TT], BF16, "ExternalInput")
    xT = k.dram("xT", [D, TT], F32, "ExternalInput")
    wout = k.dram("wout", [D, D], F32, "ExternalInput")
    yT = k.dram("yT", [D, TT], F32, "ExternalOutput")
    Bk = [k.ps("bk%d" % i, [128, 512], F32) for i in range(8)]
    Wo = k.sb("Wo", [128, 16, D], BF16)
    for kc in range(16):
        k.dma(Wo[:, kc, :], wout[kc * 128:(kc + 1) * 128, :], eng="pool")
    mb = [k.sb("mb%d" % i, [128, 16, 512], BF16) for i in range(2)]
    xb = [k.sb("xb%d" % i, [128, 16, 512], F32) for i in range(2)]
    st = [k.sb("st%d" % i, [128, 512], F32) for i in range(4)]
    mv = mixT.re("(kc p) t -> p kc t", p=128)
    xv = xT.re("(kc p) t -> p kc t", p=128)
    for tb in range(TT // 512):
        t0 = tb * 512
        m_, x_ = mb[tb % 2], xb[tb % 2]
        for q4 in range(4):
            k.dma(m_[:, q4 * 4:(q4 + 1) * 4, :], mv[:, q4 * 4:(q4 + 1) * 4, t0:t0 + 512])
            k.dma(x_[:, q4 * 4:(q4 + 1) * 4, :], xv[:, q4 * 4:(q4 + 1) * 4, t0:t0 + 512])
        for dc in range(16):
            ps = Bk[dc % 4]
            for kc in range(16):
                k.mm(ps, Wo[:, kc, dc * 128:(dc + 1) * 128], m_[:, kc, :], start=(kc == 0), stop=(kc == 15), sig=(kc == 15))
            s_ = st[dc % 4]
            k.tt("dve", s_, ps, x_[:, dc, :], ALU.add)
            k.dma(yT[dc * 128:(dc + 1) * 128, t0:t0 + 512], s_)
    k.barrier()
    return nc


def mix_perm():
    idx = []
    for g in range(2):
        for h in range(3):
            idx += list(range((3 * g + h) * 128, (3 * g + h + 1) * 128))
        for h in range(3):
            idx += list(range(768 + (3 * g + h) * 128, 768 + (3 * g + h + 1) * 128))
        for h in range(2):
            idx += list(range(1536 + (2 * g + h) * 128, 1536 + (2 * g + h + 1) * 128))
    return np.asarray(idx)


def mix_perm_odd():
    idx = []
    for g in range(2):
        for h in range(3):
            idx += list(range((3 * g + h) * 256, (3 * g + h + 1) * 256))
        for h in range(2):
            idx += list(range(1536 + (2 * g + h) * 128, 1536 + (2 * g + h + 1) * 128))
    return np.asarray(idx)


def rope_tables(k):
    C = k.sb("ropeC", [128, T])
    S = k.sb("ropeS", [128, T])
    k.push()
    pos = k.sb("pos", [128, T])
    k.emit("pool", lambda e: e.iota(pos.ap, [[1, T]], channel_multiplier=0, allow_small_or_imprecise_dtypes=True),
           w=[pos], r=[])
    pidx = k.sb("pidx", [128, 1])
    for half in range(2):
        v = pidx[half * 64:(half + 1) * 64, :]
        k.emit("pool", lambda e, v=v: e.iota(v.ap, [[0, 1]], channel_multiplier=1, allow_small_or_imprecise_dtypes=True),
               w=[v], r=[])
    inv = k.sb("inv", [128, 1])
    k.act(inv, pidx, AF.Exp, scale=-math.log(10000.0) / 64.0)
    ang = k.sb("ang", [128, T])
    k.ts("dve", ang, pos, inv, ALU.mult)
    y = k.sb("ropey", [128, T])
    ni = k.sb("ropeni", [128, T], mybir.dt.int32)
    npi = k.sb("npi", [128, 1])
    k.memset("dve", npi, -math.pi)
    c1 = 6.28125
    c2 = 2.0 * math.pi - c1
    for tab, shift in ((S, 0.0), (C, math.pi / 2)):
        k.ts("dve", y, ang, shift, ALU.add, 1.0 / (2 * math.pi), ALU.mult)
        k.cp("dve", ni, y)
        k.cp("dve", y, ni)
        k.stt(tab, y, -c1, ang, ALU.mult, ALU.add)
        k.stt(tab, y, -c2, tab, ALU.mult, ALU.add)
        if shift:
            k.ts("dve", tab, tab, shift, ALU.add)
        k.ts("dve", y, tab, math.pi, ALU.is_gt)
        k.stt(tab, y, -2 * math.pi, tab, ALU.mult, ALU.add)
        k.ts("dve", y, tab, -math.pi, ALU.is_lt)
        k.stt(tab, y, 2 * math.pi, tab, ALU.mult, ALU.add)
        k.act(tab, tab, AF.Sin)
    k.ts("dve", S[0:64, :], S[0:64, :], -1.0, ALU.mult)
    k.pop()
    return C, S


def stage_diff(k, Bk, projT, vdf, hvs, lamrep_d, subrow_d, ones, epsc, ident, identf, out, heads=(0, 1, 2)):
    k.push()
    C, S = rope_tables(k)
    swp = k.sb("swp", [128, 128], BF16)
    k.cp("pool", swp[:, 0:64], ident[:, 64:128])
    k.cp("pool", swp[:, 64:128], ident[:, 0:64])
    lamrep = k.sb("lamrep", [128, 4, 128])
    k.dma(lamrep, lamrep_d)
    lt = k.sb("lt", [128, 2, 128])
    le = k.sb("le", [128, 2])
    k.tt("dve", lt[:, 0, :], lamrep[:, 0, :], lamrep[:, 1, :], ALU.mult)
    k.tt("dve", lt[:, 1, :], lamrep[:, 2, :], lamrep[:, 3, :], ALU.mult)
    k.emit("dve", lambda e: e.tensor_reduce(out=le.ap, in_=lt.ap, axis=AX.X, op=ALU.add), w=[le], r=[lt])
    k.act(le, le, AF.Exp)
    nlam = k.sb("nlam", [128, 1])
    k.tt("dve", nlam, le[:, 1:2], le[:, 0:1], ALU.subtract)
    k.ts("dve", nlam, nlam, -LAM_INIT, ALU.add)
    subrow = k.sb("subrow", [128, 256])
    k.dma(subrow, subrow_d)
    k.ts("dve", subrow, subrow, 1.0 - LAM_INIT, ALU.mult)
    raw = k.sb("fraw", [128, T], BF16)
    qk = [k.sb("fqk%d" % i, [128, T], BF16) for i in range(4)]
    va = k.sb("fva", [128, 32, 257], BF16)
    gT = [k.sb("fg%d" % i, [128, T], BF16) for i in range(2)]
    mixst = [k.sb("fmix%d" % i, [128, T], BF16) for i in range(2)]
    sq2 = [k.sb("fsq%d" % i, [128, 512], BF16) for i in range(2)]
    rs2 = [k.sb("frs%d" % i, [128, 512]) for i in range(2)]
    rstd2 = [k.sb("frstd%d" % i, [128, 512]) for i in range(2)]
    xn2 = [k.sb("fxn%d" % i, [128, 512], BF16) for i in range(2)]
    t12 = [k.sb("ft1%d" % i, [128, 512]) for i in range(2)]
    t22 = [k.sb("ft2%d" % i, [128, 512]) for i in range(2)]
    pT = [k.sb("fpT%d" % i, [128, 512], BF16) for i in range(4)]
    accs = [[k.sb("facc%d%d" % (m, s_), [128, 257]) for s_ in range(2)] for m in range(2)]
    rc = k.sb("frc", [128, 4])
    o1 = k.sb("fo1", [128, 256])
    o = k.sb("fo", [128, 256])
    junk = k.sb("fjunk", [128, 256])
    ss = k.sb("fss", [128, 2])
    on = [k.sb("fon%d" % i, [128, 256], BF16) for i in range(2)]
    k.memset("pool", va, 1.0)
    mhalf = k.sb("mhalf", [128, 1])
    k.memset("pool", mhalf, -0.5)
    for h in heads:
        for i in range(4):
            chunk = (0 if i < 2 else 6) + 2 * h + (i % 2)
            gain = hvs[:, 4:5] if i < 2 else hvs[:, 1:2]
            k.dma(raw, projT[chunk * 128:(chunk + 1) * 128, :])
            for blk in range(T // 512):
                t0 = blk * 512
                b2 = blk % 2
                xn, t1, t2 = xn2[b2], t12[b2], t22[b2]
                rms_fm(k, Bk[6 - 2 * b2], ones, epsc, raw[:, t0:t0 + 512], xn, gain, 512, sq2[b2], rs2[b2], rstd2[b2], 128)
                k.mm(Bk[7 - 2 * b2], swp, xn)
                k.tt("pool", t1, xn, C[:, t0:t0 + 512], ALU.mult)
                k.tt("dve", t2, Bk[7 - 2 * b2], S[:, t0:t0 + 512], ALU.mult)
                k.tt("dve", qk[i][:, t0:t0 + 512], t1, t2, ALU.add)
        k.dma(va[:, :, 0:256], vdf.re("(tl p) c -> p tl c", p=128)[:, :, h * 256:(h + 1) * 256])
        for e in range(2):
            k.dma(gT[e], projT[(12 + 2 * h + e) * 128:(13 + 2 * h + e) * 128, :])
            k.act(gT[e], gT[e], AF.Silu)
        def epilogue(q0):
            for s_ in range(2):
                a0, a1 = accs[0][s_], accs[1][s_]
                k.recip(rc[:, 0:1], a0[:, 256:257])
                yield
                k.ts("dve", o1, a0[:, 0:256], rc[:, 0:1], ALU.mult)
                yield
                k.recip(rc[:, 1:2], a1[:, 256:257])
                k.tt("dve", rc[:, 2:3], rc[:, 1:2], nlam, ALU.mult)
                yield
                k.stt(o, a1[:, 0:256], rc[:, 2:3], o1, ALU.mult, ALU.add)
                yield
                k.tt("pool", junk, o, o, ALU.mult)
                yield
                k.emit("dve", lambda e_: e_.tensor_reduce(out=ss[:, 0:1].ap, in_=junk.ap, axis=AX.X, op=ALU.add),
                       w=[ss], r=[junk])
                k.ts("dve", ss[:, 1:2], ss[:, 0:1], 1.0 / 256, ALU.mult, EPS, ALU.add)
                yield
                yield
                k.tt("pool", rc[:, 3:4], ss[:, 1:2], mhalf, ALU.pow)
                yield
                o_ = on[s_]
                k.stt(o_, o, rc[:, 3:4], subrow, ALU.mult, ALU.mult)
                yield
                yield
                tok = q0 + s_ * 128
                for e in range(2):
                    pt = Bk[7].bitcast(BF16)[:, e * 128:(e + 1) * 128]
                    k.tr(pt, o_[:, e * 128:(e + 1) * 128], ident)
                    yield
                    k.tt("dve", mixst[e][:, tok:tok + 128], pt, gT[e][:, tok:tok + 128], ALU.mult)
                    yield

        pend = iter(())
        for qb in range(T // 256):
            q0 = qb * 256
            acc = [[Bk[2 + 2 * m + s_][:, :257] for s_ in range(2)] for m in range(2)]
            stb = [Bk[0], Bk[1], Bk[6]]

            def qk_(kc):
                ps = stb[kc % 3]
                for m in range(2):
                    k.mm(ps[:, m * 256:(m + 1) * 256], qk[2 + m][:, kc * 128:(kc + 1) * 128], qk[m][:, q0:q0 + 256])

            def pv_(kc):
                p_ = pT[kc % 4]
                for m in range(2):
                    for s_ in range(2):
                        k.mm(acc[m][s_], p_[:, m * 256 + s_ * 128:m * 256 + (s_ + 1) * 128], va[:, kc, :],
                             start=(kc == 0), stop=(kc == 31), sig=(kc == 31 or True))
            qk_(0)
            qk_(1)
            for kc in range(32):
                k.act(pT[kc % 4], stb[kc % 3], AF.Exp)
                if kc + 2 < 32:
                    qk_(kc + 2)
                pv_(kc)
                next(pend, None)
            for m in range(2):
                for s_ in range(2):
                    k.cp("dve", accs[m][s_], acc[m][s_])
            pend = epilogue(q0)
        for _ in pend:
            pass
        for e in range(2):
            k.dma(out[(2 * h + e) * 128:(2 * h + e + 1) * 128, :], mixst[e])
    k.pop()


def build_p3(stages=("A", "M", "DF", "ME")):
    nc = bass.Bass("TRN2", target_bir_lowering=False)
    k = K(nc)
    xT = k.dram("xT", [D, T], F32, "ExternalInput")
    memT = k.dram("memT", [D, 256], F32, "ExternalInput")
    win = k.dram("win", [D, NFM3 * 128 + 768], F32, "ExternalInput")
    wkv = k.dram("wkv", [D, 512], F32, "ExternalInput")
    lng = k.dram("lng", [128, 16], F32, "ExternalInput")
    memg = k.dram("memg", [128, 16], F32, "ExternalInput")
    hv = k.dram("hv", [128, 8], F32, "ExternalInput")
    lamrep = k.dram("lamrep", [128, 4, 128], F32, "ExternalInput")
    subrow = k.dram("subrow", [128, 256], F32, "ExternalInput")
    out = k.dram("mixT", [1024, T], BF16, "ExternalOutput")
    projT = k.dram("projT", [NFM3 * 128, T], BF16)
    vdf = k.dram("vdf", [T, 768], BF16)
    Bk = [k.ps("bk%d" % i, [128, 512], F32) for i in range(8)]
    ones, epsc, ident, identf = consts(k)
    hvs = k.sb("hvs", [128, 8])
    k.dma(hvs, hv)
    k.ts("dve", hvs[:, 4:5], hvs[:, 0:1], 128.0 ** -0.5, ALU.mult)
    k.ts("dve", hvs[:, 5:6], hvs[:, 2:3], 128.0 ** -0.5, ALU.mult)
    kmemT = k.sb("kmemT", [128, 2, 256], BF16)
    vmem = k.sb("vmem", [128, 2, 2, 129], BF16)
    if "A" in stages:
        stage_proj(k, Bk, xT, win, lng, NFM3 * 128, 768, projT,
                      [(vdf, 0, 384, BF16, "rows"), (vdf, 384, 384, BF16, "rows")], ones, epsc)
    if "M" in stages:
        stage_memkv(k, Bk, memT, wkv, memg, hvs[:, 3:4], ones, epsc, kmemT, vmem)
    k.barrier()
    if "DF" in stages:
        stage_diff(k, Bk, projT, vdf, hvs, lamrep, subrow, ones, epsc, ident, identf, out)
    if "ME" in stages:
        stage_memattn(k, Bk, projT, 18, 20, hvs[:, 5:6], kmemT, vmem, ones, epsc, ident, out, 768)
    k.barrier()
    return nc


def prep_p3(inp, x1T_b, b, g):
    f = np.float32
    w = inp["o_w_in"][0]
    DW = 1536
    cols = []
    for base in (0, DW):
        for h in range(3):
            cols += list(range(base + (3 * g + h) * 256, base + (3 * g + h + 1) * 256))
    for h in range(3):
        cols += list(range(3 * DW + (3 * g + h) * 256, 3 * DW + (3 * g + h + 1) * 256))
    mq0 = 4 * DW
    mg0 = mq0 + 512
    cols += list(range(mq0 + 2 * g * 128, mq0 + (2 * g + 2) * 128))
    cols += list(range(mg0 + 2 * g * 128, mg0 + (2 * g + 2) * 128))
    for h in range(3):
        cols += list(range(2 * DW + (3 * g + h) * 256, 2 * DW + (3 * g + h + 1) * 256))
    cols = np.asarray(cols)
    assert len(cols) == NFM3 * 128 + 768
    wkv = inp["o_w_mem_kv"][0]
    kvc = list(range(2 * g * 128, (2 * g + 2) * 128)) + list(range(512 + 2 * g * 128, 512 + (2 * g + 2) * 128))
    hv = np.zeros((128, 8), f)
    hv[:, 0] = inp["df_q_norm"][0]
    hv[:, 1] = inp["df_k_norm"][0]
    hv[:, 2] = inp["o_mem_q_norm"][0]
    hv[:, 3] = inp["o_mem_k_norm"][0]
    return {
        "xT": x1T_b,
        "memT": np.ascontiguousarray(inp["mem"][b].T),
        "win": np.ascontiguousarray(w[:, cols]),
        "wkv": np.ascontiguousarray(wkv[:, kvc]),
        "lng": np.ascontiguousarray(inp["o_ln_g"][0].reshape(16, 128).T),
        "memg": np.ascontiguousarray(inp["mem_norm_g"].reshape(16, 128).T),
        "hv": hv,
        "lamrep": np.ascontiguousarray(np.broadcast_to(inp["df_lambda"][0][None], (128, 4, 128))).astype(f),
        "subrow": np.ascontiguousarray(np.broadcast_to(inp["df_subln"][0][None], (128, 256))).astype(f),
    }


T = 4096
D = 2048
PAIRS = [[0, 1], [2, 3], [4, 5], [6, 7]]


def stage_wout(k, Bk, mixall, xsrc, wout, ydst, ntok=T):
    k.push()
    Wo = k.sb("Wo", [128, 16, D], BF16)
    for kc in range(16):
        k.dma(Wo[:, kc, :], wout[kc * 128:(kc + 1) * 128, :], eng="pool")
    mb = [k.sb("mb%d" % i, [128, 16, 512], BF16) for i in range(2)]
    xb = [k.sb("xwb%d" % i, [128, 16, 512], F32) for i in range(2)]
    st = [k.sb("wst%d" % i, [128, 512], F32) for i in range(4)]
    mv = mixall.re("(kc p) t -> p kc t", p=128)
    xv = xsrc.re("(kc p) t -> p kc t", p=128)
    def load(tb):
        t0 = tb * 512
        m_, x_ = mb[tb % 2], xb[tb % 2]
        for q4 in range(4):
            k.dma(m_[:, q4 * 4:(q4 + 1) * 4, :], mv[:, q4 * 4:(q4 + 1) * 4, t0:t0 + 512])
            k.dma(x_[:, q4 * 4:(q4 + 1) * 4, :], xv[:, q4 * 4:(q4 + 1) * 4, t0:t0 + 512])

    nb = ntok // 512
    load(0)
    for tb in range(nb):
        t0 = tb * 512
        m_, x_ = mb[tb % 2], xb[tb % 2]
        if tb + 1 < nb:
            load(tb + 1)
        for dc in range(16):
            ps = Bk[dc % 4]
            for kc in range(16):
                k.mm(ps, Wo[:, kc, dc * 128:(dc + 1) * 128], m_[:, kc, :], start=(kc == 0), stop=(kc == 15), sig=(kc == 15))
            s_ = st[dc % 4]
            k.tt("dve", s_, ps, x_[:, dc, :], ALU.add)
            k.dma(ydst[dc * 128:(dc + 1) * 128, t0:t0 + 512], s_)
    k.pop()


def build_fused():
    nc = bass.Bass("TRN2", target_bir_lowering=False)
    k = K(nc)
    EI = "ExternalInput"
    xT = k.dram("xT", [D, T], F32, EI)
    memT = k.dram("memT", [D, 256], F32, EI)
    memg = k.dram("memg", [128, 16], F32, EI)
    win0 = k.dram("win0", [D, NFM1 * 128 + TM1], F32, EI)
    wkv0 = k.dram("wkv0", [D, 512], F32, EI)
    lng0 = k.dram("lng0", [128, 16], F32, EI)
    hv0 = k.dram("hv0", [128, 8], F32, EI)
    nab = k.dram("nab", [3, 8, 128, 256], F32, EI)
    nam = k.dram("nam", [128, 256], F32, EI)
    dnrow = k.dram("dnrow", [128, 2, 2, 32, 3], F32, EI)
    convw = k.dram("convw", [128, 9, 5], F32, EI)
    onrow = k.dram("onrow", [128, 128], F32, EI)
    wout0 = k.dram("wout0", [D, D], F32, EI)
    win1 = k.dram("win1", [D, NFM3 * 128 + 768], F32, EI)
    wkv1 = k.dram("wkv1", [D, 512], F32, EI)
    lng1 = k.dram("lng1", [128, 16], F32, EI)
    hv1 = k.dram("hv1", [128, 8], F32, EI)
    lamrep = k.dram("lamrep", [128, 4, 128], F32, EI)
    subrow = k.dram("subrow", [128, 256], F32, EI)
    wout1 = k.dram("wout1", [D, D], F32, EI)
    yT = k.dram("yT", [D, T], F32, "ExternalOutput")
    projT = k.dram("projT", [NFM1 * 128, T], BF16)
    vna = k.dram("vna", [T, 384], BF16)
    ba = k.dram("ba", [128, 32 * 12], F32)
    vdf = k.dram("vdf", [T, 768], BF16)
    mixloc_f = k.dram("mixloc", [1024, T // 2], F32)
    mixall_f = k.dram("mixall", [2048, T // 2], F32)
    mixloc = mixloc_f.bitcast(BF16)
    mixall = mixall_f.bitcast(BF16)

    def exchange():
        for j in range(4):
            k.allgather(mixall_f[j * 512:(j + 1) * 512, :], mixloc_f[j * 256:(j + 1) * 256, :], PAIRS)
    x1T = k.dram("x1T", [D, T], F32)
    Bk = [k.ps("bk%d" % i, [128, 512], F32) for i in range(8)]
    ones, epsc, ident, identf = consts(k)

    def hvload(hv):
        hvs = k.sb("hvs", [128, 8])
        k.dma(hvs, hv)
        k.ts("dve", hvs[:, 4:5], hvs[:, 0:1], 128.0 ** -0.5, ALU.mult)
        k.ts("dve", hvs[:, 5:6], hvs[:, 2:3], 128.0 ** -0.5, ALU.mult)
        return hvs

    k.push()
    hvs = hvload(hv0)
    kmemT = k.sb("kmemT", [128, 2, 256], BF16)
    vmem = k.sb("vmem", [128, 2, 2, 129], BF16)
    stage_proj(k, Bk, xT, win0, lng0, NFM1 * 128, TM1, projT,
                  [(vna, 0, 384, BF16, "rows"), (ba, 384, 12, F32, "ptile")], ones, epsc)
    stage_memkv(k, Bk, memT, wkv0, memg, hvs[:, 3:4], ones, epsc, kmemT, vmem)
    k.barrier()
    stage_na(k, Bk, projT, vna, nab, nam, hvs[:, 4:5], hvs[:, 1:2], ones, epsc, ident, mixloc)
    stage_memattn(k, Bk, projT, 21, 23, hvs[:, 5:6], kmemT, vmem, ones, epsc, ident, mixloc, 768)
    stage_dn(k, Bk, projT, ba, dnrow, convw, onrow, ident, ones, epsc, mixloc)
    k.pop()
    exchange()
    stage_wout(k, Bk, mixall, xT, wout0, x1T)
    k.barrier()
    k.push()
    hvs = hvload(hv1)
    kmemT = k.sb("kmemT", [128, 2, 256], BF16)
    vmem = k.sb("vmem", [128, 2, 2, 129], BF16)
    stage_proj(k, Bk, x1T, win1, lng1, NFM3 * 128, 768, projT,
                  [(vdf, 0, 384, BF16, "rows"), (vdf, 384, 384, BF16, "rows")], ones, epsc)
    stage_memkv(k, Bk, memT, wkv1, memg, hvs[:, 3:4], ones, epsc, kmemT, vmem)
    k.barrier()
    stage_diff(k, Bk, projT, vdf, hvs, lamrep, subrow, ones, epsc, ident, identf, mixloc)
    stage_memattn(k, Bk, projT, 18, 20, hvs[:, 5:6], kmemT, vmem, ones, epsc, ident, mixloc, 768)
    k.pop()
    exchange()
    stage_wout(k, Bk, mixall, x1T, wout1, yT)
    k.barrier()
    return nc


def fused_rowmap(perm_groups):
    idx = []
    for j in range(4):
        for r in range(2):
            idx += list(perm_groups[r][j * 256:(j + 1) * 256])
    return np.asarray(idx)


def prep_fused(inp, b, g):
    a = prep_p1(inp, b, g)
    c = prep_p3(inp, None, b, g)
    m = {
        "xT": a["xT"], "memT": a["memT"], "memg": a["memg"],
        "win0": a["win"], "wkv0": a["wkv"], "lng0": a["lng"], "hv0": a["hv"],
        "nab": a["nab"], "nam": a["nam"], "dnrow": a["dnrow"], "convw": a["convw"], "onrow": a["onrow"],
        "win1": c["win"], "wkv1": c["wkv"], "lng1": c["lng"], "hv1": c["hv"],
        "lamrep": c["lamrep"], "subrow": c["subrow"],
    }
    pe = mix_perm()
    po = mix_perm_odd()
    m["wout0"] = np.ascontiguousarray(inp["e_w_out"][0][fused_rowmap([pe[:1024], pe[1024:]]), :])
    m["wout1"] = np.ascontiguousarray(inp["o_w_out"][0][fused_rowmap([po[:1024], po[1024:]]), :])
    return m


def kernel(**inputs):
    inp = {k_: np.asarray(v_) for k_, v_ in inputs.items()}
    nc = build_fused()
    maps = [prep_fused(inp, c // 2, c % 2) for c in range(8)]
    res = run_bass_kernel_spmd(nc, maps, core_ids=list(range(8)))
    out = np.stack([np.ascontiguousarray(np.asarray(res.results[2 * b]["yT"]).T) for b in range(4)], axis=0)
    return out.astype(np.float32)
```
